# Optimizing a Trainium2 kernel written in Bass

```python
import math
import jax, jax.numpy as jnp
from jax import lax
import numpy as np

D_MODEL = 2048
BATCH = 4
SEQ = 2048
DEPTH = 2
DEC_BATCH = 128
DEC_SEQ = 1
PAST_LEN = 16384
PAGE_SIZE = 128

D_FF = 5632
EPS = 1e-6
CHUNK = 64
N_BRANCH = 3
M_WIDTH = D_MODEL // 2
M_HEADDIM = 64
M_HEADS = M_WIDTH // M_HEADDIM
M_GROUPS = 2
M_STATE = 128
M_CONV = 4
M_CONV_DIM = M_WIDTH + 2 * M_GROUPS * M_STATE
H_WIDTH = D_MODEL // 2
H_EXPAND = 128
H_HEADS = H_WIDTH // H_EXPAND
G_WIDTH = D_MODEL // 2
G_KWIDTH = G_WIDTH // 2
G_HEADS = 4
G_DK = G_KWIDTH // G_HEADS
G_DV = G_WIDTH // G_HEADS
G_RANK = 16
G_NORMALIZER = 16.0
IN_SPLIT_WIDTHS = (M_WIDTH, M_CONV_DIM, M_HEADS,
                   H_WIDTH, H_WIDTH, H_WIDTH, H_WIDTH,
                   G_KWIDTH, G_KWIDTH, G_WIDTH, G_WIDTH, G_RANK,
                   N_BRANCH * D_MODEL)
IN_COLS = (M_WIDTH + M_CONV_DIM + M_HEADS + 4 * H_WIDTH + 2 * G_KWIDTH + 2 * G_WIDTH + G_RANK
           + N_BRANCH * D_MODEL)

kernel_name = 'hybrid_ssd_hgrn2_gla_macaron_step'

F32 = jnp.float32
F32_TINY = float(np.finfo(np.float32).tiny)


def _rmsnorm(x, w):
    x32 = x.astype(F32)
    y = x32 * lax.rsqrt(jnp.mean(x32 * x32, axis=-1, keepdims=True) + EPS)
    return (y * w.astype(F32)).astype(x.dtype)


def _swiglu(h, w_gate_up, w_down):
    g, u = jnp.split(h @ w_gate_up, 2, axis=-1)
    return (jax.nn.silu(g) * u) @ w_down


def _to_chunks(a, c):
    b, L = a.shape[:2]
    n = -(-L // c)
    a = jnp.pad(a, [(0, 0), (0, n * c - L)] + [(0, 0)] * (a.ndim - 2))
    a = a.reshape((b, n, c) + a.shape[2:])
    return jnp.moveaxis(a, 1, 0)


def _from_chunks(a, L):
    a = jnp.moveaxis(a, 0, 1)
    a = a.reshape((a.shape[0], -1) + a.shape[3:])
    return a[:, :L]


def _masked_decay(diff, mask):
    return jnp.where(mask, jnp.exp(jnp.where(mask, diff, 0.0)), 0.0)


def _ssd_scan(x, dt, a_head, bm, cm, s0):
    L = x.shape[1]
    c = min(CHUNK, L)
    mask = jnp.tril(jnp.ones((c, c), dtype=bool))[None, :, :, None, None]

    def step(s, inp):
        xc, dtc, bc, cc = inp
        cum = jnp.cumsum(dtc * a_head, axis=1)
        diff = cum[:, :, None] - cum[:, None, :]
        decay = _masked_decay(diff, mask)
        cb = jnp.einsum('btgn,bsgn->btsg', cc, bc)
        y = jnp.einsum('btsg,btsgj,bsgj,bsgjp->btgjp', cb, decay, dtc, xc)
        y = y + jnp.einsum('btgn,bgjpn,btgj->btgjp', cc, s, jnp.exp(cum))
        w = dtc * jnp.exp(cum[:, -1:] - cum)
        s = jnp.exp(cum[:, -1])[..., None, None] * s + jnp.einsum('bsgn,bsgj,bsgjp->bgjpn', bc, w, xc)
        return s, y

    xs = (_to_chunks(x.astype(F32), c), _to_chunks(dt.astype(F32), c),
          _to_chunks(bm.astype(F32), c), _to_chunks(cm.astype(F32), c))
    s, y = lax.scan(step, s0.astype(F32), xs)
    return _from_chunks(y, L), s


def _gla_scan(q, k, v, log_a, s0):
    L = q.shape[1]
    c = min(CHUNK, L)
    mask = jnp.tril(jnp.ones((c, c), dtype=bool))[None, :, :, None, None]

    def step(s, inp):
        qc, kc, vc, gc = inp
        cum = jnp.cumsum(gc, axis=1)
        diff = cum[:, :, None] - cum[:, None, :]
        decay = _masked_decay(diff, mask)
        att = jnp.einsum('bthd,btshd,bshd->bhts', qc, decay, kc)
        o = jnp.einsum('bhts,bshv->bthv', att, vc) + jnp.einsum('bthd,bhdv->bthv', qc * jnp.exp(cum), s)
        s = jnp.exp(cum[:, -1])[..., None] * s + jnp.einsum('bshd,bshv->bhdv', kc * jnp.exp(cum[:, -1:] - cum), vc)
        return s, o

    xs = (_to_chunks(q.astype(F32), c), _to_chunks(k.astype(F32), c),
          _to_chunks(v.astype(F32), c), _to_chunks(log_a.astype(F32), c))
    s, o = lax.scan(step, s0.astype(F32), xs)
    return _from_chunks(o, L), s


def _mamba_branch(z, xbc, dt_raw, conv_buf, s0, conv_w, conv_b, dt_bias, a_log, d_skip, norm_w):
    b, L = xbc.shape[:2]
    j = M_HEADS // M_GROUPS
    full = jnp.concatenate([conv_buf.astype(xbc.dtype), xbc], axis=1)
    new_buf = full[:, -(M_CONV - 1):]
    conv = lax.conv_general_dilated(full, conv_w[:, None, :].astype(full.dtype), window_strides=(1,),
                                    padding='VALID', dimension_numbers=('NWC', 'WIO', 'NWC'),
                                    feature_group_count=M_CONV_DIM)
    xbc = jax.nn.silu(conv + conv_b)
    xs, bm, cm = jnp.split(xbc, [M_WIDTH, M_WIDTH + M_GROUPS * M_STATE], axis=-1)
    xs = xs.reshape(b, L, M_GROUPS, j, M_HEADDIM)
    bm = bm.reshape(b, L, M_GROUPS, M_STATE)
    cm = cm.reshape(b, L, M_GROUPS, M_STATE)
    dt = jax.nn.softplus((dt_raw + dt_bias).astype(F32)).reshape(b, L, M_GROUPS, j)
    a = -jnp.exp(a_log.astype(F32)).reshape(M_GROUPS, j)
    y, s = _ssd_scan(xs, dt, a, bm, cm, s0.reshape(b, M_GROUPS, j, M_HEADDIM, M_STATE))
    y = y + d_skip.astype(F32).reshape(M_GROUPS, j)[:, :, None] * xs.astype(F32)
    y = y.reshape(b, L, M_WIDTH) * jax.nn.silu(z.astype(F32))
    gs = M_WIDTH // M_GROUPS
    y = _rmsnorm(y.reshape(b, L, M_GROUPS, gs), norm_w.reshape(M_GROUPS, gs)).reshape(b, L, M_WIDTH)
    return y.astype(z.dtype), new_buf, s.reshape(b, M_HEADS, M_HEADDIM, M_STATE)


def _hgrn2_branch(hq, hf, hi, hg, s0, lb, norm_w):
    b, L = hq.shape[:2]
    shp = (b, L, H_HEADS, H_EXPAND)
    q = jax.nn.silu(hq.astype(F32)).reshape(shp) * H_EXPAND ** -0.5
    zf = hf.astype(F32)
    lb = lb.astype(F32)
    f = lb + (1.0 - lb) * jax.nn.sigmoid(zf)
    log_f = jnp.log(jnp.maximum(f, F32_TINY))
    k = (1.0 - lb) * jax.nn.sigmoid(-zf)
    o, s = _gla_scan(q, k.reshape(shp), hi.astype(F32).reshape(shp), log_f.reshape(shp), s0)
    o = _rmsnorm(o, norm_w) * jax.nn.silu(hg.astype(F32)).reshape(shp)
    return o.reshape(b, L, H_WIDTH).astype(hq.dtype), s


def _gla_branch(gq, gk, gv, gg, ga, s0, w_decay, b_decay, norm_w):
    b, L = gq.shape[:2]
    q = gq.astype(F32).reshape(b, L, G_HEADS, G_DK) * G_DK ** -0.5
    k = gk.astype(F32).reshape(b, L, G_HEADS, G_DK)
    v = gv.astype(F32).reshape(b, L, G_HEADS, G_DV)
    log_a = jax.nn.log_sigmoid((ga @ w_decay + b_decay).astype(F32)) / G_NORMALIZER
    o, s = _gla_scan(q, k, v, log_a.reshape(b, L, G_HEADS, G_DK), s0)
    o = _rmsnorm(o, norm_w) * jax.nn.silu(gg.astype(F32)).reshape(b, L, G_HEADS, G_DV)
    return o.reshape(b, L, G_WIDTH).astype(gq.dtype), s


def _layer(x, conv_buf, s_ssm, s_hgrn, s_gla, lb, p):
    b, L = x.shape[:2]
    x = x + 0.5 * _swiglu(_rmsnorm(x, p['ffn1_norm']), p['ffn1_w_gate_up'], p['ffn1_w_down'])
    h = _rmsnorm(x, p['mix_norm'])
    offs = np.cumsum(IN_SPLIT_WIDTHS)[:-1].tolist()
    (m_z, m_xbc, m_dt, h_q, h_f, h_i, h_g, g_q, g_k, g_v, g_g, g_a, gate) = jnp.split(h @ p['w_in'], offs, axis=-1)
    y_m, new_conv, new_ssm = _mamba_branch(m_z, m_xbc, m_dt, conv_buf, s_ssm, p['conv_w'], p['conv_b'],
                                           p['dt_bias'], p['a_log'], p['d_skip'], p['mamba_norm'])
    y_h, new_hgrn = _hgrn2_branch(h_q, h_f, h_i, h_g, s_hgrn, lb, p['hgrn_norm'])
    y_g, new_gla = _gla_branch(g_q, g_k, g_v, g_g, g_a, s_gla, p['gla_w_decay'], p['gla_b_decay'], p['gla_norm'])
    gates = jax.nn.sigmoid(gate.astype(F32)).reshape(b, L, N_BRANCH, D_MODEL).astype(x.dtype)
    merged = (gates[:, :, 0] * (y_m @ p['w_branch_mamba'])
              + gates[:, :, 1] * (y_h @ p['w_branch_hgrn'])
              + gates[:, :, 2] * (y_g @ p['w_branch_gla']))
    x = x + merged @ p['w_out']
    x = x + 0.5 * _swiglu(_rmsnorm(x, p['ffn2_norm']), p['ffn2_w_gate_up'], p['ffn2_w_down'])
    return x, new_conv, new_ssm, new_hgrn, new_gla


def setup_inputs(seed: int = 0) -> dict:
    key = jax.random.key(seed)
    ks = jax.random.split(key, 32)

    def nrm(i, shape, scale):
        return jax.random.normal(ks[i], shape, F32) * scale

    def gain(i, shape):
        return 1.0 + 0.02 * jax.random.normal(ks[i], shape, F32)

    dt0 = jnp.exp(jax.random.uniform(ks[12], (DEPTH, M_HEADS), F32)
                  * (math.log(0.1) - math.log(0.001)) + math.log(0.001))
    dt_bias = dt0 + jnp.log(-jnp.expm1(-dt0))
    a_log = jnp.log(jax.random.uniform(ks[13], (DEPTH, M_HEADS), F32, 1.0, 16.0))
    return {
        'x_prompt': nrm(0, (BATCH, SEQ, D_MODEL), 1.0),
        'x_sample': nrm(1, (DEC_BATCH, DEC_SEQ, D_MODEL), 1.0),
        'state_conv': nrm(2, (DEPTH, DEC_BATCH, M_CONV - 1, M_CONV_DIM), 1.0),
        'state_ssm': nrm(3, (DEPTH, DEC_BATCH, M_HEADS, M_HEADDIM, M_STATE), 0.5),
        'state_hgrn': nrm(4, (DEPTH, DEC_BATCH, H_HEADS, H_EXPAND, H_EXPAND), 1.0),
        'state_gla': nrm(5, (DEPTH, DEC_BATCH, G_HEADS, G_DK, G_DV), 1.0),
        'ffn1_norm': gain(6, (DEPTH, D_MODEL)),
        'ffn1_w_gate_up': nrm(7, (DEPTH, D_MODEL, 2 * D_FF), D_MODEL ** -0.5),
        'ffn1_w_down': nrm(8, (DEPTH, D_FF, D_MODEL), D_FF ** -0.5),
        'mix_norm': gain(9, (DEPTH, D_MODEL)),
        'w_in': nrm(10, (DEPTH, D_MODEL, IN_COLS), D_MODEL ** -0.5),
        'conv_w': nrm(11, (DEPTH, M_CONV, M_CONV_DIM), M_CONV ** -0.5),
        'conv_b': nrm(14, (DEPTH, M_CONV_DIM), 0.02),
        'dt_bias': dt_bias,
        'a_log': a_log,
        'd_skip': gain(15, (DEPTH, M_HEADS)),
        'mamba_norm': gain(16, (DEPTH, M_WIDTH)),
        'hgrn_lb_logits': nrm(17, (DEPTH, H_WIDTH), 1.0),
        'hgrn_norm': gain(18, (DEPTH, H_EXPAND)),
        'gla_w_decay': nrm(19, (DEPTH, G_RANK, G_KWIDTH), G_RANK ** -0.5),
        'gla_b_decay': nrm(20, (DEPTH, G_KWIDTH), 0.02),
        'gla_norm': gain(21, (DEPTH, G_DV)),
        'w_branch_mamba': nrm(22, (DEPTH, M_WIDTH, D_MODEL), M_WIDTH ** -0.5),
        'w_branch_hgrn': nrm(23, (DEPTH, H_WIDTH, D_MODEL), H_WIDTH ** -0.5),
        'w_branch_gla': nrm(24, (DEPTH, G_WIDTH, D_MODEL), G_WIDTH ** -0.5),
        'w_out': nrm(25, (DEPTH, D_MODEL, D_MODEL), D_MODEL ** -0.5),
        'ffn2_norm': gain(26, (DEPTH, D_MODEL)),
        'ffn2_w_gate_up': nrm(27, (DEPTH, D_MODEL, 2 * D_FF), D_MODEL ** -0.5),
        'ffn2_w_down': nrm(28, (DEPTH, D_FF, D_MODEL), D_FF ** -0.5),
        'final_norm': gain(29, (D_MODEL,)),
    }


def reference(x_prompt, x_sample, state_conv, state_ssm, state_hgrn, state_gla,
              ffn1_norm, ffn1_w_gate_up, ffn1_w_down, mix_norm, w_in, conv_w, conv_b,
              dt_bias, a_log, d_skip, mamba_norm, hgrn_lb_logits, hgrn_norm,
              gla_w_decay, gla_b_decay, gla_norm, w_branch_mamba, w_branch_hgrn, w_branch_gla,
              w_out, ffn2_norm, ffn2_w_gate_up, ffn2_w_down, final_norm):
    sm = jax.nn.softmax(hgrn_lb_logits.astype(F32), axis=0)
    lower_bounds = jnp.cumsum(sm, axis=0) - sm[0]

    bp = x_prompt.shape[0]
    sdt = state_ssm.dtype
    xp, xs = x_prompt, x_sample
    pc, ps, ph, pg = [], [], [], []
    sc, ss, sh, sg = [], [], [], []
    for l in range(DEPTH):
        p = {'ffn1_norm': ffn1_norm[l], 'ffn1_w_gate_up': ffn1_w_gate_up[l], 'ffn1_w_down': ffn1_w_down[l],
             'mix_norm': mix_norm[l], 'w_in': w_in[l], 'conv_w': conv_w[l], 'conv_b': conv_b[l],
             'dt_bias': dt_bias[l], 'a_log': a_log[l], 'd_skip': d_skip[l], 'mamba_norm': mamba_norm[l],
             'hgrn_norm': hgrn_norm[l], 'gla_w_decay': gla_w_decay[l], 'gla_b_decay': gla_b_decay[l],
             'gla_norm': gla_norm[l], 'w_branch_mamba': w_branch_mamba[l], 'w_branch_hgrn': w_branch_hgrn[l],
             'w_branch_gla': w_branch_gla[l], 'w_out': w_out[l], 'ffn2_norm': ffn2_norm[l],
             'ffn2_w_gate_up': ffn2_w_gate_up[l], 'ffn2_w_down': ffn2_w_down[l]}
        lb = lower_bounds[l]
        xp, c1, s1, h1, g1 = _layer(
            xp,
            jnp.zeros((bp, M_CONV - 1, M_CONV_DIM), x_prompt.dtype),
            jnp.zeros((bp, M_HEADS, M_HEADDIM, M_STATE), F32),
            jnp.zeros((bp, H_HEADS, H_EXPAND, H_EXPAND), F32),
            jnp.zeros((bp, G_HEADS, G_DK, G_DV), F32),
            lb, p)
        xs, c2, s2, h2, g2 = _layer(xs, state_conv[l], state_ssm[l], state_hgrn[l], state_gla[l], lb, p)
        pc.append(c1.astype(state_conv.dtype)); ps.append(s1.astype(sdt))
        ph.append(h1.astype(state_hgrn.dtype)); pg.append(g1.astype(state_gla.dtype))
        sc.append(c2.astype(state_conv.dtype)); ss.append(s2.astype(sdt))
        sh.append(h2.astype(state_hgrn.dtype)); sg.append(g2.astype(state_gla.dtype))
    y_prompt = _rmsnorm(xp, final_norm)
    y_sample = _rmsnorm(xs, final_norm)
    return (y_prompt, y_sample,
            jnp.stack(pc), jnp.stack(ps), jnp.stack(ph), jnp.stack(pg),
            jnp.stack(sc), jnp.stack(ss), jnp.stack(sh), jnp.stack(sg))
```

```python
import numpy as np
import ml_dtypes
import concourse.bass as bass
import concourse.mybir as mybir
from concourse.bass_utils import run_bass_kernel_spmd

F32 = mybir.dt.float32
BF16 = mybir.dt.bfloat16
AF = mybir.ActivationFunctionType
ALU = mybir.AluOpType

ENGS = ("pe", "act", "dve", "pool", "sp")

D = 2048
DFF = 5632
TT = 512
NS = 16
TC = TT + NS
EPS = 1e-6
O_Z, O_XBC, O_DT, O_HQ, O_HF, O_HI, O_HG, O_GQ, O_GK, O_GV, O_GG, O_GA, O_GATE = (
    0, 1024, 2560, 2576, 3600, 4624, 5648, 6672, 7184, 7696, 8720, 9744, 9760)


class MK:
    N_DMA_SEMS = 6

    def __init__(self, nc):
        self.nc = nc
        self.lists = {e: [] for e in ENGS}
        self.cnt = {e: 0 for e in ENGS}
        self.sem = {e: nc.alloc_semaphore(name=f"c_{e}") for e in ENGS}
        self.dsem = {e: [nc.alloc_semaphore(name=f"d_{e}{i}") for i in range(self.N_DMA_SEMS)]
                     for e in ("sp", "pool", "act")}
        self.dcnt = {e: 0 for e in self.dsem}
        self.dval = {e: [0] * self.N_DMA_SEMS for e in self.dsem}
        self.known = {e: {} for e in ENGS}
        self.last_w = {}
        self.readers = {}
        self.n_instr = 0

    def all_sems(self):
        return list(self.sem.values()) + [s for v in self.dsem.values() for s in v]

    def _need(self, eng, tok, waits):
        if tok is None:
            return
        sem, val, sid = tok
        if self.known[eng].get(sid, 0) >= val:
            return
        prev = waits.get(sid)
        if prev is None or prev[1] < val:
            waits[sid] = (sem, val)

    def _deps(self, eng, reads, writes, skip_self=False):
        waits = {}
        own = "c_" + eng
        for k in reads:
            self._need(eng, self.last_w.get(k), waits)
        for k in writes:
            self._need(eng, self.last_w.get(k), waits)
            for t in self.readers.get(k, ()):
                self._need(eng, t, waits)
        if skip_self:
            waits.pop(own, None)
        for sid, (sem, val) in waits.items():
            self.known[eng][sid] = val
        return list(waits.values())

    def _commit(self, tok, reads, writes):
        for k in writes:
            self.last_w[k] = tok
            self.readers[k] = []
        for k in reads:
            self.readers.setdefault(k, []).append(tok)

    def op(self, eng, fn, reads=(), writes=(), skip_self=False):
        waits = self._deps(eng, reads, writes, skip_self)
        self.cnt[eng] += 1
        tok = (self.sem[eng], self.cnt[eng], "c_" + eng)
        self.lists[eng].append((waits, fn, self.sem[eng], 1))
        self._commit(tok, reads, writes)
        self.n_instr += 1
        return tok

    def dma(self, q, fn, reads=(), writes=()):
        waits = self._deps(q, reads, writes)
        i = self.dcnt[q] % self.N_DMA_SEMS
        self.dcnt[q] += 1
        sem = self.dsem[q][i]
        sid = f"d_{q}{i}"
        prev = self.dval[q][i]
        if prev and self.known[q].get(sid, 0) < prev:
            waits.append((sem, prev))
            self.known[q][sid] = prev
        self.dval[q][i] = prev + 16
        tok = (sem, prev + 16, sid)
        self.lists[q].append((waits, fn, sem, 16))
        self._commit(tok, reads, writes)
        self.n_instr += 1
        return tok

    def _all_tokens(self, skip_pool):
        toks = []
        for e in ENGS:
            if self.cnt[e] and not (skip_pool and e == "pool"):
                toks.append((self.sem[e], self.cnt[e], "c_" + e))
        for q in self.dsem:
            if skip_pool and q == "pool":
                continue
            for i in range(self.N_DMA_SEMS):
                if self.dval[q][i]:
                    toks.append((self.dsem[q][i], self.dval[q][i], f"d_{q}{i}"))
        return toks

    def barrier(self, engines=("pe", "act", "dve", "sp")):
        toks = self._all_tokens(skip_pool=True)
        for eng in engines:
            waits = {}
            for t in toks:
                self._need(eng, t, waits)
            for sid, (sem, val) in waits.items():
                self.known[eng][sid] = val
            if waits:
                self.lists[eng].append((list(waits.values()), None, None, 0))

    def wait_all(self, eng):
        waits = {}
        for t in self._all_tokens(skip_pool=False):
            self._need(eng, t, waits)
        self.lists[eng].append((list(waits.values()), None, None, 0))

    def emit(self):
        nc = self.nc
        lists = self.lists
        sems = self.all_sems()
        with nc.Block() as b0:
            @b0.sync
            def _(e):
                for s in sems:
                    e.sem_clear(s)
        with nc.Block() as block:
            def run(e, items):
                for waits, fn, sem, inc in items:
                    for (s, v) in waits:
                        e.wait_ge(s, v)
                    if fn is not None:
                        fn(e).then_inc(sem, inc)

            @block.tensor
            def _(e):
                run(e, lists["pe"])

            @block.scalar
            def _(e):
                run(e, lists["act"])

            @block.vector
            def _(e):
                run(e, lists["dve"])

            @block.gpsimd
            def _(e):
                run(e, lists["pool"])

            @block.sync
            def _(e):
                run(e, lists["sp"])


def build(NT, stages=("ffn1", "mix", "ffn2"), nlayers=2, dbg=None, mixparts=("mamba", "hgrn", "gla", "merge")):
    nc = bass.Bass("TRN2", target_bir_lowering=False)
    m = MK(nc)
    NTOK = NT * TT

    def din(name, shape, dt=F32):
        return nc.dram_tensor(name, list(shape), dt, kind="ExternalInput").ap()

    def dout(name, shape, dt=F32):
        return nc.dram_tensor(name, list(shape), dt, kind="ExternalOutput").ap()

    def sb(name, shape, dt=F32):
        return nc.alloc_sbuf_tensor("s_" + name, list(shape), dt).ap()

    xTp = din("xTp", [D, NTOK]); xTs = din("xTs", [D, NS])
    w_gu = [din("w_gu1", [2, D, 2 * DFF]), din("w_gu2", [2, D, 2 * DFF])]
    w_dn = [din("w_d1", [2, DFF, D]), din("w_d2", [2, DFF, D])]
    w_in = din("w_in", [2, D, 15904])
    w_br = [din("w_bm", [2, 1024, D]), din("w_bh", [2, 1024, D]), din("w_bg", [2, 1024, D])]
    w_out = din("w_out", [2, D, D])
    d_nrm = din("nrm", [128, 7, 16])
    d_convw = din("convw", [128, 2, 12, 4]); d_convb = din("convb", [128, 2, 12])
    d_hd = din("hd", [16, 4])
    d_dskip = din("dskip_bc", [128, 2, 16]); d_mnorm = din("mnormF", [128, 2, 8])
    d_lbl = din("lbl", [128, 2, 8]); d_hnorm = din("hnorm", [128, 2]); d_gnorm = din("gnorm", [128, 2, 2])
    d_wdec = din("wdec", [16, 2, 512]); d_bdec = din("bdec", [128, 2, 4])
    d_sconv = din("sconv", [128, 2, 12, NS, 3])
    d_sssm = din("s_ssm", [2, NS, 8, 128, 128])
    d_shgrn = din("s_hgrn", [2, NS, 8, 128, 128]); d_sgla = din("s_gla", [2, NS, 4, 128, 256])
    d_identf = din("identf", [128, 128]); d_trif = din("trif", [128, 128]); d_negm = din("negm", [128, 128])
    d_sel = din("sel", [16, 16, 128])
    d_id16 = din("id16", [16, 16])

    yTp = dout("yTp", [D, NTOK]); yTs = dout("yTs", [D, NS])
    o_convp = dout("conv_p", [2, 128, 12, 3]); o_ssmp = dout("ssm_p", [2, 128, 1024])
    o_hgrnp = dout("hgrn_p", [2, 128, 1024]); o_glap = dout("gla_p", [2, 128, 1024])
    o_convs = dout("conv_s", [2, 128, 12, NS, 3]); o_ssms = dout("ssm_s", [2, NS, 8, 128, 128])
    o_hgrns = dout("hgrn_s", [2, NS, 8, 128, 128]); o_glas = dout("gla_s", [2, NS, 4, 128, 256])
    dbg_out = {}
    if dbg:
        for name, shape in dbg.items():
            dbg_out[name] = dout("dbg_" + name, shape)

    xT = sb("xT", [128, 16, TC]); hn = sb("hn", [128, 16, TC], BF16)
    NSLAB = 4
    slabs = [sb(f"slab{i}", [128, 4096], BF16) for i in range(NSLAB)]
    identf = sb("identf", [128, 128]); identb = sb("identb", [128, 128], BF16)
    trif = sb("trif", [128, 128]); negm = sb("negm", [128, 128])
    onesf = sb("onesf", [128, 128]); onesb = sb("onesb", [128, 128], BF16)
    nrm = sb("nrm", [128, 7, 16])
    RBYTES = 84480
    convw = sb("convw", [128, 2, 12, 4]); convb = sb("convb", [128, 2, 12])
    hd = sb("hd", [16, 4]); aneg = sb("aneg", [16, 2])
    dskip = sb("dskip", [128, 2, 16]); mnormF = sb("mnormF", [128, 2, 8])
    lbl = sb("lbl", [128, 2, 8]); lb = sb("lb", [128, 2, 8]); oml = sb("oml", [128, 2, 8]); noml = sb("noml", [128, 2, 8])
    hnorm = sb("hnorm", [128, 2]); gnorm = sb("gnorm", [128, 2, 2])
    wdec = sb("wdec", [16, 2, 512]); bdec = sb("bdec", [128, 2, 4])
    sel = sb("sel", [16, 16, 128]); id16 = sb("id16", [16, 16]); E01 = sb("E01", [16, 2, 128])
    Sm = [sb(f"Sm{l}", [128, 1024]) for l in range(2)]
    Sh = [sb(f"Sh{l}", [128, 1024]) for l in range(2)]
    Sg = [sb(f"Sg{l}", [128, 1024]) for l in range(2)]
    cst = [sb(f"cst{l}", [128, 12, 3]) for l in range(2)]
    Smb = sb("Smb", [128, 1024], BF16)
    R = sb("R", [128, RBYTES // 4])
    ps = [nc.alloc_psum_tensor(f"p_ps{i}", [128, 512], F32).ap() for i in range(8)]

    def carve(off, shape, dt):
        esz = 2 if dt == BF16 else 4
        n = int(np.prod(shape[1:]))
        assert off % 4 == 0 and off + n * esz <= RBYTES, (off, shape)
        v = R[:, off // 4: off // 4 + (n * esz + 3) // 4]
        if dt != F32:
            v = v.bitcast(dt)[:, 0:n]
        if len(shape) == 3:
            v = v.rearrange("p (a b) -> p a b", a=shape[1])
        elif len(shape) == 4:
            v = v.rearrange("p (a b c) -> p a b c", a=shape[1], b=shape[2])
        return v[0:shape[0]]

    sqs = [carve(RBYTES - 6144 + i * 2048, [128, 512], F32) for i in range(2)]
    rstd = carve(RBYTES - 2048, [128, 512], F32)

    def act(out, in_, func, r, w, **kw):
        return m.op("act", lambda e: e.activation(out=out, in_=in_, func=func, **kw), r, w)

    def ts(out, in0, s1, s2, op0, op1, r, w):
        if op1 is None:
            return m.op("dve", lambda e: e.tensor_scalar(out=out, in0=in0, scalar1=s1, scalar2=None, op0=op0), r, w)
        return m.op("dve", lambda e: e.tensor_scalar(out=out, in0=in0, scalar1=s1, scalar2=s2, op0=op0, op1=op1), r, w)

    def stt(out, in0, scalar, in1, op0, op1, r, w):
        return m.op("dve", lambda e: e.scalar_tensor_tensor(out=out, in0=in0, scalar=scalar, in1=in1,
                                                            op0=op0, op1=op1), r, w)

    def tt(out, in0, in1, op, r, w):
        return m.op("dve", lambda e: e.tensor_tensor(out=out, in0=in0, in1=in1, op=op), r, w)

    def cp(eng, out, in_, r, w):
        if eng == "act":
            return m.op("act", lambda e: e.copy(out=out, in_=in_), r, w)
        return m.op(eng, lambda e: e.tensor_copy(out=out, in_=in_), r, w)

    def ttr(out, in0, in1, accum, r, w):
        m.op("dve", lambda e: e.tensor_tensor(out=out, in0=in0, in1=in1, op=ALU.mult), r, [w[0]])
        return m.op("dve", lambda e: e.reduce_sum(out=accum, in_=out, axis=mybir.AxisListType.X), [w[0]], w[1:])

    def mset(out, val, w):
        return m.op("dve", lambda e: e.memset(out, val), [], w)

    def mm(out, lhsT, rhs, start, stop, r, w):
        return m.op("pe", lambda e: e.matmul(out, lhsT=lhsT, rhs=rhs, start=start, stop=stop), r, w, skip_self=True)

    def tr(out, in_, ident, r, w):
        return m.op("pe", lambda e: e.transpose(out, in_, ident), r, w, skip_self=True)

    def dma(q, out, in_, r, w):
        return m.dma(q, lambda e: e.dma_start(out=out, in_=in_), r, w)

    def scan(out, d0, d1, init, op0, op1, r, w):
        return m.op("dve", lambda e: e.tensor_tensor_scan(out=out, data0=d0, data1=d1, initial=init,
                                                          op0=op0, op1=op1), r, w)

    slab_ctr = [0]

    def load_slab(src, kc, ncols):
        i = slab_ctr[0] % NSLAB
        slab_ctr[0] += 1
        view = slabs[i][:, 0:kc * ncols].rearrange("p (c n) -> p c n", c=kc)
        dma("pool", view, src.rearrange("(c p) n -> p c n", p=128), [], [f"slab{i}"])
        return view, f"slab{i}"

    psrot = [0]

    def next_ps(nb=6):
        b = psrot[0] % nb
        psrot[0] += 1
        return ps[b], f"ps{b}"

    def colblocks(t):
        return [(0, TT)] + ([(TT, NS)] if t == 0 else [])

    XT_ALL = [f"xT{c}" for c in range(16)]

    def rmsnorm_fm(t, widx, dst, dkey):
        for (c0, n) in colblocks(t):
            pss, psk = ps[7], "ps7"
            for c in range(16):
                sq = sqs[c % 2]
                act(sq[:, 0:n], xT[:, c, c0:c0 + n], AF.Square, [f"xT{c}"], [f"sq{c % 2}"])
                mm(pss[:, 0:n], onesf, sq[:, 0:n], c == 0, c == 15, [f"sq{c % 2}", "onesf"], [psk])
            ts(rstd[:, 0:n], pss[:, 0:n], 1.0 / D, EPS, ALU.mult, ALU.add, [psk], ["rstd"])
            act(rstd[:, 0:n], rstd[:, 0:n], AF.Ln, ["rstd"], ["rstd"])
            act(rstd[:, 0:n], rstd[:, 0:n], AF.Exp, ["rstd"], ["rstd"], scale=-0.5)
            for c in range(16):
                stt(dst[:, c, c0:c0 + n], xT[:, c, c0:c0 + n], nrm[:, widx, c:c + 1], rstd[:, 0:n],
                    ALU.mult, ALU.mult, [f"xT{c}", "rstd", "nrm"], [f"{dkey}{c}"])

    def ffn(t, l, which):
        hT = carve(0, [128, 44, TC], BF16)
        sg = [carve(46464 + i * 2112, [128, TC], F32) for i in range(2)]
        rmsnorm_fm(t, (0 if which == 0 else 4) + l, hn, "hn")
        wgu = w_gu[which][l]
        wdn = w_dn[which][l]
        cbs = colblocks(t)
        sgi = 0
        for s in range(22):
            gv, gk = load_slab(wgu[:, s * 256:(s + 1) * 256], 16, 256)
            uv, uk = load_slab(wgu[:, DFF + s * 256: DFF + (s + 1) * 256], 16, 256)
            for jj in range(2):
                j = s * 2 + jj
                for (c0, n) in cbs:
                    pg, pgk = next_ps()
                    pu, puk = next_ps()
                    for k in range(16):
                        mm(pg[:, 0:n], gv[:, k, jj * 128:(jj + 1) * 128], hn[:, k, c0:c0 + n], k == 0, k == 15,
                           [gk, f"hn{k}"], [pgk])
                    for k in range(16):
                        mm(pu[:, 0:n], uv[:, k, jj * 128:(jj + 1) * 128], hn[:, k, c0:c0 + n], k == 0, k == 15,
                           [uk, f"hn{k}"], [puk])
                    sgt = sg[sgi % 2]
                    sgk = f"sg{sgi % 2}"
                    sgi += 1
                    act(sgt[:, 0:n], pg[:, 0:n], AF.Silu, [pgk], [sgk])
                    tt(hT[:, j, c0:c0 + n], sgt[:, 0:n], pu[:, 0:n], ALU.mult, [sgk, puk], [f"hT{j}"])
        for fo in range(16):
            dva, dka = load_slab(wdn[0:22 * 128, fo * 128:(fo + 1) * 128], 22, 128)
            dvb, dkb = load_slab(wdn[22 * 128:44 * 128, fo * 128:(fo + 1) * 128], 22, 128)
            for (c0, n) in cbs:
                pd, pdk = next_ps()
                for j in range(44):
                    dv, dk = (dva, dka) if j < 22 else (dvb, dkb)
                    mm(pd[:, 0:n], dv[:, j % 22, :], hT[:, j, c0:c0 + n], j == 0, j == 43, [dk, f"hT{j}"], [pdk])
                stt(xT[:, fo, c0:c0 + n], pd[:, 0:n], 0.5, xT[:, fo, c0:c0 + n], ALU.mult, ALU.add,
                    [pdk, f"xT{fo}"], [f"xT{fo}"])
        m.barrier()

    Y_OFF = 25344
    ymT = carve(0, [128, 8, TC], BF16)
    yhT = carve(8448, [128, 8, TC], BF16)
    ygT = carve(16896, [128, 8, TC], BF16)

    def blocks(t):
        b = [(i * 128, 128) for i in range(4)]
        if t == 0:
            b.append((TT, NS))
        return b

    def proj_fm(t, l, col0, ncols, evac):
        done = 0
        while done < ncols:
            w = min(256, ncols - done)
            sv, sk = load_slab(w_in[l][:, col0 + done: col0 + done + w], 16, w)
            for jj in range((w + 127) // 128):
                mcols = min(128, w - jj * 128)
                for (c0, n) in colblocks(t):
                    pp, pk = next_ps(4)
                    for k in range(16):
                        mm(pp[0:mcols, 0:n], sv[:, k, jj * 128: jj * 128 + mcols], hn[:, k, c0:c0 + n], k == 0, k == 15,
                           [sk, f"hn{k}"], [pk])
                    evac(done // 128 + jj, c0, n, pp[0:mcols, 0:n], pk)
            done += w

    def proj_tm(t, l, col0, ncols, evac):
        done = 0
        while done < ncols:
            w = min(256, ncols - done)
            sv, sk = load_slab(w_in[l][:, col0 + done: col0 + done + w], 16, w)
            for bi, (b0, rows) in enumerate(blocks(t)):
                pp, pk = next_ps(4)
                for k in range(16):
                    mm(pp[0:rows, 0:w], hn[:, k, b0:b0 + rows], sv[:, k, :], k == 0, k == 15, [sk, f"hn{k}"], [pk])
                evac(bi, rows, done, w, pp[0:rows, 0:w], pk)
            done += w

    def mamba(t, l):
        o = Y_OFF
        X_tm = carve(o, [128, 5, 1024], BF16); o += 10240
        B_tm = carve(o, [128, 5, 256], BF16); o += 2560
        C_tm = carve(o, [128, 256], BF16); o += 512
        BCT = carve(o, [128, 4, TC], BF16); o += 4224
        cols = carve(o, [128, 5, 48], F32); o += 960
        expc = carve(o, [128, 5, 16], F32); o += 320
        eclbc = carve(o, [128, 4, 16], F32); o += 256
        dtT = carve(o, [128, TC], F32); o += 2112
        cumT = carve(o, [128, TC], F32); o += 2112
        ytm = carve(o, [128, 1024], F32); o += 4096
        zs = carve(o, [128, 1024], BF16); o += 2048
        yn = carve(o, [128, 1024], BF16); o += 2048
        ss2 = carve(o, [128, 4], F32); o += 16
        tmpx = carve(o, [128, 512], F32); o += 2048
        junk = tmpx
        U0 = o
        cstage = [carve(o + i * 2128, [128, 532], F32) for i in range(2)]; o += 4256
        xcs = [carve(o + i * 1056, [128, TC], BF16) for i in range(2)]; o += 2112
        cacc = carve(o, [128, 512], F32); o += 2048
        wT = carve(o, [128, TC], F32); o += 2112
        gaT_ = carve(o, [128, TC], F32); o += 2112
        dg4 = carve(o, [128, 4, 16], F32); o += 256
        ecl4 = carve(o, [128, 4], F32); o += 16
        xs_s = carve(o, [128, 12, NS], F32); o += 768
        acc_s = carve(o, [128, 12, NS], F32); o += 768
        tmp_s = carve(o, [128, 12, NS], F32); o += 768
        xc_s = carve(o, [128, 12, NS], BF16); o += 384
        cso = carve(o, [128, 12, NS, 3], F32); o += 2304
        sconv = carve(o, [128, 12, NS, 3], F32); o += 2304
        assert o <= RBYTES, o
        o = U0
        Dm = [carve(o + i * 512, [128, 128], F32) for i in range(2)]; o += 1024
        Lm = [carve(o + i * 512, [128, 128], F32) for i in range(2)]; o += 1024
        Mt = [carve(o + i * 256, [128, 128], BF16) for i in range(2)]; o += 512
        ys = carve(o, [128, 512], F32); o += 2048
        Xw = carve(o, [128, 512], BF16); o += 1024
        o = U0
        Xd = carve(o, [128, 1024], F32); o += 4096
        Xdm = carve(o, [128, NS, 128], BF16); o += 4096
        Dme = carve(o, [128, NS, 8], F32); o += 512
        Dmo = carve(o, [128, NS, 8], F32); o += 512
        decbc = carve(o, [128, NS, 8], F32); o += 512
        Cm = carve(o, [128, NS, 128], BF16); o += 4096
        Sst = [carve(o + i * 512, [128, 128], F32) for i in range(3)]; o += 1536
        Snw = [carve(o + i * 512, [128, 128], F32) for i in range(3)]; o += 1536
        ysT = carve(o, [128, 8, NS], F32); o += 512
        assert o <= RBYTES, o
        samp = (t == 0)
        nblk = 5 if samp else 4

        if samp:
            dma("sp", sconv, d_sconv[:, l], [], ["sconv"])

        def evac_xbc(c, c0, n, pp, pk):
            st = cstage[c % 2]
            sk_ = f"cstage{c % 2}"
            if c0 == 0:
                cp("act", st[:, 3:3 + TT], pp, [pk], [sk_])
                cp("dve", st[:, 0:3], cst[l][:, c, :], [f"cst{l}"], [sk_])
                ts(cacc, st[:, 0:TT], convw[:, l, c, 0:1], None, ALU.mult, None, [sk_, "convw"], ["cacc"])
                for i in range(1, 4):
                    stt(cacc, st[:, i:i + TT], convw[:, l, c, i:i + 1], cacc, ALU.mult, ALU.add,
                        [sk_, "convw", "cacc"], ["cacc"])
                cp("dve", cst[l][:, c, :], st[:, TT:TT + 3], [sk_], [f"cst{l}"])
                xc = xcs[c % 2]
                xk = f"xcs{c % 2}"
                act(xc[:, 0:TT], cacc, AF.Silu, ["cacc", "convb"], [xk], bias=convb[:, l, c:c + 1])
                if c < 10:
                    pt, ptk = ps[4 + c % 2], f"ps{4 + c % 2}"
                    ptb = pt.bitcast(BF16)
                    for b in range(4):
                        tr(ptb[:, b * 128:(b + 1) * 128], xc[:, b * 128:(b + 1) * 128], identb, [xk, "identb"], [ptk])
                    src = ptb[:, 0:512].rearrange("p (b f) -> p b f", b=4)
                    if c < 8:
                        cp("act", X_tm[:, 0:4, c * 128:(c + 1) * 128], src, [ptk], ["X_tm"])
                    else:
                        cp("act", B_tm[:, 0:4, (c - 8) * 128:(c - 7) * 128], src, [ptk], ["B_tm"])
                if c >= 8:
                    cp("dve", BCT[:, c - 8, 0:TT], xc[:, 0:TT], [xk], ["BCT"])
            else:
                cp("act", xs_s[:, c, :], pp, [pk], ["xs_s"])

        proj_fm(t, l, O_XBC, 1536, evac_xbc)

        if samp:
            def wb(i):
                return convw[:, l, :, i:i + 1].broadcast_to([128, 12, NS])
            tt(acc_s, sconv[:, :, :, 0], wb(0), ALU.mult, ["sconv", "convw"], ["acc_s"])
            for i in (1, 2):
                tt(tmp_s, sconv[:, :, :, i], wb(i), ALU.mult, ["sconv", "convw"], ["tmp_s"])
                tt(acc_s, acc_s, tmp_s, ALU.add, ["acc_s", "tmp_s"], ["acc_s"])
            tt(tmp_s, xs_s, wb(3), ALU.mult, ["xs_s", "convw"], ["tmp_s"])
            tt(acc_s, acc_s, tmp_s, ALU.add, ["acc_s", "tmp_s"], ["acc_s"])
            tt(acc_s, acc_s, convb[:, l, :].unsqueeze(2).broadcast_to([128, 12, NS]), ALU.add, ["acc_s", "convb"], ["acc_s"])
            act(xc_s, acc_s, AF.Silu, ["acc_s"], ["xc_s"])
            cp("dve", cso[:, :, :, 0:2], sconv[:, :, :, 1:3], ["sconv"], ["cso"])
            cp("dve", cso[:, :, :, 2], xs_s, ["xs_s", "cso"], ["cso"])
            dma("sp", o_convs[l], cso, ["cso"], [f"o_convs{l}"])
            pt, ptk = ps[4], "ps4"
            ptb = pt.bitcast(BF16)
            for c in range(8):
                tr(ptb[0:NS, c * 128:(c + 1) * 128], xc_s[:, c, :], identb, ["xc_s", "identb"], [ptk])
            cp("act", X_tm[0:NS, 4, :], ptb[0:NS, 0:1024], [ptk], ["X_tm"])
            pt, ptk = ps[5], "ps5"
            ptb = pt.bitcast(BF16)
            for c in range(4):
                tr(ptb[0:NS, c * 128:(c + 1) * 128], xc_s[:, 8 + c, :], identb, ["xc_s", "identb"], [ptk])
            cp("act", B_tm[0:NS, 4, :], ptb[0:NS, 0:256], [ptk], ["B_tm"])
            cp("act", C_tm[0:NS, :], ptb[0:NS, 256:512], [ptk], ["C_tm"])

        ncol = TC if samp else TT

        def evac_dt(c, c0, n, pp, pk):
            act(dtT[0:16, c0:c0 + n], pp, AF.Exp, [pk, "hd"], ["dtT"], bias=hd[:, l:l + 1])
        proj_fm(t, l, O_DT, 16, evac_dt)
        act(dtT[0:16, 0:ncol], dtT[0:16, 0:ncol], AF.Ln, ["dtT"], ["dtT"], bias=1.0)
        ts(gaT_[0:16, 0:ncol], dtT[0:16, 0:ncol], aneg[:, l:l + 1], None, ALU.mult, None, ["dtT", "aneg"], ["gaT"])
        for c in range(4):
            scan(cumT[0:16, c * 128:(c + 1) * 128], onesf[0:16, :], gaT_[0:16, c * 128:(c + 1) * 128], 0.0,
                 ALU.mult, ALU.add, ["gaT", "onesf"], ["cumT"])
            act(wT[0:16, c * 128:(c + 1) * 128], cumT[0:16, c * 128:(c + 1) * 128], AF.Exp, ["cumT"], ["wT"],
                scale=-1.0, bias=cumT[0:16, c * 128 + 127:c * 128 + 128])
        tt(wT[0:16, 0:TT], wT[0:16, 0:TT], dtT[0:16, 0:TT], ALU.mult, ["wT", "dtT"], ["wT"])
        if samp:
            cp("dve", cumT[0:16, TT:TC], gaT_[0:16, TT:TC], ["gaT"], ["cumT"])
            cp("dve", wT[0:16, TT:TC], dtT[0:16, TT:TC], ["dtT"], ["wT"])
        for bi, (b0, rows) in enumerate(blocks(t)):
            pt, ptk = ps[6], "ps6"
            for j, srcT in enumerate((dtT, cumT, wT)):
                tr(pt[0:rows, j * 16:(j + 1) * 16], srcT[0:16, b0:b0 + rows], identf[0:16, 0:16],
                   ["dtT", "cumT", "wT", "identf"], [ptk])
            cp("dve", cols[0:rows, bi, :], pt[0:rows, 0:48], [ptk], ["cols"])
        act(expc[:, 0:4, :], cols[:, 0:4, 16:32], AF.Exp, ["cols"], ["expc"])
        if samp:
            act(expc[0:NS, 4, :], cols[0:NS, 4, 16:32], AF.Exp, ["cols"], ["expc"])
        act(ecl4[0:16, :], cumT[0:16, 127:TT:128], AF.Exp, ["cumT"], ["ecl4"])
        tt(dg4[0:16], id16.unsqueeze(1).broadcast_to([16, 4, 16]), ecl4[0:16, :].unsqueeze(2).broadcast_to([16, 4, 16]),
           ALU.mult, ["id16", "ecl4"], ["dg4"])
        pt, ptk = ps[6], "ps6"
        mm(pt[:, 0:64], onesf[0:16, :], dg4[0:16].rearrange("p a b -> p (a b)"), True, True, ["onesf", "dg4"], [ptk])
        cp("dve", eclbc.rearrange("p a b -> p (a b)"), pt[:, 0:64], [ptk], ["eclbc"])

        cp("act", Smb, Sm[l], [f"Sm{l}"], ["Smb"])

        zsl = [load_slab(w_in[l][:, O_Z + q * 256: O_Z + (q + 1) * 256], 16, 256) for q in range(4)]

        def post_block(bi, b0, rows):
            for q in range(4):
                pp, pk = next_ps(4)
                for k in range(16):
                    mm(pp[0:rows, 0:256], hn[:, k, b0:b0 + rows], zsl[q][0][:, k, :], k == 0, k == 15,
                       [zsl[q][1], f"hn{k}"], [pk])
                act(zs[0:rows, q * 256:(q + 1) * 256], pp[0:rows, 0:256], AF.Silu, [pk], ["zs"])
            tt(ytm[0:rows], ytm[0:rows], zs[0:rows], ALU.mult, ["ytm", "zs"], ["ytm"])
            for g in range(2):
                act(junk[0:rows], ytm[0:rows, g * 512:(g + 1) * 512], AF.Square, ["ytm"], ["tmpx", "ss2"],
                    accum_out=ss2[0:rows, g:g + 1])
            ts(ss2[0:rows, 0:2], ss2[0:rows, 0:2], 1.0 / 512, EPS, ALU.mult, ALU.add, ["ss2"], ["ss2"])
            act(ss2[0:rows, 0:2], ss2[0:rows, 0:2], AF.Ln, ["ss2"], ["ss2"])
            act(ss2[0:rows, 0:2], ss2[0:rows, 0:2], AF.Exp, ["ss2"], ["ss2"], scale=-0.5)
            for g in range(2):
                ts(yn[0:rows, g * 512:(g + 1) * 512], ytm[0:rows, g * 512:(g + 1) * 512], ss2[0:rows, g:g + 1], None,
                   ALU.mult, None, ["ytm", "ss2"], ["yn"])
            pt, ptk = ps[6], "ps6"
            ptb = pt.bitcast(BF16)
            for cc in range(8):
                tr(ptb[:, cc * 128: cc * 128 + rows], yn[0:rows, cc * 128:(cc + 1) * 128], identb[0:rows, 0:rows],
                   ["yn", "identb"], [ptk])
            tt(ymT[:, :, b0:b0 + rows], ptb.rearrange("p (c r) -> p c r", c=8)[:, :, 0:rows],
               mnormF[:, l, :].unsqueeze(2).broadcast_to([128, 8, rows]), ALU.mult, [ptk, "mnormF"], ["ymT"])

        m.barrier()
        it = 0
        for c in range(4):
            ch = slice(c * 128, (c + 1) * 128)
            for g in range(2):
                pcb, pcbk = ps[4], "ps4"
                mm(pcb[:, g * 128:(g + 1) * 128], BCT[:, g, ch], BCT[:, 2 + g, ch], True, True, ["BCT"], [pcbk + f"_{g}"])
                pin, pink = ps[5], "ps5"
                mm(pin, BCT[:, 2 + g, ch], Smb[:, g * 512:(g + 1) * 512], True, True, ["BCT", "Smb"], [pink])
                pia, piak = ps[6], "ps6"
                for hh in range(8):
                    h = g * 8 + hh
                    pb_, pbk = ps[7], f"ps7_{it % 4}"
                    pbs = pb_[:, (it % 4) * 128:(it % 4 + 1) * 128]
                    mm(pbs, sel[:, h, :], cumT[0:16, ch], True, True, ["sel", "cumT"], [pbk])
                    d_, dk_ = Dm[it % 2], f"Dm{it % 2}"
                    l_, lk_ = Lm[it % 2], f"Lm{it % 2}"
                    m_, mk_ = Mt[it % 2], f"Mt{it % 2}"
                    stt(d_, pbs, cols[:, c, 16 + h:17 + h], negm, ALU.subtract, ALU.add, [pbk, "cols", "negm"], [dk_])
                    act(l_, d_, AF.Exp, [dk_], [lk_])
                    stt(m_, l_, cols[:, c, h:h + 1], pcb[:, g * 128:(g + 1) * 128], ALU.mult, ALU.mult,
                        [lk_, "cols", pcbk + f"_{g}"], [mk_])
                    mm(pia[:, hh * 64:(hh + 1) * 64], m_, X_tm[:, c, h * 64:(h + 1) * 64], True, True, [mk_, "X_tm"], [piak])
                    it += 1
                gs = slice(g * 512, (g + 1) * 512)
                e8 = expc[:, c, g * 8:(g + 1) * 8].unsqueeze(2).broadcast_to([128, 8, 64])
                tt(ys.rearrange("p (h q) -> p h q", h=8), pin.rearrange("p (h q) -> p h q", h=8), e8, ALU.mult,
                   [pink, "expc"], ["ys"])
                tt(ytm[:, gs], ys, pia, ALU.add, ["ys", piak], ["ytm"])
                d8 = dskip[:, l, g * 8:(g + 1) * 8].unsqueeze(2).broadcast_to([128, 8, 64])
                tt(tmpx.rearrange("p (h q) -> p h q", h=8), X_tm[:, c, gs].rearrange("p (h q) -> p h q", h=8), d8,
                   ALU.mult, ["X_tm", "dskip"], ["tmpx"])
                tt(ytm[:, gs], ytm[:, gs], tmpx, ALU.add, ["ytm", "tmpx"], ["ytm"])
                w8 = cols[:, c, 32 + g * 8:32 + (g + 1) * 8].unsqueeze(2).broadcast_to([128, 8, 64])
                tt(Xw.rearrange("p (h q) -> p h q", h=8), X_tm[:, c, gs].rearrange("p (h q) -> p h q", h=8), w8, ALU.mult,
                   ["X_tm", "cols"], ["Xw"])
                mm(pin, B_tm[:, c, g * 128:(g + 1) * 128], Xw, True, True, ["B_tm", "Xw"], [pink])
                k8 = eclbc[:, c, g * 8:(g + 1) * 8].unsqueeze(2).broadcast_to([128, 8, 64])
                tt(Sm[l][:, gs].rearrange("p (h q) -> p h q", h=8), Sm[l][:, gs].rearrange("p (h q) -> p h q", h=8), k8,
                   ALU.mult, [f"Sm{l}", "eclbc"], [f"Sm{l}"])
                tt(Sm[l][:, gs], Sm[l][:, gs], pin, ALU.add, [f"Sm{l}", pink], [f"Sm{l}"])
                cp("act", Smb[:, gs], Sm[l][:, gs], [f"Sm{l}"], ["Smb"])
            post_block(c, c * 128, 128)

        if samp:
            m.barrier()
            x4 = X_tm[0:NS, 4, :].rearrange("p (h q) -> p h q", h=16)
            tt(Xd[0:NS].rearrange("p (h q) -> p h q", h=16), x4, cols[0:NS, 4, 0:16].unsqueeze(2).broadcast_to([NS, 16, 64]),
               ALU.mult, ["X_tm", "cols"], ["Xd"])
            idb8 = id16.unsqueeze(2).broadcast_to([NS, NS, 8])
            tt(Dme[0:NS], expc[0:NS, 4, 0:16:2].unsqueeze(1).broadcast_to([NS, NS, 8]), idb8, ALU.mult, ["expc", "id16"], ["Dme"])
            tt(Dmo[0:NS], expc[0:NS, 4, 1:16:2].unsqueeze(1).broadcast_to([NS, NS, 8]), idb8, ALU.mult, ["expc", "id16"], ["Dmo"])
            pt, ptk = ps[6], "ps6"
            mm(pt[:, 0:128], E01[:, 0, :], Dme[0:NS].rearrange("p a b -> p (a b)"), True, False, ["E01", "Dme"], [ptk])
            mm(pt[:, 0:128], E01[:, 1, :], Dmo[0:NS].rearrange("p a b -> p (a b)"), False, True, ["E01", "Dmo"], [ptk])
            cp("dve", decbc.rearrange("p a b -> p (a b)"), pt[:, 0:128], [ptk], ["decbc"])
            idb128 = id16.unsqueeze(2).broadcast_to([NS, NS, 128])
            si = 0
            for g in range(2):
                tt(Cm[0:NS], C_tm[0:NS, g * 128:(g + 1) * 128].unsqueeze(1).broadcast_to([NS, NS, 128]), idb128, ALU.mult,
                   ["C_tm", "id16"], ["Cm"])
                for q in range(4):
                    mm(ps[q], onesb[0:NS, :], Cm[0:NS, q * 4:(q + 1) * 4, :].rearrange("p a b -> p (a b)"), True, True,
                       ["onesb", "Cm"], [f"ps{q}"])
                for jj in range(4):
                    j = g * 4 + jj
                    tt(Xdm[0:NS], Xd[0:NS, j * 128:(j + 1) * 128].unsqueeze(1).broadcast_to([NS, NS, 128]), idb128, ALU.mult,
                       ["Xd", "id16"], ["Xdm"])
                    for i in range(NS):
                        s_in, sk_in = Sst[si % 3], f"Sst{si % 3}"
                        s_nw, sk_nw = Snw[si % 3], f"Snw{si % 3}"
                        po, pok = ps[4 + si % 2], f"ps{4 + si % 2}"
                        si += 1
                        dma("sp", s_in, d_sssm[l, i, j], [], [sk_in])
                        mm(po[:, 0:128], Xdm[0:NS, i, :], B_tm[0:NS, 4, g * 128:(g + 1) * 128], True, True,
                           ["Xdm", "B_tm"], [pok])
                        stt(s_nw, s_in, decbc[:, i, j:j + 1], po[:, 0:128], ALU.mult, ALU.add, [sk_in, "decbc", pok], [sk_nw])
                        dma("sp", o_ssms[l, i, j], s_nw, [sk_nw], [f"o_ssms{l}_{i}_{j}"])
                        ttr(junk[:, 0:128], s_nw, ps[i // 4][:, (i % 4) * 128:(i % 4 + 1) * 128], ysT[:, j, i:i + 1],
                            [sk_nw, f"ps{i // 4}"], ["tmpx", "ysT"])
            for half in range(2):
                pt, ptk = ps[4 + half], f"ps{4 + half}"
                for cc in range(4):
                    tr(pt[0:NS, cc * 128:(cc + 1) * 128], ysT[:, half * 4 + cc, :], identf, ["ysT", "identf"], [ptk])
                hs = slice(half * 512, (half + 1) * 512)
                d8 = dskip[0:NS, l, half * 8:(half + 1) * 8].unsqueeze(2).broadcast_to([NS, 8, 64])
                tt(tmpx[0:NS].rearrange("p (h q) -> p h q", h=8), X_tm[0:NS, 4, hs].rearrange("p (h q) -> p h q", h=8), d8,
                   ALU.mult, ["X_tm", "dskip"], ["tmpx"])
                tt(ytm[0:NS, hs], tmpx[0:NS], pt[0:NS, 0:512], ALU.add, ["tmpx", ptk], ["ytm"])
            post_block(4, TT, NS)
        m.barrier()

    def gla_alloc(o0, nh, dv, tagk):
        o = o0
        b = {}
        nvb = nh * dv // 128
        b["qT"] = carve(o, [128, nh, TC], F32); o += nh * TC * 4
        b["kT"] = carve(o, [128, nh, TC], F32); o += nh * TC * 4
        b["gT"] = carve(o, [128, nh, TC], F32); o += nh * TC * 4
        b["sgT"] = carve(o, [128, nvb, TC], BF16); o += nvb * TC * 2
        b["V"] = carve(o, [128, 5, 512], BF16); o += 5120
        b["sig"] = carve(o, [128, TC], F32); o += TC * 4
        b["bT"] = carve(o, [128, 128], F32); o += 512
        b["e1"] = carve(o, [128, 128], F32); o += 512
        b["e2"] = carve(o, [128, 128], F32); o += 512
        b["qt"] = carve(o, [128, 128], BF16); o += 256
        b["kt"] = carve(o, [128, 128], BF16); o += 256
        b["At"] = carve(o, [128, 128], BF16); o += 256
        b["ktm"] = carve(o, [128, 128], BF16); o += 256
        b["Sr"] = carve(o, [128, 256], BF16); o += 512
        b["tU"] = carve(o, [128, 256], F32); o += 1024
        b["sq"] = [carve(o + i * 512, [128, 128], F32) for i in range(2)]; o += 1024
        b["rs"] = carve(o, [128, 128], F32); o += 512
        b["on"] = carve(o, [128, 128], F32); o += 512
        b["cc"] = carve(o, [128, 8], F32); o += 32
        b["ea"] = carve(o, [128, nh, NS], F32); o += nh * NS * 4
        b["Ktm"] = carve(o, [128, 128], BF16); o += 256
        b["Km"] = carve(o, [128, NS, 128], BF16); o += 4096
        b["Sin"] = [carve(o + i * 1024, [128, 256], F32) for i in range(3)]; o += 3072
        b["Snw"] = [carve(o + i * 1024, [128, 256], F32) for i in range(3)]; o += 3072
        assert o <= RBYTES, o
        mset(b["At"][64:128, 0:64], 0.0, [tagk + "At"])
        return b

    def gla_post(b, po, dv, n, outs, sgs, nws, tag):
        nb = dv // 128
        pss, pssk = ps[4], "ps4"
        for blk in range(nb):
            act(b["sq"][blk][:, 0:n], po[:, blk * n:(blk + 1) * n], AF.Square, ["ps2"], [f"{tag}sq{blk}"])
            mm(pss[:, 0:n], onesf, b["sq"][blk][:, 0:n], blk == 0, blk == nb - 1, [f"{tag}sq{blk}", "onesf"], [pssk])
        ts(b["rs"][:, 0:n], pss[:, 0:n], float(dv * EPS), None, ALU.add, None, [pssk], [tag + "rs"])
        act(b["rs"][:, 0:n], b["rs"][:, 0:n], AF.Ln, [tag + "rs"], [tag + "rs"])
        act(b["rs"][:, 0:n], b["rs"][:, 0:n], AF.Exp, [tag + "rs"], [tag + "rs"], scale=-0.5)
        for blk in range(nb):
            tt(b["on"][:, 0:n], po[:, blk * n:(blk + 1) * n], b["rs"][:, 0:n], ALU.mult, ["ps2", tag + "rs"], [tag + "on"])
            stt(outs[blk], b["on"][:, 0:n], nws[blk], sgs[blk], ALU.mult, ALU.mult, [tag + "on", tag + "sg", "hnorm", "gnorm"],
                [tag + "y"])

    def gla_chunk(b, hh, c, S, skey, dv, outs, sgs, nws, tag):
        ch = slice(c * 128, (c + 1) * 128)
        qc, kc, gc = b["qT"][:, hh, ch], b["kT"][:, hh, ch], b["gT"][:, hh, ch]
        Vc = b["V"][:, c, hh * dv:(hh + 1) * dv]
        bT, e1, e2, cc = b["bT"], b["e1"], b["e2"], b["cc"]
        T = tag
        scan(bT, onesf, gc, 0.0, ALU.mult, ALU.add, [T + "gT", "onesf"], [T + "bT"])
        ts(cc[:, 0:1], bT[:, 63:64], -1.0, None, ALU.mult, None, [T + "bT"], [T + "cc"])
        act(e1, bT, AF.Exp, [T + "bT", T + "cc"], [T + "e1"], bias=cc[:, 0:1])
        act(e2, bT, AF.Exp, [T + "bT"], [T + "e2"], scale=-1.0, bias=bT[:, 63:64])
        act(cc[:, 1:2], bT[:, 63:64], AF.Exp, [T + "bT"], [T + "cc"])
        act(cc[:, 2:3], bT[:, 127:128], AF.Exp, [T + "bT"], [T + "cc"])
        act(cc[:, 3:4], bT[:, 127:128], AF.Exp, [T + "bT", T + "cc"], [T + "cc"], bias=cc[:, 0:1])
        tt(b["qt"], qc, e1, ALU.mult, [T + "qT", T + "e1"], [T + "qt"])
        tt(b["kt"], kc, e2, ALU.mult, [T + "kT", T + "e2"], [T + "kt"])
        pa, pak = ps[0], "ps0"
        mm(pa[:, 64:128], b["kt"], b["qt"][:, 64:128], True, True, [T + "kt", T + "qt"], [pak])
        mm(pa[0:64, 0:64], b["kt"][:, 0:64], b["qt"][:, 0:64], True, True, [T + "kt", T + "qt"], [pak])
        tt(b["At"][:, 64:128], pa[:, 64:128], trif[:, 64:128], ALU.mult, [pak, "trif"], [T + "At"])
        tt(b["At"][0:64, 0:64], pa[0:64, 0:64], trif[0:64, 0:64], ALU.mult, [pak, "trif"], [T + "At"])
        pk_, pkk = ps[1], "ps1"
        pkb = pk_.bitcast(BF16)
        tr(pkb[:, 0:128], b["kt"], identb, [T + "kt", "identb"], [pkk])
        cp("act", b["ktm"], pkb[:, 0:128], [pkk], [T + "ktm"])
        ts(b["Sr"][:, 0:dv], S, cc[:, 1:2], None, ALU.mult, None, [skey, T + "cc"], [T + "Sr"])
        po, pok = ps[2], "ps2"
        for blk in range(dv // 128):
            mm(po[:, blk * 128:(blk + 1) * 128], Vc[:, blk * 128:(blk + 1) * 128], b["At"], True, False, [T + "V", T + "At"], [pok])
            mm(po[:, blk * 128:(blk + 1) * 128], b["Sr"][:, blk * 128:(blk + 1) * 128], b["qt"], False, True,
               [T + "Sr", T + "qt"], [pok])
        pu, puk = ps[3], "ps3"
        mm(pu[:, 0:dv], b["ktm"], Vc, True, True, [T + "ktm", T + "V"], [puk])
        ts(b["tU"][:, 0:dv], pu[:, 0:dv], cc[:, 3:4], None, ALU.mult, None, [puk, T + "cc"], [T + "tU"])
        stt(S, S, cc[:, 2:3], b["tU"][:, 0:dv], ALU.mult, ALU.add, [skey, T + "cc", T + "tU"], [skey])
        gla_post(b, po, dv, 128, outs, sgs, nws, T)

    def gla_decode(b, l, hh, h, S_dram_in, S_dram_out, dv, outs, sgs, nws, tag):
        T = tag
        idb128 = id16.unsqueeze(2).broadcast_to([NS, NS, 128])
        pk_, pkk = ps[1], "ps1"
        tr(pk_[0:NS, 0:128], b["kT"][:, hh, TT:TC], identf, [T + "kT", "identf"], [pkk])
        cp("act", b["Ktm"][0:NS], pk_[0:NS, 0:128], [pkk], [T + "Ktm"])
        tt(b["Km"][0:NS], b["Ktm"][0:NS].unsqueeze(1).broadcast_to([NS, NS, 128]), idb128, ALU.mult, [T + "Ktm", "id16"], [T + "Km"])
        po, pok = ps[2], "ps2"
        nb = dv // 128
        for i in range(NS):
            s_in, sk_in = b["Sin"][i % 3], f"{T}Sin{i % 3}"
            s_nw, sk_nw = b["Snw"][i % 3], f"{T}Snw{i % 3}"
            pu, puk = ps[5 + i % 2], f"ps{5 + i % 2}"
            dma("sp", s_in[:, 0:dv], S_dram_in[l, i, h], [], [sk_in])
            mm(pu[:, 0:dv], b["Km"][0:NS, i, :], b["V"][0:NS, 4, hh * dv:(hh + 1) * dv], True, True, [T + "Km", T + "V"], [puk])
            stt(s_nw[:, 0:dv], s_in[:, 0:dv], b["ea"][:, hh, i:i + 1], pu[:, 0:dv], ALU.mult, ALU.add, [sk_in, T + "ea", puk], [sk_nw])
            dma("sp", S_dram_out[l, i, h], s_nw[:, 0:dv], [sk_nw], [f"{T}o_{l}_{i}_{h}"])
            for blk in range(nb):
                mm(po[:, blk * NS + i: blk * NS + i + 1], s_nw[:, blk * 128:(blk + 1) * 128], b["qT"][:, hh, TT + i:TT + i + 1],
                   True, True, [sk_nw, T + "qT"], [pok])
        gla_post(b, po, dv, NS, outs, sgs, nws, T)

    def hgrn(t, l):
        samp = (t == 0)
        for hf in range(2):
            b = gla_alloc(Y_OFF, 4, 128, "h")
            T = "h"

            def ev_q(c, c0, n, pp, pk):
                act(b["qT"][:, c, c0:c0 + n], pp, AF.Silu, [pk], [T + "qT"])
                ts(b["qT"][:, c, c0:c0 + n], b["qT"][:, c, c0:c0 + n], float(128.0 ** -0.5), None, ALU.mult, None, [T + "qT"], [T + "qT"])

            def ev_f(c, c0, n, pp, pk):
                chn = hf * 4 + c
                act(b["sig"][:, 0:n], pp, AF.Sigmoid, [pk], [T + "sig"])
                ts(b["gT"][:, c, c0:c0 + n], b["sig"][:, 0:n], oml[:, l, chn:chn + 1], lb[:, l, chn:chn + 1], ALU.mult, ALU.add,
                   [T + "sig", "oml", "lb"], [T + "gT"])
                act(b["gT"][:, c, c0:c0 + n], b["gT"][:, c, c0:c0 + n], AF.Ln, [T + "gT"], [T + "gT"])
                ts(b["kT"][:, c, c0:c0 + n], b["sig"][:, 0:n], noml[:, l, chn:chn + 1], oml[:, l, chn:chn + 1], ALU.mult, ALU.add,
                   [T + "sig", "oml", "noml"], [T + "kT"])

            def ev_g(c, c0, n, pp, pk):
                act(b["sgT"][:, c, c0:c0 + n], pp, AF.Silu, [pk], [T + "sg"])

            def ev_v(bi, rows, co, w, pp, pk):
                cp("act", b["V"][0:rows, bi, co:co + w], pp, [pk], [T + "V"])

            proj_fm(t, l, O_HQ + hf * 512, 512, ev_q)
            proj_fm(t, l, O_HF + hf * 512, 512, ev_f)
            proj_fm(t, l, O_HG + hf * 512, 512, ev_g)
            proj_tm(t, l, O_HI + hf * 512, 512, ev_v)
            for c in range(4):
                for hh in range(4):
                    h = hf * 4 + hh
                    gla_chunk(b, hh, c, Sh[l][:, h * 128:(h + 1) * 128], f"Sh{l}_{h}", 128,
                              [yhT[:, h, c * 128:(c + 1) * 128]], [b["sgT"][:, hh, c * 128:(c + 1) * 128]],
                              [hnorm[:, l:l + 1]], T)
            if samp:
                act(b["ea"], b["gT"][:, :, TT:TC], AF.Exp, [T + "gT"], [T + "ea"])
                for hh in range(4):
                    h = hf * 4 + hh
                    gla_decode(b, l, hh, h, d_shgrn, o_hgrns, 128, [yhT[:, h, TT:TC]], [b["sgT"][:, hh, TT:TC]],
                               [hnorm[:, l:l + 1]], T)
            m.barrier()

    def gla(t, l):
        samp = (t == 0)
        ncol = TC if samp else TT
        for gf in range(2):
            b = gla_alloc(Y_OFF, 2, 256, "g")
            gaT = b["sig"]
            T = "g"

            def ev_q(c, c0, n, pp, pk):
                act(b["qT"][:, c, c0:c0 + n], pp, AF.Copy, [pk], [T + "qT"], scale=float(128.0 ** -0.5))

            def ev_k(c, c0, n, pp, pk):
                cp("act", b["kT"][:, c, c0:c0 + n], pp, [pk], [T + "kT"])

            def ev_g(c, c0, n, pp, pk):
                act(b["sgT"][:, c, c0:c0 + n], pp, AF.Silu, [pk], [T + "sg"])

            def ev_a(c, c0, n, pp, pk):
                cp("act", gaT[0:16, c0:c0 + n], pp, [pk], [T + "ga"])

            def ev_v(bi, rows, co, w, pp, pk):
                cp("act", b["V"][0:rows, bi, co:co + w], pp, [pk], [T + "V"])

            proj_fm(t, l, O_GQ + gf * 256, 256, ev_q)
            proj_fm(t, l, O_GK + gf * 256, 256, ev_k)
            proj_fm(t, l, O_GG + gf * 512, 512, ev_g)
            proj_fm(t, l, O_GA, 16, ev_a)
            for hh in range(2):
                chn = gf * 2 + hh
                for (c0, n) in colblocks(t):
                    pp, pk = next_ps(4)
                    mm(pp[:, 0:n], wdec[:, l, chn * 128:(chn + 1) * 128], gaT[0:16, c0:c0 + n], True, True, ["wdec", T + "ga"], [pk])
                    act(b["gT"][:, hh, c0:c0 + n], pp[:, 0:n], AF.Exp, [pk, "bdec"], [T + "gT"], scale=-1.0, bias=bdec[:, l, chn:chn + 1])
            act(b["gT"][:, :, 0:ncol], b["gT"][:, :, 0:ncol], AF.Ln, [T + "gT"], [T + "gT"], bias=1.0)
            ts(b["gT"][:, :, 0:ncol], b["gT"][:, :, 0:ncol], -1.0 / 16.0, None, ALU.mult, None, [T + "gT"], [T + "gT"])
            proj_tm(t, l, O_GV + gf * 512, 512, ev_v)
            for c in range(4):
                for hh in range(2):
                    h = gf * 2 + hh
                    gla_chunk(b, hh, c, Sg[l][:, h * 256:(h + 1) * 256], f"Sg{l}_{h}", 256,
                              [ygT[:, h * 2 + k2, c * 128:(c + 1) * 128] for k2 in range(2)],
                              [b["sgT"][:, hh * 2 + k2, c * 128:(c + 1) * 128] for k2 in range(2)],
                              [gnorm[:, l, k2:k2 + 1] for k2 in range(2)], T)
            if samp:
                act(b["ea"], b["gT"][:, :, TT:TC], AF.Exp, [T + "gT"], [T + "ea"])
                for hh in range(2):
                    h = gf * 2 + hh
                    gla_decode(b, l, hh, h, d_sgla, o_glas, 256,
                               [ygT[:, h * 2 + k2, TT:TC] for k2 in range(2)],
                               [b["sgT"][:, hh * 2 + k2, TT:TC] for k2 in range(2)],
                               [gnorm[:, l, k2:k2 + 1] for k2 in range(2)], T)
            m.barrier()

    def merge(t, l):
        merged = carve(Y_OFF, [128, 16, TC], BF16)
        o = Y_OFF + 16 * TC * 2
        macc = [carve(o + i * TC * 4, [128, TC], F32) for i in range(2)]; o += 2 * TC * 4
        sgm = carve(o, [128, TC], F32); o += TC * 4
        mtmp = carve(o, [128, TC], F32); o += TC * 4
        assert o <= RBYTES
        ysrc = (ymT, yhT, ygT)
        cbs = colblocks(t)
        for fo2 in range(8):
            for bidx in range(3):
                wv, wk = load_slab(w_br[bidx][l][:, fo2 * 256:(fo2 + 1) * 256], 8, 256)
                gv, gk = load_slab(w_in[l][:, O_GATE + bidx * 2048 + fo2 * 256: O_GATE + bidx * 2048 + (fo2 + 1) * 256], 16, 256)
                for jj in range(2):
                    fo = fo2 * 2 + jj
                    for (c0, n) in cbs:
                        pP, pPk = next_ps(4)
                        pG, pGk = next_ps(4)
                        for k in range(8):
                            mm(pP[:, 0:n], wv[:, k, jj * 128:(jj + 1) * 128], ysrc[bidx][:, k, c0:c0 + n], k == 0, k == 7,
                               [wk, "ymT", "hy", "gy"], [pPk])
                        for k in range(16):
                            mm(pG[:, 0:n], gv[:, k, jj * 128:(jj + 1) * 128], hn[:, k, c0:c0 + n], k == 0, k == 15,
                               [gk, f"hn{k}"], [pGk])
                        act(sgm[:, 0:n], pG[:, 0:n], AF.Sigmoid, [pGk], ["sgm"])
                        mk = f"macc{jj}"
                        if bidx == 0:
                            tt(macc[jj][:, c0:c0 + n], sgm[:, 0:n], pP[:, 0:n], ALU.mult, ["sgm", pPk], [mk])
                        elif bidx == 1:
                            tt(mtmp[:, 0:n], sgm[:, 0:n], pP[:, 0:n], ALU.mult, ["sgm", pPk], ["mtmp"])
                            tt(macc[jj][:, c0:c0 + n], macc[jj][:, c0:c0 + n], mtmp[:, 0:n], ALU.add, [mk, "mtmp"], [mk])
                        else:
                            tt(mtmp[:, 0:n], sgm[:, 0:n], pP[:, 0:n], ALU.mult, ["sgm", pPk], ["mtmp"])
                            tt(merged[:, fo, c0:c0 + n], macc[jj][:, c0:c0 + n], mtmp[:, 0:n], ALU.add, [mk, "mtmp"], [f"mg{fo}"])
        for fo2 in range(8):
            wv, wk = load_slab(w_out[l][:, fo2 * 256:(fo2 + 1) * 256], 16, 256)
            for jj in range(2):
                fo = fo2 * 2 + jj
                for (c0, n) in cbs:
                    pp, pk = next_ps(4)
                    for k in range(16):
                        mm(pp[:, 0:n], wv[:, k, jj * 128:(jj + 1) * 128], merged[:, k, c0:c0 + n], k == 0, k == 15,
                           [wk, f"mg{k}"], [pk])
                    tt(xT[:, fo, c0:c0 + n], pp[:, 0:n], xT[:, fo, c0:c0 + n], ALU.add, [pk, f"xT{fo}"], [f"xT{fo}"])
        m.barrier()

    def mixer(t, l):
        rmsnorm_fm(t, 2 + l, hn, "hn")
        m.barrier()
        if "mamba" in mixparts:
            mamba(t, l)
        if "hgrn" in mixparts:
            hgrn(t, l)
        if "gla" in mixparts:
            gla(t, l)
        if "merge" in mixparts:
            merge(t, l)
        m.barrier()

    for (dst, src, key) in ((identf, d_identf, "identf"), (trif, d_trif, "trif"), (negm, d_negm, "negm"),
                            (nrm, d_nrm, "nrm")):
        dma("sp", dst, src, [], [key])
    for (dst, src, key) in ((convw, d_convw, "convw"), (convb, d_convb, "convb"), (hd, d_hd, "hd"),
                            (dskip, d_dskip, "dskip"), (mnormF, d_mnorm, "mnormF"), (lbl, d_lbl, "lbl"),
                            (hnorm, d_hnorm, "hnorm"), (gnorm, d_gnorm, "gnorm"), (wdec, d_wdec, "wdec"),
                            (bdec, d_bdec, "bdec"), (sel, d_sel, "sel"), (id16, d_id16, "id16")):
        dma("sp", dst, src, [], [key])
    act(aneg, hd[:, 2:4], AF.Exp, ["hd"], ["aneg"])
    ts(aneg, aneg, -1.0, None, ALU.mult, None, ["aneg"], ["aneg"])
    mset(lb, 0.0, ["lb"])
    tt(lb[:, 1, :], lbl[:, 1, :], lbl[:, 0, :], ALU.subtract, ["lbl", "lb"], ["lb"])
    act(lb[:, 1, :], lb[:, 1, :], AF.Sigmoid, ["lb"], ["lb"])
    ts(oml, lb, -1.0, 1.0, ALU.mult, ALU.add, ["lb"], ["oml"])
    ts(noml, lb, 1.0, -1.0, ALU.mult, ALU.add, ["lb"], ["noml"])
    ts(hnorm, hnorm, float(np.sqrt(128.0)), None, ALU.mult, None, ["hnorm"], ["hnorm"])
    ts(gnorm, gnorm, 16.0, None, ALU.mult, None, ["gnorm"], ["gnorm"])
    ts(bdec, bdec, -1.0, None, ALU.mult, None, ["bdec"], ["bdec"])
    mset(E01, 0.0, ["E01"])
    mset(E01[:, 0, 0:64], 1.0, ["E01"])
    mset(E01[:, 1, 64:128], 1.0, ["E01"])
    for l in range(2):
        mset(Sm[l], 0.0, [f"Sm{l}"])
        mset(Sh[l], 0.0, [f"Sh{l}_{h}" for h in range(8)])
        mset(Sg[l], 0.0, [f"Sg{l}_{h}" for h in range(4)])
        mset(cst[l], 0.0, [f"cst{l}"])
    mset(onesf, 1.0, ["onesf"])
    mset(onesb, 1.0, ["onesb"])
    cp("dve", identb, identf, ["identf"], ["identb"])

    for t in range(NT):
        dma("sp", xT[:, :, 0:TT], xTp[:, t * TT:(t + 1) * TT].rearrange("(c p) n -> p c n", p=128), [], XT_ALL)
        if t == 0:
            dma("sp", xT[:, :, TT:TC], xTs.rearrange("(c p) n -> p c n", p=128), [], XT_ALL)
        for l in range(nlayers):
            if "ffn1" in stages:
                ffn(t, l, 0)
            if "mix" in stages:
                mixer(t, l)
            if "ffn2" in stages:
                ffn(t, l, 1)
        m.barrier()
        rmsnorm_fm(t, 6, xT, "xT")
        dma("sp", yTp[:, t * TT:(t + 1) * TT].rearrange("(c p) n -> p c n", p=128), xT[:, :, 0:TT], XT_ALL, ["yTp"])
        if t == 0:
            dma("sp", yTs.rearrange("(c p) n -> p c n", p=128), xT[:, :, TT:TC], XT_ALL, ["yTs"])
        m.barrier()
    for l in range(2):
        dma("sp", o_convp[l], cst[l], [f"cst{l}"], [f"o_convp{l}"])
        dma("sp", o_ssmp[l], Sm[l], [f"Sm{l}"], [f"o_ssmp{l}"])
        dma("sp", o_hgrnp[l], Sh[l], [f"Sh{l}_{h}" for h in range(8)], [f"o_hgrnp{l}"])
        dma("sp", o_glap[l], Sg[l], [f"Sg{l}_{h}" for h in range(4)], [f"o_glap{l}"])
    m.wait_all("sp")
    m.emit()
    return nc


def _consts():
    i = np.arange(128)
    identf = np.eye(128, dtype=np.float32)
    trif = (i[:, None] <= i[None, :]).astype(np.float32)
    negm = np.where(i[:, None] <= i[None, :], 0.0, -30000.0).astype(np.float32)
    sel = np.zeros((16, 16, 128), np.float32)
    for h in range(16):
        sel[h, h, :] = 1.0
    id16 = np.eye(16, dtype=np.float32)
    return dict(identf=identf, trif=trif, negm=negm, sel=sel, id16=id16)


def _fm(v, nch):
    v = np.asarray(v, np.float32)
    lead = v.shape[:-1]
    v = v.reshape(lead + (nch, 128))
    v = np.moveaxis(v, -1, 0)
    return np.ascontiguousarray(v)


def _shared_inputs(inp):
    f = lambda k: np.ascontiguousarray(np.asarray(inp[k], np.float32))
    sh = dict(w_gu1=f("ffn1_w_gate_up"), w_d1=f("ffn1_w_down"), w_gu2=f("ffn2_w_gate_up"), w_d2=f("ffn2_w_down"),
              w_in=f("w_in"), w_bm=f("w_branch_mamba"), w_bh=f("w_branch_hgrn"), w_bg=f("w_branch_gla"),
              w_out=f("w_out"))
    nrm = np.stack([f("ffn1_norm")[0], f("ffn1_norm")[1], f("mix_norm")[0], f("mix_norm")[1],
                    f("ffn2_norm")[0], f("ffn2_norm")[1], f("final_norm")], 0)
    sh["nrm"] = _fm(nrm, 16)
    sh["convw"] = np.ascontiguousarray(np.transpose(_fm(f("conv_w"), 12), (0, 1, 3, 2)))
    sh["convb"] = _fm(f("conv_b"), 12)
    sh["hd"] = np.ascontiguousarray(np.concatenate([f("dt_bias").T, f("a_log").T], 1))
    sh["dskip_bc"] = np.ascontiguousarray(np.broadcast_to(f("d_skip")[None], (128, 2, 16)))
    sh["mnormF"] = _fm(f("mamba_norm"), 8)
    sh["lbl"] = _fm(f("hgrn_lb_logits"), 8)
    sh["hnorm"] = np.ascontiguousarray(f("hgrn_norm").T)
    sh["gnorm"] = _fm(f("gla_norm"), 2)
    sh["wdec"] = np.ascontiguousarray(np.transpose(f("gla_w_decay"), (1, 0, 2)))
    sh["bdec"] = _fm(f("gla_b_decay"), 4)
    sh.update(_consts())
    return sh


def _core_inputs(inp, sh, b, tok0, NT, s0):
    xp = np.asarray(inp["x_prompt"], np.float32)
    xs = np.asarray(inp["x_sample"], np.float32)
    d = dict(sh)
    d["xTp"] = np.ascontiguousarray(xp[b, tok0:tok0 + NT * TT, :].T)
    d["xTs"] = np.ascontiguousarray(xs[s0:s0 + NS, 0, :].T)
    sc = np.asarray(inp["state_conv"], np.float32)[:, s0:s0 + NS]
    sc = sc.reshape(2, NS, 3, 12, 128)
    d["sconv"] = np.ascontiguousarray(np.transpose(sc, (4, 0, 3, 1, 2)))
    d["s_ssm"] = np.ascontiguousarray(np.asarray(inp["state_ssm"], np.float32)[:, s0:s0 + NS].reshape(2, NS, 8, 128, 128))
    d["s_hgrn"] = np.ascontiguousarray(np.asarray(inp["state_hgrn"], np.float32)[:, s0:s0 + NS])
    d["s_gla"] = np.ascontiguousarray(np.asarray(inp["state_gla"], np.float32)[:, s0:s0 + NS])
    return d


def _unpack_core(r, NT):
    o = {}
    o["y_p"] = np.ascontiguousarray(r["yTp"].T)
    o["y_s"] = np.ascontiguousarray(r["yTs"].T)
    o["conv_p"] = np.transpose(r["conv_p"], (0, 3, 2, 1)).reshape(2, 3, 1536)
    o["ssm_p"] = np.transpose(r["ssm_p"].reshape(2, 128, 16, 64), (0, 2, 3, 1))
    o["hgrn_p"] = np.transpose(r["hgrn_p"].reshape(2, 128, 8, 128), (0, 2, 1, 3))
    o["gla_p"] = np.transpose(r["gla_p"].reshape(2, 128, 4, 256), (0, 2, 1, 3))
    o["conv_s"] = np.transpose(r["conv_s"], (0, 3, 4, 2, 1)).reshape(2, NS, 3, 1536)
    o["ssm_s"] = r["ssm_s"].reshape(2, NS, 16, 64, 128)
    o["hgrn_s"] = r["hgrn_s"]
    o["gla_s"] = r["gla_s"]
    return o


_NC_CACHE = {}


def kernel(**inputs):
    NT = 4
    key = ("full", NT)
    if key not in _NC_CACHE:
        _NC_CACHE[key] = build(NT)
    nc = _NC_CACHE[key]
    sh = _shared_inputs(inputs)
    in_maps = [_core_inputs(inputs, sh, c % 4, 0, NT, c * NS) for c in range(8)]
    res = run_bass_kernel_spmd(nc, in_maps, core_ids=list(range(8)))
    outs = [_unpack_core(r, NT) for r in res.results]
    y_prompt = np.stack([outs[b]["y_p"] for b in range(4)], 0)
    y_sample = np.concatenate([o["y_s"] for o in outs], 0)[:, None, :]
    st = lambda k: np.ascontiguousarray(np.stack([outs[b][k] for b in range(4)], 1))
    cat = lambda k: np.ascontiguousarray(np.concatenate([o[k] for o in outs], 1))
    return (y_prompt.astype(np.float32), y_sample.astype(np.float32),
            st("conv_p"), st("ssm_p"), st("hgrn_p"), st("gla_p"),
            cat("conv_s"), cat("ssm_s"), cat("hgrn_s"), cat("gla_s"))
```

```python
import os
import numpy as np
import ml_dtypes
import concourse.bass as bass
import concourse.mybir as mybir
from concourse.bass_utils import run_bass_kernel_spmd

F32 = mybir.dt.float32
BF16 = mybir.dt.bfloat16
AF = mybir.ActivationFunctionType
ALU = mybir.AluOpType

ENGS = ("pe", "act", "dve", "pool", "sp")

D = 2048
DFF = 5632
TT = 512
NS = 16
TC = TT + NS
EPS = 1e-6
O_Z, O_XBC, O_DT, O_HQ, O_HF, O_HI, O_HG, O_GQ, O_GK, O_GV, O_GG, O_GA, O_GATE = (
    0, 1024, 2560, 2576, 3600, 4624, 5648, 6672, 7184, 7696, 8720, 9744, 9760)


class MK:
    N_DMA_SEMS = 6

    def __init__(self, nc):
        self.nc = nc
        self.lists = {e: [] for e in ENGS}
        self.cnt = {e: 0 for e in ENGS}
        self.sem = {e: nc.alloc_semaphore(name=f"c_{e}") for e in ENGS}
        self.dsem = {e: [nc.alloc_semaphore(name=f"d_{e}{i}") for i in range(self.N_DMA_SEMS)]
                     for e in ("sp", "pool", "act")}
        self.dcnt = {e: 0 for e in self.dsem}
        self.dval = {e: [0] * self.N_DMA_SEMS for e in self.dsem}
        self.known = {e: {} for e in ENGS}
        self.last_w = {}
        self.readers = {}
        self.n_instr = 0

    def all_sems(self):
        return list(self.sem.values()) + [s for v in self.dsem.values() for s in v]

    def _need(self, eng, tok, waits):
        if tok is None:
            return
        sem, val, sid = tok
        if self.known[eng].get(sid, 0) >= val:
            return
        prev = waits.get(sid)
        if prev is None or prev[1] < val:
            waits[sid] = (sem, val)

    def _deps(self, eng, reads, writes, skip_self=False):
        waits = {}
        own = "c_" + eng
        for k in reads:
            self._need(eng, self.last_w.get(k), waits)
        for k in writes:
            self._need(eng, self.last_w.get(k), waits)
            for t in self.readers.get(k, ()):
                self._need(eng, t, waits)
        if skip_self:
            waits.pop(own, None)
        for sid, (sem, val) in waits.items():
            self.known[eng][sid] = val
        return list(waits.values())

    def _commit(self, tok, reads, writes):
        for k in writes:
            self.last_w[k] = tok
            self.readers[k] = []
        for k in reads:
            self.readers.setdefault(k, []).append(tok)

    def op(self, eng, fn, reads=(), writes=(), skip_self=False):
        waits = self._deps(eng, reads, writes, skip_self)
        self.cnt[eng] += 1
        tok = (self.sem[eng], self.cnt[eng], "c_" + eng)
        self.lists[eng].append((waits, fn, self.sem[eng], 1))
        self._commit(tok, reads, writes)
        self.n_instr += 1
        return tok

    def dma(self, q, fn, reads=(), writes=()):
        waits = self._deps(q, reads, writes)
        i = self.dcnt[q] % self.N_DMA_SEMS
        self.dcnt[q] += 1
        sem = self.dsem[q][i]
        sid = f"d_{q}{i}"
        prev = self.dval[q][i]
        if prev and self.known[q].get(sid, 0) < prev:
            waits.append((sem, prev))
            self.known[q][sid] = prev
        self.dval[q][i] = prev + 16
        tok = (sem, prev + 16, sid)
        self.lists[q].append((waits, fn, sem, 16))
        self._commit(tok, reads, writes)
        self.n_instr += 1
        return tok

    def _all_tokens(self, skip_pool):
        toks = []
        for e in ENGS:
            if self.cnt[e] and not (skip_pool and e == "pool"):
                toks.append((self.sem[e], self.cnt[e], "c_" + e))
        for q in self.dsem:
            if skip_pool and q == "pool":
                continue
            for i in range(self.N_DMA_SEMS):
                if self.dval[q][i]:
                    toks.append((self.dsem[q][i], self.dval[q][i], f"d_{q}{i}"))
        return toks

    def barrier(self, engines=("pe", "act", "dve", "sp")):
        toks = self._all_tokens(skip_pool=True)
        for eng in engines:
            waits = {}
            for t in toks:
                self._need(eng, t, waits)
            for sid, (sem, val) in waits.items():
                self.known[eng][sid] = val
            if waits:
                self.lists[eng].append((list(waits.values()), None, None, 0))

    def wait_all(self, eng):
        waits = {}
        for t in self._all_tokens(skip_pool=False):
            self._need(eng, t, waits)
        self.lists[eng].append((list(waits.values()), None, None, 0))

    def emit(self):
        nc = self.nc
        lists = self.lists
        sems = self.all_sems()
        with nc.Block() as b0:
            @b0.sync
            def _(e):
                for s in sems:
                    e.sem_clear(s)
        with nc.Block() as block:
            def run(e, items):
                for waits, fn, sem, inc in items:
                    for (s, v) in waits:
                        e.wait_ge(s, v)
                    if fn is not None:
                        fn(e).then_inc(sem, inc)

            @block.tensor
            def _(e):
                run(e, lists["pe"])

            @block.scalar
            def _(e):
                run(e, lists["act"])

            @block.vector
            def _(e):
                run(e, lists["dve"])

            @block.gpsimd
            def _(e):
                run(e, lists["pool"])

            @block.sync
            def _(e):
                run(e, lists["sp"])


def build(NT, stages=("ffn1", "mix", "ffn2"), nlayers=2, dbg=None, mixparts=("mamba", "hgrn", "gla", "merge")):
    nc = bass.Bass("TRN2", target_bir_lowering=False)
    m = MK(nc)
    NTOK = NT * TT

    def din(name, shape, dt=F32):
        return nc.dram_tensor(name, list(shape), dt, kind="ExternalInput").ap()

    def dout(name, shape, dt=F32):
        return nc.dram_tensor(name, list(shape), dt, kind="ExternalOutput").ap()

    def sb(name, shape, dt=F32):
        return nc.alloc_sbuf_tensor("s_" + name, list(shape), dt).ap()

    xTp = din("xTp", [D, NTOK]); xTs = din("xTs", [D, NS])
    w_gu = [din("w_gu1", [2, D, 2 * DFF]), din("w_gu2", [2, D, 2 * DFF])]
    w_dn = [din("w_d1", [2, DFF, D]), din("w_d2", [2, DFF, D])]
    w_in = din("w_in", [2, D, 15904])
    w_br = [din("w_bm", [2, 1024, D]), din("w_bh", [2, 1024, D]), din("w_bg", [2, 1024, D])]
    w_out = din("w_out", [2, D, D])
    d_nrm = din("nrm", [128, 7, 16])
    d_convw = din("convw", [128, 2, 12, 4]); d_convb = din("convb", [128, 2, 12])
    d_hd = din("hd", [16, 4])
    d_dskip = din("dskip_bc", [128, 2, 16]); d_mnorm = din("mnormF", [128, 2, 8])
    d_lbl = din("lbl", [128, 2, 8]); d_hnorm = din("hnorm", [128, 2]); d_gnorm = din("gnorm", [128, 2, 2])
    d_wdec = din("wdec", [16, 2, 512]); d_bdec = din("bdec", [128, 2, 4])
    d_sconv = din("sconv", [128, 2, 12, NS, 3])
    d_sssm = din("s_ssm", [2, NS, 8, 128, 128])
    d_shgrn = din("s_hgrn", [2, NS, 8, 128, 128]); d_sgla = din("s_gla", [2, NS, 4, 128, 256])
    d_identf = din("identf", [128, 128]); d_trif = din("trif", [128, 128]); d_negm = din("negm", [128, 128])
    d_sel = din("sel", [16, 16, 128])
    d_id16 = din("id16", [16, 16])

    yTp = dout("yTp", [D, NTOK]); yTs = dout("yTs", [D, NS])
    o_convp = dout("conv_p", [2, 128, 12, 3]); o_ssmp = dout("ssm_p", [2, 128, 1024])
    o_hgrnp = dout("hgrn_p", [2, 128, 1024]); o_glap = dout("gla_p", [2, 128, 1024])
    o_convs = dout("conv_s", [2, 128, 12, NS, 3]); o_ssms = dout("ssm_s", [2, NS, 8, 128, 128])
    o_hgrns = dout("hgrn_s", [2, NS, 8, 128, 128]); o_glas = dout("gla_s", [2, NS, 4, 128, 256])
    dbg_out = {}
    if dbg:
        for name, shape in dbg.items():
            dbg_out[name] = dout("dbg_" + name, shape)

    xT = sb("xT", [128, 16, TC]); hn = sb("hn", [128, 16, TC], BF16)
    NSLAB = 4
    slabs = [sb(f"slab{i}", [128, 4096], BF16) for i in range(NSLAB)]
    identf = sb("identf", [128, 128]); identb = sb("identb", [128, 128], BF16)
    trif = sb("trif", [128, 128]); negm = sb("negm", [128, 128])
    onesf = sb("onesf", [128, 128]); onesb = sb("onesb", [128, 128], BF16)
    nrm = sb("nrm", [128, 7, 16])
    RBYTES = 84480
    convw = sb("convw", [128, 2, 12, 4]); convb = sb("convb", [128, 2, 12])
    hd = sb("hd", [16, 4]); aneg = sb("aneg", [16, 2])
    dskip = sb("dskip", [128, 2, 16]); mnormF = sb("mnormF", [128, 2, 8])
    lbl = sb("lbl", [128, 2, 8]); lb = sb("lb", [128, 2, 8]); oml = sb("oml", [128, 2, 8]); noml = sb("noml", [128, 2, 8])
    hnorm = sb("hnorm", [128, 2]); gnorm = sb("gnorm", [128, 2, 2])
    wdec = sb("wdec", [16, 2, 512]); bdec = sb("bdec", [128, 2, 4])
    sel = sb("sel", [16, 16, 128]); id16 = sb("id16", [16, 16]); E01 = sb("E01", [16, 2, 128])
    Sm = [sb(f"Sm{l}", [128, 1024]) for l in range(2)]
    Sh = [sb(f"Sh{l}", [128, 1024]) for l in range(2)]
    Sg = [sb(f"Sg{l}", [128, 1024]) for l in range(2)]
    cst = [sb(f"cst{l}", [128, 12, 3]) for l in range(2)]
    Smb = sb("Smb", [128, 1024], BF16)
    R = sb("R", [128, RBYTES // 4])
    ps = [nc.alloc_psum_tensor(f"p_ps{i}", [128, 512], F32).ap() for i in range(8)]

    def carve(off, shape, dt):
        esz = 2 if dt == BF16 else 4
        n = int(np.prod(shape[1:]))
        assert off % 4 == 0 and off + n * esz <= RBYTES, (off, shape)
        v = R[:, off // 4: off // 4 + (n * esz + 3) // 4]
        if dt != F32:
            v = v.bitcast(dt)[:, 0:n]
        if len(shape) == 3:
            v = v.rearrange("p (a b) -> p a b", a=shape[1])
        elif len(shape) == 4:
            v = v.rearrange("p (a b c) -> p a b c", a=shape[1], b=shape[2])
        return v[0:shape[0]]

    sqs = [carve(RBYTES - 6144 + i * 2048, [128, 512], F32) for i in range(2)]
    rstd = carve(RBYTES - 2048, [128, 512], F32)

    def act(out, in_, func, r, w, **kw):
        return m.op("act", lambda e: e.activation(out=out, in_=in_, func=func, **kw), r, w)

    def ts(out, in0, s1, s2, op0, op1, r, w):
        if op1 is None:
            return m.op("dve", lambda e: e.tensor_scalar(out=out, in0=in0, scalar1=s1, scalar2=None, op0=op0), r, w)
        return m.op("dve", lambda e: e.tensor_scalar(out=out, in0=in0, scalar1=s1, scalar2=s2, op0=op0, op1=op1), r, w)

    def stt(out, in0, scalar, in1, op0, op1, r, w):
        return m.op("dve", lambda e: e.scalar_tensor_tensor(out=out, in0=in0, scalar=scalar, in1=in1,
                                                            op0=op0, op1=op1), r, w)

    def tt(out, in0, in1, op, r, w):
        return m.op("dve", lambda e: e.tensor_tensor(out=out, in0=in0, in1=in1, op=op), r, w)

    def cp(eng, out, in_, r, w):
        if eng == "act":
            return m.op("act", lambda e: e.copy(out=out, in_=in_), r, w)
        return m.op(eng, lambda e: e.tensor_copy(out=out, in_=in_), r, w)

    def ttr(out, in0, in1, accum, r, w):
        m.op("dve", lambda e: e.tensor_tensor(out=out, in0=in0, in1=in1, op=ALU.mult), r, [w[0]])
        return m.op("dve", lambda e: e.reduce_sum(out=accum, in_=out, axis=mybir.AxisListType.X), [w[0]], w[1:])

    def mset(out, val, w):
        return m.op("dve", lambda e: e.memset(out, val), [], w)

    def mm(out, lhsT, rhs, start, stop, r, w):
        return m.op("pe", lambda e: e.matmul(out, lhsT=lhsT, rhs=rhs, start=start, stop=stop), r, w, skip_self=True)

    def tr(out, in_, ident, r, w):
        return m.op("pe", lambda e: e.transpose(out, in_, ident), r, w, skip_self=True)

    def dma(q, out, in_, r, w):
        return m.dma(q, lambda e: e.dma_start(out=out, in_=in_), r, w)

    def scan(out, d0, d1, init, op0, op1, r, w):
        return m.op("dve", lambda e: e.tensor_tensor_scan(out=out, data0=d0, data1=d1, initial=init,
                                                          op0=op0, op1=op1), r, w)

    slab_ctr = [0]

    def load_slab(src, kc, ncols):
        i = slab_ctr[0] % NSLAB
        slab_ctr[0] += 1
        view = slabs[i][:, 0:kc * ncols].rearrange("p (c n) -> p c n", c=kc)
        dma("pool", view, src.rearrange("(c p) n -> p c n", p=128), [], [f"slab{i}"])
        return view, f"slab{i}"

    psrot = [0]

    def next_ps(nb=6):
        b = psrot[0] % nb
        psrot[0] += 1
        return ps[b], f"ps{b}"

    def colblocks(t):
        return [(0, TT)] + ([(TT, NS)] if t == 0 else [])

    XT_ALL = [f"xT{c}" for c in range(16)]

    def rmsnorm_fm(t, widx, dst, dkey):
        for (c0, n) in colblocks(t):
            pss, psk = ps[7], "ps7"
            for c in range(16):
                sq = sqs[c % 2]
                act(sq[:, 0:n], xT[:, c, c0:c0 + n], AF.Square, [f"xT{c}"], [f"sq{c % 2}"])
                mm(pss[:, 0:n], onesf, sq[:, 0:n], c == 0, c == 15, [f"sq{c % 2}", "onesf"], [psk])
            ts(rstd[:, 0:n], pss[:, 0:n], 1.0 / D, EPS, ALU.mult, ALU.add, [psk], ["rstd"])
            act(rstd[:, 0:n], rstd[:, 0:n], AF.Ln, ["rstd"], ["rstd"])
            act(rstd[:, 0:n], rstd[:, 0:n], AF.Exp, ["rstd"], ["rstd"], scale=-0.5)
            for c in range(16):
                stt(dst[:, c, c0:c0 + n], xT[:, c, c0:c0 + n], nrm[:, widx, c:c + 1], rstd[:, 0:n],
                    ALU.mult, ALU.mult, [f"xT{c}", "rstd", "nrm"], [f"{dkey}{c}"])

    def ffn(t, l, which):
        hT = carve(0, [128, 44, TC], BF16)
        sg = [carve(46464 + i * 2112, [128, TC], F32) for i in range(2)]
        rmsnorm_fm(t, (0 if which == 0 else 4) + l, hn, "hn")
        wgu = w_gu[which][l]
        wdn = w_dn[which][l]
        cbs = colblocks(t)
        sgi = 0
        for s in range(22):
            gv, gk = load_slab(wgu[:, s * 256:(s + 1) * 256], 16, 256)
            uv, uk = load_slab(wgu[:, DFF + s * 256: DFF + (s + 1) * 256], 16, 256)
            for jj in range(2):
                j = s * 2 + jj
                for (c0, n) in cbs:
                    pg, pgk = next_ps()
                    pu, puk = next_ps()
                    for k in range(16):
                        mm(pg[:, 0:n], gv[:, k, jj * 128:(jj + 1) * 128], hn[:, k, c0:c0 + n], k == 0, k == 15,
                           [gk, f"hn{k}"], [pgk])
                    for k in range(16):
                        mm(pu[:, 0:n], uv[:, k, jj * 128:(jj + 1) * 128], hn[:, k, c0:c0 + n], k == 0, k == 15,
                           [uk, f"hn{k}"], [puk])
                    sgt = sg[sgi % 2]
                    sgk = f"sg{sgi % 2}"
                    sgi += 1
                    act(sgt[:, 0:n], pg[:, 0:n], AF.Silu, [pgk], [sgk])
                    tt(hT[:, j, c0:c0 + n], sgt[:, 0:n], pu[:, 0:n], ALU.mult, [sgk, puk], [f"hT{j}"])
        for fo in range(16):
            dva, dka = load_slab(wdn[0:22 * 128, fo * 128:(fo + 1) * 128], 22, 128)
            dvb, dkb = load_slab(wdn[22 * 128:44 * 128, fo * 128:(fo + 1) * 128], 22, 128)
            for (c0, n) in cbs:
                pd, pdk = next_ps()
                for j in range(44):
                    dv, dk = (dva, dka) if j < 22 else (dvb, dkb)
                    mm(pd[:, 0:n], dv[:, j % 22, :], hT[:, j, c0:c0 + n], j == 0, j == 43, [dk, f"hT{j}"], [pdk])
                stt(xT[:, fo, c0:c0 + n], pd[:, 0:n], 0.5, xT[:, fo, c0:c0 + n], ALU.mult, ALU.add,
                    [pdk, f"xT{fo}"], [f"xT{fo}"])
        m.barrier()

    Y_OFF = 25344
    ymT = carve(0, [128, 8, TC], BF16)
    yhT = carve(8448, [128, 8, TC], BF16)
    ygT = carve(16896, [128, 8, TC], BF16)

    def blocks(t):
        b = [(i * 128, 128) for i in range(4)]
        if t == 0:
            b.append((TT, NS))
        return b

    def proj_fm(t, l, col0, ncols, evac):
        done = 0
        while done < ncols:
            w = min(256, ncols - done)
            sv, sk = load_slab(w_in[l][:, col0 + done: col0 + done + w], 16, w)
            for jj in range((w + 127) // 128):
                mcols = min(128, w - jj * 128)
                for (c0, n) in colblocks(t):
                    pp, pk = next_ps(4)
                    for k in range(16):
                        mm(pp[0:mcols, 0:n], sv[:, k, jj * 128: jj * 128 + mcols], hn[:, k, c0:c0 + n], k == 0, k == 15,
                           [sk, f"hn{k}"], [pk])
                    evac(done // 128 + jj, c0, n, pp[0:mcols, 0:n], pk)
            done += w

    def proj_tm(t, l, col0, ncols, evac):
        done = 0
        while done < ncols:
            w = min(256, ncols - done)
            sv, sk = load_slab(w_in[l][:, col0 + done: col0 + done + w], 16, w)
            for bi, (b0, rows) in enumerate(blocks(t)):
                pp, pk = next_ps(4)
                for k in range(16):
                    mm(pp[0:rows, 0:w], hn[:, k, b0:b0 + rows], sv[:, k, :], k == 0, k == 15, [sk, f"hn{k}"], [pk])
                evac(bi, rows, done, w, pp[0:rows, 0:w], pk)
            done += w

    def mamba(t, l):
        o = Y_OFF
        X_tm = carve(o, [128, 5, 1024], BF16); o += 10240
        B_tm = carve(o, [128, 5, 256], BF16); o += 2560
        C_tm = carve(o, [128, 256], BF16); o += 512
        BCT = carve(o, [128, 4, TC], BF16); o += 4224
        cols = carve(o, [128, 5, 48], F32); o += 960
        expc = carve(o, [128, 5, 16], F32); o += 320
        eclbc = carve(o, [128, 4, 16], F32); o += 256
        dtT = carve(o, [128, TC], F32); o += 2112
        cumT = carve(o, [128, TC], F32); o += 2112
        ytm = carve(o, [128, 1024], F32); o += 4096
        zs = carve(o, [128, 1024], BF16); o += 2048
        yn = carve(o, [128, 1024], BF16); o += 2048
        ss2 = carve(o, [128, 4], F32); o += 16
        tmpx = carve(o, [128, 512], F32); o += 2048
        junk = tmpx
        U0 = o
        cstage = [carve(o + i * 2128, [128, 532], F32) for i in range(2)]; o += 4256
        xcs = [carve(o + i * 1056, [128, TC], BF16) for i in range(2)]; o += 2112
        cacc = carve(o, [128, 512], F32); o += 2048
        wT = carve(o, [128, TC], F32); o += 2112
        gaT_ = carve(o, [128, TC], F32); o += 2112
        dg4 = carve(o, [128, 4, 16], F32); o += 256
        ecl4 = carve(o, [128, 4], F32); o += 16
        xs_s = carve(o, [128, 12, NS], F32); o += 768
        acc_s = carve(o, [128, 12, NS], F32); o += 768
        tmp_s = carve(o, [128, 12, NS], F32); o += 768
        xc_s = carve(o, [128, 12, NS], BF16); o += 384
        cso = carve(o, [128, 12, NS, 3], F32); o += 2304
        sconv = carve(o, [128, 12, NS, 3], F32); o += 2304
        assert o <= RBYTES, o
        o = U0
        Dm = [carve(o + i * 512, [128, 128], F32) for i in range(2)]; o += 1024
        Lm = [carve(o + i * 512, [128, 128], F32) for i in range(2)]; o += 1024
        Mt = [carve(o + i * 256, [128, 128], BF16) for i in range(2)]; o += 512
        ys = carve(o, [128, 512], F32); o += 2048
        Xw = carve(o, [128, 512], BF16); o += 1024
        o = U0
        Xd = carve(o, [128, 1024], F32); o += 4096
        Xdm = carve(o, [128, NS, 128], BF16); o += 4096
        Dme = carve(o, [128, NS, 8], F32); o += 512
        Dmo = carve(o, [128, NS, 8], F32); o += 512
        decbc = carve(o, [128, NS, 8], F32); o += 512
        Cm = carve(o, [128, NS, 128], BF16); o += 4096
        Sst = [carve(o + i * 512, [128, 128], F32) for i in range(3)]; o += 1536
        Snw = [carve(o + i * 512, [128, 128], F32) for i in range(3)]; o += 1536
        ysT = carve(o, [128, 8, NS], F32); o += 512
        assert o <= RBYTES, o
        samp = (t == 0)
        nblk = 5 if samp else 4

        if samp:
            dma("sp", sconv, d_sconv[:, l], [], ["sconv"])

        def evac_xbc(c, c0, n, pp, pk):
            st = cstage[c % 2]
            sk_ = f"cstage{c % 2}"
            if c0 == 0:
                cp("act", st[:, 3:3 + TT], pp, [pk], [sk_])
                cp("dve", st[:, 0:3], cst[l][:, c, :], [f"cst{l}"], [sk_])
                ts(cacc, st[:, 0:TT], convw[:, l, c, 0:1], None, ALU.mult, None, [sk_, "convw"], ["cacc"])
                for i in range(1, 4):
                    stt(cacc, st[:, i:i + TT], convw[:, l, c, i:i + 1], cacc, ALU.mult, ALU.add,
                        [sk_, "convw", "cacc"], ["cacc"])
                cp("dve", cst[l][:, c, :], st[:, TT:TT + 3], [sk_], [f"cst{l}"])
                xc = xcs[c % 2]
                xk = f"xcs{c % 2}"
                act(xc[:, 0:TT], cacc, AF.Silu, ["cacc", "convb"], [xk], bias=convb[:, l, c:c + 1])
                if c < 10:
                    pt, ptk = ps[4 + c % 2], f"ps{4 + c % 2}"
                    ptb = pt.bitcast(BF16)
                    for b in range(4):
                        tr(ptb[:, b * 128:(b + 1) * 128], xc[:, b * 128:(b + 1) * 128], identb, [xk, "identb"], [ptk])
                    src = ptb[:, 0:512].rearrange("p (b f) -> p b f", b=4)
                    if c < 8:
                        cp("act", X_tm[:, 0:4, c * 128:(c + 1) * 128], src, [ptk], ["X_tm"])
                    else:
                        cp("act", B_tm[:, 0:4, (c - 8) * 128:(c - 7) * 128], src, [ptk], ["B_tm"])
                if c >= 8:
                    cp("dve", BCT[:, c - 8, 0:TT], xc[:, 0:TT], [xk], ["BCT"])
            else:
                cp("act", xs_s[:, c, :], pp, [pk], ["xs_s"])

        proj_fm(t, l, O_XBC, 1536, evac_xbc)

        if samp:
            def wb(i):
                return convw[:, l, :, i:i + 1].broadcast_to([128, 12, NS])
            tt(acc_s, sconv[:, :, :, 0], wb(0), ALU.mult, ["sconv", "convw"], ["acc_s"])
            for i in (1, 2):
                tt(tmp_s, sconv[:, :, :, i], wb(i), ALU.mult, ["sconv", "convw"], ["tmp_s"])
                tt(acc_s, acc_s, tmp_s, ALU.add, ["acc_s", "tmp_s"], ["acc_s"])
            tt(tmp_s, xs_s, wb(3), ALU.mult, ["xs_s", "convw"], ["tmp_s"])
            tt(acc_s, acc_s, tmp_s, ALU.add, ["acc_s", "tmp_s"], ["acc_s"])
            tt(acc_s, acc_s, convb[:, l, :].unsqueeze(2).broadcast_to([128, 12, NS]), ALU.add, ["acc_s", "convb"], ["acc_s"])
            act(xc_s, acc_s, AF.Silu, ["acc_s"], ["xc_s"])
            cp("dve", cso[:, :, :, 0:2], sconv[:, :, :, 1:3], ["sconv"], ["cso"])
            cp("dve", cso[:, :, :, 2], xs_s, ["xs_s", "cso"], ["cso"])
            dma("sp", o_convs[l], cso, ["cso"], [f"o_convs{l}"])
            pt, ptk = ps[4], "ps4"
            ptb = pt.bitcast(BF16)
            for c in range(8):
                tr(ptb[0:NS, c * 128:(c + 1) * 128], xc_s[:, c, :], identb, ["xc_s", "identb"], [ptk])
            cp("act", X_tm[0:NS, 4, :], ptb[0:NS, 0:1024], [ptk], ["X_tm"])
            pt, ptk = ps[5], "ps5"
            ptb = pt.bitcast(BF16)
            for c in range(4):
                tr(ptb[0:NS, c * 128:(c + 1) * 128], xc_s[:, 8 + c, :], identb, ["xc_s", "identb"], [ptk])
            cp("act", B_tm[0:NS, 4, :], ptb[0:NS, 0:256], [ptk], ["B_tm"])
            cp("act", C_tm[0:NS, :], ptb[0:NS, 256:512], [ptk], ["C_tm"])

        ncol = TC if samp else TT

        def evac_dt(c, c0, n, pp, pk):
            act(dtT[0:16, c0:c0 + n], pp, AF.Exp, [pk, "hd"], ["dtT"], bias=hd[:, l:l + 1])
        proj_fm(t, l, O_DT, 16, evac_dt)
        act(dtT[0:16, 0:ncol], dtT[0:16, 0:ncol], AF.Ln, ["dtT"], ["dtT"], bias=1.0)
        ts(gaT_[0:16, 0:ncol], dtT[0:16, 0:ncol], aneg[:, l:l + 1], None, ALU.mult, None, ["dtT", "aneg"], ["gaT"])
        for c in range(4):
            scan(cumT[0:16, c * 128:(c + 1) * 128], onesf[0:16, :], gaT_[0:16, c * 128:(c + 1) * 128], 0.0,
                 ALU.mult, ALU.add, ["gaT", "onesf"], ["cumT"])
            act(wT[0:16, c * 128:(c + 1) * 128], cumT[0:16, c * 128:(c + 1) * 128], AF.Exp, ["cumT"], ["wT"],
                scale=-1.0, bias=cumT[0:16, c * 128 + 127:c * 128 + 128])
        tt(wT[0:16, 0:TT], wT[0:16, 0:TT], dtT[0:16, 0:TT], ALU.mult, ["wT", "dtT"], ["wT"])
        if samp:
            cp("dve", cumT[0:16, TT:TC], gaT_[0:16, TT:TC], ["gaT"], ["cumT"])
            cp("dve", wT[0:16, TT:TC], dtT[0:16, TT:TC], ["dtT"], ["wT"])
        for bi, (b0, rows) in enumerate(blocks(t)):
            pt, ptk = ps[6], "ps6"
            for j, srcT in enumerate((dtT, cumT, wT)):
                tr(pt[0:rows, j * 16:(j + 1) * 16], srcT[0:16, b0:b0 + rows], identf[0:16, 0:16],
                   ["dtT", "cumT", "wT", "identf"], [ptk])
            cp("dve", cols[0:rows, bi, :], pt[0:rows, 0:48], [ptk], ["cols"])
        act(expc[:, 0:4, :], cols[:, 0:4, 16:32], AF.Exp, ["cols"], ["expc"])
        if samp:
            act(expc[0:NS, 4, :], cols[0:NS, 4, 16:32], AF.Exp, ["cols"], ["expc"])
        act(ecl4[0:16, :], cumT[0:16, 127:TT:128], AF.Exp, ["cumT"], ["ecl4"])
        tt(dg4[0:16], id16.unsqueeze(1).broadcast_to([16, 4, 16]), ecl4[0:16, :].unsqueeze(2).broadcast_to([16, 4, 16]),
           ALU.mult, ["id16", "ecl4"], ["dg4"])
        pt, ptk = ps[6], "ps6"
        mm(pt[:, 0:64], onesf[0:16, :], dg4[0:16].rearrange("p a b -> p (a b)"), True, True, ["onesf", "dg4"], [ptk])
        cp("dve", eclbc.rearrange("p a b -> p (a b)"), pt[:, 0:64], [ptk], ["eclbc"])

        cp("act", Smb, Sm[l], [f"Sm{l}"], ["Smb"])

        zsl = [load_slab(w_in[l][:, O_Z + q * 256: O_Z + (q + 1) * 256], 16, 256) for q in range(4)]

        def post_block(bi, b0, rows):
            for q in range(4):
                pp, pk = next_ps(4)
                for k in range(16):
                    mm(pp[0:rows, 0:256], hn[:, k, b0:b0 + rows], zsl[q][0][:, k, :], k == 0, k == 15,
                       [zsl[q][1], f"hn{k}"], [pk])
                act(zs[0:rows, q * 256:(q + 1) * 256], pp[0:rows, 0:256], AF.Silu, [pk], ["zs"])
            tt(ytm[0:rows], ytm[0:rows], zs[0:rows], ALU.mult, ["ytm", "zs"], ["ytm"])
            for g in range(2):
                act(junk[0:rows], ytm[0:rows, g * 512:(g + 1) * 512], AF.Square, ["ytm"], ["tmpx", "ss2"],
                    accum_out=ss2[0:rows, g:g + 1])
            ts(ss2[0:rows, 0:2], ss2[0:rows, 0:2], 1.0 / 512, EPS, ALU.mult, ALU.add, ["ss2"], ["ss2"])
            act(ss2[0:rows, 0:2], ss2[0:rows, 0:2], AF.Ln, ["ss2"], ["ss2"])
            act(ss2[0:rows, 0:2], ss2[0:rows, 0:2], AF.Exp, ["ss2"], ["ss2"], scale=-0.5)
            for g in range(2):
                ts(yn[0:rows, g * 512:(g + 1) * 512], ytm[0:rows, g * 512:(g + 1) * 512], ss2[0:rows, g:g + 1], None,
                   ALU.mult, None, ["ytm", "ss2"], ["yn"])
            pt, ptk = ps[6], "ps6"
            ptb = pt.bitcast(BF16)
            for cc in range(8):
                tr(ptb[:, cc * 128: cc * 128 + rows], yn[0:rows, cc * 128:(cc + 1) * 128], identb[0:rows, 0:rows],
                   ["yn", "identb"], [ptk])
            tt(ymT[:, :, b0:b0 + rows], ptb.rearrange("p (c r) -> p c r", c=8)[:, :, 0:rows],
               mnormF[:, l, :].unsqueeze(2).broadcast_to([128, 8, rows]), ALU.mult, [ptk, "mnormF"], ["ymT"])

        m.barrier()
        it = 0
        for c in range(4):
            ch = slice(c * 128, (c + 1) * 128)
            for g in range(2):
                pcb, pcbk = ps[4], "ps4"
                mm(pcb[:, g * 128:(g + 1) * 128], BCT[:, g, ch], BCT[:, 2 + g, ch], True, True, ["BCT"], [pcbk + f"_{g}"])
                pin, pink = ps[5], "ps5"
                mm(pin, BCT[:, 2 + g, ch], Smb[:, g * 512:(g + 1) * 512], True, True, ["BCT", "Smb"], [pink])
                pia, piak = ps[6], "ps6"
                for hh in range(8):
                    h = g * 8 + hh
                    pb_, pbk = ps[7], f"ps7_{it % 4}"
                    pbs = pb_[:, (it % 4) * 128:(it % 4 + 1) * 128]
                    mm(pbs, sel[:, h, :], cumT[0:16, ch], True, True, ["sel", "cumT"], [pbk])
                    d_, dk_ = Dm[it % 2], f"Dm{it % 2}"
                    l_, lk_ = Lm[it % 2], f"Lm{it % 2}"
                    m_, mk_ = Mt[it % 2], f"Mt{it % 2}"
                    stt(d_, pbs, cols[:, c, 16 + h:17 + h], negm, ALU.subtract, ALU.add, [pbk, "cols", "negm"], [dk_])
                    act(l_, d_, AF.Exp, [dk_], [lk_])
                    stt(m_, l_, cols[:, c, h:h + 1], pcb[:, g * 128:(g + 1) * 128], ALU.mult, ALU.mult,
                        [lk_, "cols", pcbk + f"_{g}"], [mk_])
                    mm(pia[:, hh * 64:(hh + 1) * 64], m_, X_tm[:, c, h * 64:(h + 1) * 64], True, True, [mk_, "X_tm"], [piak])
                    it += 1
                gs = slice(g * 512, (g + 1) * 512)
                e8 = expc[:, c, g * 8:(g + 1) * 8].unsqueeze(2).broadcast_to([128, 8, 64])
                tt(ys.rearrange("p (h q) -> p h q", h=8), pin.rearrange("p (h q) -> p h q", h=8), e8, ALU.mult,
                   [pink, "expc"], ["ys"])
                tt(ytm[:, gs], ys, pia, ALU.add, ["ys", piak], ["ytm"])
                d8 = dskip[:, l, g * 8:(g + 1) * 8].unsqueeze(2).broadcast_to([128, 8, 64])
                tt(tmpx.rearrange("p (h q) -> p h q", h=8), X_tm[:, c, gs].rearrange("p (h q) -> p h q", h=8), d8,
                   ALU.mult, ["X_tm", "dskip"], ["tmpx"])
                tt(ytm[:, gs], ytm[:, gs], tmpx, ALU.add, ["ytm", "tmpx"], ["ytm"])
                w8 = cols[:, c, 32 + g * 8:32 + (g + 1) * 8].unsqueeze(2).broadcast_to([128, 8, 64])
                tt(Xw.rearrange("p (h q) -> p h q", h=8), X_tm[:, c, gs].rearrange("p (h q) -> p h q", h=8), w8, ALU.mult,
                   ["X_tm", "cols"], ["Xw"])
                mm(pin, B_tm[:, c, g * 128:(g + 1) * 128], Xw, True, True, ["B_tm", "Xw"], [pink])
                k8 = eclbc[:, c, g * 8:(g + 1) * 8].unsqueeze(2).broadcast_to([128, 8, 64])
                tt(Sm[l][:, gs].rearrange("p (h q) -> p h q", h=8), Sm[l][:, gs].rearrange("p (h q) -> p h q", h=8), k8,
                   ALU.mult, [f"Sm{l}", "eclbc"], [f"Sm{l}"])
                tt(Sm[l][:, gs], Sm[l][:, gs], pin, ALU.add, [f"Sm{l}", pink], [f"Sm{l}"])
                cp("act", Smb[:, gs], Sm[l][:, gs], [f"Sm{l}"], ["Smb"])
            post_block(c, c * 128, 128)

        if samp:
            m.barrier()
            x4 = X_tm[0:NS, 4, :].rearrange("p (h q) -> p h q", h=16)
            tt(Xd[0:NS].rearrange("p (h q) -> p h q", h=16), x4, cols[0:NS, 4, 0:16].unsqueeze(2).broadcast_to([NS, 16, 64]),
               ALU.mult, ["X_tm", "cols"], ["Xd"])
            idb8 = id16.unsqueeze(2).broadcast_to([NS, NS, 8])
            tt(Dme[0:NS], expc[0:NS, 4, 0:16:2].unsqueeze(1).broadcast_to([NS, NS, 8]), idb8, ALU.mult, ["expc", "id16"], ["Dme"])
            tt(Dmo[0:NS], expc[0:NS, 4, 1:16:2].unsqueeze(1).broadcast_to([NS, NS, 8]), idb8, ALU.mult, ["expc", "id16"], ["Dmo"])
            pt, ptk = ps[6], "ps6"
            mm(pt[:, 0:128], E01[:, 0, :], Dme[0:NS].rearrange("p a b -> p (a b)"), True, False, ["E01", "Dme"], [ptk])
            mm(pt[:, 0:128], E01[:, 1, :], Dmo[0:NS].rearrange("p a b -> p (a b)"), False, True, ["E01", "Dmo"], [ptk])
            cp("dve", decbc.rearrange("p a b -> p (a b)"), pt[:, 0:128], [ptk], ["decbc"])
            idb128 = id16.unsqueeze(2).broadcast_to([NS, NS, 128])
            order = [(j, i) for j in range(8) for i in range(NS)]

            def loadm(idx):
                j_, i_ = order[idx]
                dma("sp", Sst[idx % 3], d_sssm[l, i_, j_], [], [f"Sst{idx % 3}"])
            PF = int(os.environ.get('MK_PF', '2'))
            for idx in range(PF):
                loadm(idx)
            si = 0
            for g in range(2):
                tt(Cm[0:NS], C_tm[0:NS, g * 128:(g + 1) * 128].unsqueeze(1).broadcast_to([NS, NS, 128]), idb128, ALU.mult,
                   ["C_tm", "id16"], ["Cm"])
                for q in range(4):
                    mm(ps[q], onesb[0:NS, :], Cm[0:NS, q * 4:(q + 1) * 4, :].rearrange("p a b -> p (a b)"), True, True,
                       ["onesb", "Cm"], [f"ps{q}"])
                for jj in range(4):
                    j = g * 4 + jj
                    tt(Xdm[0:NS], Xd[0:NS, j * 128:(j + 1) * 128].unsqueeze(1).broadcast_to([NS, NS, 128]), idb128, ALU.mult,
                       ["Xd", "id16"], ["Xdm"])
                    for i in range(NS):
                        s_in, sk_in = Sst[si % 3], f"Sst{si % 3}"
                        s_nw, sk_nw = Snw[si % 3], f"Snw{si % 3}"
                        po, pok = ps[4 + si % 2], f"ps{4 + si % 2}"
                        if PF == 0:
                            loadm(si)
                        mm(po[:, 0:128], Xdm[0:NS, i, :], B_tm[0:NS, 4, g * 128:(g + 1) * 128], True, True,
                           ["Xdm", "B_tm"], [pok])
                        stt(s_nw, s_in, decbc[:, i, j:j + 1], po[:, 0:128], ALU.mult, ALU.add, [sk_in, "decbc", pok], [sk_nw])
                        if PF and si + PF < len(order):
                            loadm(si + PF)
                        dma("sp", o_ssms[l, i, j], s_nw, [sk_nw], [f"o_ssms{l}_{i}_{j}"])
                        ttr(junk[:, 0:128], s_nw, ps[i // 4][:, (i % 4) * 128:(i % 4 + 1) * 128], ysT[:, j, i:i + 1],
                            [sk_nw, f"ps{i // 4}"], ["tmpx", "ysT"])
                        si += 1
            for half in range(2):
                pt, ptk = ps[4 + half], f"ps{4 + half}"
                for cc in range(4):
                    tr(pt[0:NS, cc * 128:(cc + 1) * 128], ysT[:, half * 4 + cc, :], identf, ["ysT", "identf"], [ptk])
                hs = slice(half * 512, (half + 1) * 512)
                d8 = dskip[0:NS, l, half * 8:(half + 1) * 8].unsqueeze(2).broadcast_to([NS, 8, 64])
                tt(tmpx[0:NS].rearrange("p (h q) -> p h q", h=8), X_tm[0:NS, 4, hs].rearrange("p (h q) -> p h q", h=8), d8,
                   ALU.mult, ["X_tm", "dskip"], ["tmpx"])
                tt(ytm[0:NS, hs], tmpx[0:NS], pt[0:NS, 0:512], ALU.add, ["tmpx", ptk], ["ytm"])
            post_block(4, TT, NS)
        m.barrier()

    def gla_alloc(o0, nh, dv, tagk):
        o = o0
        b = {}
        nvb = nh * dv // 128
        b["qT"] = carve(o, [128, nh, TC], F32); o += nh * TC * 4
        b["kT"] = carve(o, [128, nh, TC], F32); o += nh * TC * 4
        b["gT"] = carve(o, [128, nh, TC], F32); o += nh * TC * 4
        b["sgT"] = carve(o, [128, nvb, TC], BF16); o += nvb * TC * 2
        b["V"] = carve(o, [128, 5, 512], BF16); o += 5120
        b["sig"] = carve(o, [128, TC], F32); o += TC * 4
        for s_ in range(2):
            b[f"bT{s_}"] = carve(o, [128, 128], F32); o += 512
            b[f"e1{s_}"] = carve(o, [128, 128], F32); o += 512
            b[f"e2{s_}"] = carve(o, [128, 128], F32); o += 512
            b[f"qt{s_}"] = carve(o, [128, 128], BF16); o += 256
            b[f"kt{s_}"] = carve(o, [128, 128], BF16); o += 256
            b[f"At{s_}"] = carve(o, [128, 128], BF16); o += 256
            b[f"ktm{s_}"] = carve(o, [128, 128], BF16); o += 256
            b[f"Sr{s_}"] = carve(o, [128, 256], BF16); o += 512
            b[f"tU{s_}"] = carve(o, [128, 256], F32); o += 1024
            b[f"sq{s_}"] = [carve(o + i * 512, [128, 128], F32) for i in range(2)]; o += 1024
            b[f"rs{s_}"] = carve(o, [128, 128], F32); o += 512
            b[f"on{s_}"] = carve(o, [128, 128], F32); o += 512
            b[f"cc{s_}"] = carve(o, [128, 8], F32); o += 32
        b["ea"] = carve(o, [128, nh, NS], F32); o += nh * NS * 4
        b["Ktm"] = carve(o, [128, 128], BF16); o += 256
        b["Km"] = carve(o, [128, NS, 128], BF16); o += 4096
        b["Sin"] = [carve(o + i * dv * 4, [128, dv], F32) for i in range(3)]; o += 3 * dv * 4
        b["Snw"] = [carve(o + i * dv * 4, [128, dv], F32) for i in range(3)]; o += 3 * dv * 4
        assert o <= RBYTES, o
        for s_ in range(2):
            mset(b[f"At{s_}"][64:128, 0:64], 0.0, [f"{tagk}{s_}At"])
        return b

    def run_lockstep(gens):
        gens = list(gens)
        if os.environ.get("MK_NOLOCK"):
            for g_ in gens:
                for _ in g_:
                    pass
            return
        while gens:
            for g_ in list(gens):
                try:
                    next(g_)
                except StopIteration:
                    gens.remove(g_)

    def gla_post(b, s, po, pok, pss, pssk, dv, n, outs, sgs, nws, tag):
        nb = dv // 128
        T = f"{tag}{s}"
        sq, rs, on = b[f"sq{s}"], b[f"rs{s}"], b[f"on{s}"]
        for blk in range(nb):
            act(sq[blk][:, 0:n], po[:, blk * n:(blk + 1) * n], AF.Square, [pok], [f"{T}sq{blk}"])
            mm(pss[:, 0:n], onesf, sq[blk][:, 0:n], blk == 0, blk == nb - 1, [f"{T}sq{blk}", "onesf"], [pssk])
            yield
        ts(rs[:, 0:n], pss[:, 0:n], float(dv * EPS), None, ALU.add, None, [pssk], [T + "rs"])
        yield
        act(rs[:, 0:n], rs[:, 0:n], AF.Ln, [T + "rs"], [T + "rs"])
        yield
        act(rs[:, 0:n], rs[:, 0:n], AF.Exp, [T + "rs"], [T + "rs"], scale=-0.5)
        yield
        for blk in range(nb):
            tt(on[:, 0:n], po[:, blk * n:(blk + 1) * n], rs[:, 0:n], ALU.mult, [pok, T + "rs"], [T + "on"])
            yield
            stt(outs[blk], on[:, 0:n], nws[blk], sgs[blk], ALU.mult, ALU.mult, [T + "on", tag + "sg", "hnorm", "gnorm"],
                [tag + "y"])
            yield

    def gla_chunk(b, s, hh, c, S, skey, dv, outs, sgs, nws, tag):
        if os.environ.get("MK_ONESET"):
            s = 0
        ch = slice(c * 128, (c + 1) * 128)
        qc, kc, gc = b["qT"][:, hh, ch], b["kT"][:, hh, ch], b["gT"][:, hh, ch]
        Vc = b["V"][:, c, hh * dv:(hh + 1) * dv]
        bT, e1, e2, cc = b[f"bT{s}"], b[f"e1{s}"], b[f"e2{s}"], b[f"cc{s}"]
        qt, kt, At, ktm, Sr, tU = b[f"qt{s}"], b[f"kt{s}"], b[f"At{s}"], b[f"ktm{s}"], b[f"Sr{s}"], b[f"tU{s}"]
        T = f"{tag}{s}"
        G = tag
        pa, pak = ps[4 * s][:, 0:128], f"ps{4 * s}"
        pkb, pkk = ps[4 * s + 1].bitcast(BF16)[:, 0:128], f"ps{4 * s + 1}"
        pss, pssk = ps[4 * s][:, 0:128], f"ps{4 * s}"
        po, pok = ps[4 * s + 2][:, 0:256], f"ps{4 * s + 2}"
        pu, puk = ps[4 * s + 3][:, 0:256], f"ps{4 * s + 3}"
        scan(bT, onesf, gc, 0.0, ALU.mult, ALU.add, [G + "gT", "onesf"], [T + "bT"])
        yield
        ts(cc[:, 0:1], bT[:, 63:64], -1.0, None, ALU.mult, None, [T + "bT"], [T + "cc"])
        yield
        act(e1, bT, AF.Exp, [T + "bT", T + "cc"], [T + "e1"], bias=cc[:, 0:1])
        act(e2, bT, AF.Exp, [T + "bT"], [T + "e2"], scale=-1.0, bias=bT[:, 63:64])
        yield
        act(cc[:, 1:2], bT[:, 63:64], AF.Exp, [T + "bT"], [T + "cc"])
        act(cc[:, 2:3], bT[:, 127:128], AF.Exp, [T + "bT"], [T + "cc"])
        act(cc[:, 3:4], bT[:, 127:128], AF.Exp, [T + "bT", T + "cc"], [T + "cc"], bias=cc[:, 0:1])
        tt(qt, qc, e1, ALU.mult, [G + "qT", T + "e1"], [T + "qt"])
        tt(kt, kc, e2, ALU.mult, [G + "kT", T + "e2"], [T + "kt"])
        yield
        mm(pa[:, 64:128], kt, qt[:, 64:128], True, True, [T + "kt", T + "qt"], [pak])
        mm(pa[0:64, 0:64], kt[:, 0:64], qt[:, 0:64], True, True, [T + "kt", T + "qt"], [pak])
        tr(pkb, kt, identb, [T + "kt", "identb"], [pkk])
        yield
        tt(At[:, 64:128], pa[:, 64:128], trif[:, 64:128], ALU.mult, [pak, "trif"], [T + "At"])
        tt(At[0:64, 0:64], pa[0:64, 0:64], trif[0:64, 0:64], ALU.mult, [pak, "trif"], [T + "At"])
        cp("act", ktm, pkb, [pkk], [T + "ktm"])
        ts(Sr[:, 0:dv], S, cc[:, 1:2], None, ALU.mult, None, [skey, T + "cc"], [T + "Sr"])
        yield
        for blk in range(dv // 128):
            mm(po[:, blk * 128:(blk + 1) * 128], Vc[:, blk * 128:(blk + 1) * 128], At, True, False, [G + "V", T + "At"], [pok])
            mm(po[:, blk * 128:(blk + 1) * 128], Sr[:, blk * 128:(blk + 1) * 128], qt, False, True,
               [T + "Sr", T + "qt"], [pok])
        mm(pu[:, 0:dv], ktm, Vc, True, True, [T + "ktm", G + "V"], [puk])
        yield
        ts(tU[:, 0:dv], pu[:, 0:dv], cc[:, 3:4], None, ALU.mult, None, [puk, T + "cc"], [T + "tU"])
        yield
        stt(S, S, cc[:, 2:3], tU[:, 0:dv], ALU.mult, ALU.add, [skey, T + "cc", T + "tU"], [skey])
        yield
        yield from gla_post(b, s, po, pok, pss, pssk, dv, 128, outs, sgs, nws, tag)

    def gla_decode(b, l, hh, h, S_dram_in, S_dram_out, dv, outs, sgs, nws, tag):
        T = tag
        idb128 = id16.unsqueeze(2).broadcast_to([NS, NS, 128])
        pk_, pkk = ps[1], "ps1"
        tr(pk_[0:NS, 0:128], b["kT"][:, hh, TT:TC], identf, [T + "kT", "identf"], [pkk])
        cp("act", b["Ktm"][0:NS], pk_[0:NS, 0:128], [pkk], [T + "Ktm"])
        tt(b["Km"][0:NS], b["Ktm"][0:NS].unsqueeze(1).broadcast_to([NS, NS, 128]), idb128, ALU.mult, [T + "Ktm", "id16"], [T + "Km"])
        po, pok = ps[2], "ps2"
        nb = dv // 128
        PF = int(os.environ.get('MK_PF', '2'))

        def load(i):
            dma("sp", b["Sin"][i % 3][:, 0:dv], S_dram_in[l, i, h], [], [f"{T}Sin{i % 3}"])
        for i in range(min(PF, NS)):
            load(i)
        for i in range(NS):
            s_in, sk_in = b["Sin"][i % 3], f"{T}Sin{i % 3}"
            s_nw, sk_nw = b["Snw"][i % 3], f"{T}Snw{i % 3}"
            pu, puk = ps[5 + i % 2], f"ps{5 + i % 2}"
            if PF == 0:
                load(i)
            mm(pu[:, 0:dv], b["Km"][0:NS, i, :], b["V"][0:NS, 4, hh * dv:(hh + 1) * dv], True, True, [T + "Km", T + "V"], [puk])
            stt(s_nw[:, 0:dv], s_in[:, 0:dv], b["ea"][:, hh, i:i + 1], pu[:, 0:dv], ALU.mult, ALU.add, [sk_in, T + "ea", puk], [sk_nw])
            if PF and i + PF < NS:
                load(i + PF)
            dma("sp", S_dram_out[l, i, h], s_nw[:, 0:dv], [sk_nw], [f"{T}o_{l}_{i}_{h}"])
            for blk in range(nb):
                mm(po[:, blk * NS + i: blk * NS + i + 1], s_nw[:, blk * 128:(blk + 1) * 128], b["qT"][:, hh, TT + i:TT + i + 1],
                   True, True, [sk_nw, T + "qT"], [pok])
        for _ in gla_post(b, 0, po, pok, ps[4], "ps4", dv, NS, outs, sgs, nws, T):
            pass

    def hgrn(t, l):
        samp = (t == 0)
        for hf in range(2):
            b = gla_alloc(Y_OFF, 4, 128, "h")
            T = "h"

            def ev_q(c, c0, n, pp, pk):
                act(b["qT"][:, c, c0:c0 + n], pp, AF.Silu, [pk], [T + "qT"])
                ts(b["qT"][:, c, c0:c0 + n], b["qT"][:, c, c0:c0 + n], float(128.0 ** -0.5), None, ALU.mult, None, [T + "qT"], [T + "qT"])

            def ev_f(c, c0, n, pp, pk):
                chn = hf * 4 + c
                act(b["sig"][:, 0:n], pp, AF.Sigmoid, [pk], [T + "sig"])
                ts(b["gT"][:, c, c0:c0 + n], b["sig"][:, 0:n], oml[:, l, chn:chn + 1], lb[:, l, chn:chn + 1], ALU.mult, ALU.add,
                   [T + "sig", "oml", "lb"], [T + "gT"])
                act(b["gT"][:, c, c0:c0 + n], b["gT"][:, c, c0:c0 + n], AF.Ln, [T + "gT"], [T + "gT"])
                ts(b["kT"][:, c, c0:c0 + n], b["sig"][:, 0:n], noml[:, l, chn:chn + 1], oml[:, l, chn:chn + 1], ALU.mult, ALU.add,
                   [T + "sig", "oml", "noml"], [T + "kT"])

            def ev_g(c, c0, n, pp, pk):
                act(b["sgT"][:, c, c0:c0 + n], pp, AF.Silu, [pk], [T + "sg"])

            def ev_v(bi, rows, co, w, pp, pk):
                cp("act", b["V"][0:rows, bi, co:co + w], pp, [pk], [T + "V"])

            proj_fm(t, l, O_HQ + hf * 512, 512, ev_q)
            proj_fm(t, l, O_HF + hf * 512, 512, ev_f)
            proj_fm(t, l, O_HG + hf * 512, 512, ev_g)
            proj_tm(t, l, O_HI + hf * 512, 512, ev_v)
            m.barrier()
            for c in range(4):
                for hp in range(2):
                    gens = []
                    for s_ in range(2):
                        hh = hp * 2 + s_
                        h = hf * 4 + hh
                        gens.append(gla_chunk(b, s_, hh, c, Sh[l][:, h * 128:(h + 1) * 128], f"Sh{l}_{h}", 128,
                                              [yhT[:, h, c * 128:(c + 1) * 128]], [b["sgT"][:, hh, c * 128:(c + 1) * 128]],
                                              [hnorm[:, l:l + 1]], T))
                    run_lockstep(gens)
            m.barrier()
            if samp:
                act(b["ea"], b["gT"][:, :, TT:TC], AF.Exp, [T + "gT"], [T + "ea"])
                for hh in range(4):
                    h = hf * 4 + hh
                    gla_decode(b, l, hh, h, d_shgrn, o_hgrns, 128, [yhT[:, h, TT:TC]], [b["sgT"][:, hh, TT:TC]],
                               [hnorm[:, l:l + 1]], T)
            m.barrier()

    def gla(t, l):
        samp = (t == 0)
        ncol = TC if samp else TT
        for gf in range(2):
            b = gla_alloc(Y_OFF, 2, 256, "g")
            gaT = b["sig"]
            T = "g"

            def ev_q(c, c0, n, pp, pk):
                act(b["qT"][:, c, c0:c0 + n], pp, AF.Copy, [pk], [T + "qT"], scale=float(128.0 ** -0.5))

            def ev_k(c, c0, n, pp, pk):
                cp("act", b["kT"][:, c, c0:c0 + n], pp, [pk], [T + "kT"])

            def ev_g(c, c0, n, pp, pk):
                act(b["sgT"][:, c, c0:c0 + n], pp, AF.Silu, [pk], [T + "sg"])

            def ev_a(c, c0, n, pp, pk):
                cp("act", gaT[0:16, c0:c0 + n], pp, [pk], [T + "ga"])

            def ev_v(bi, rows, co, w, pp, pk):
                cp("act", b["V"][0:rows, bi, co:co + w], pp, [pk], [T + "V"])

            proj_fm(t, l, O_GQ + gf * 256, 256, ev_q)
            proj_fm(t, l, O_GK + gf * 256, 256, ev_k)
            proj_fm(t, l, O_GG + gf * 512, 512, ev_g)
            proj_fm(t, l, O_GA, 16, ev_a)
            for hh in range(2):
                chn = gf * 2 + hh
                for (c0, n) in colblocks(t):
                    pp, pk = next_ps(4)
                    mm(pp[:, 0:n], wdec[:, l, chn * 128:(chn + 1) * 128], gaT[0:16, c0:c0 + n], True, True, ["wdec", T + "ga"], [pk])
                    act(b["gT"][:, hh, c0:c0 + n], pp[:, 0:n], AF.Exp, [pk, "bdec"], [T + "gT"], scale=-1.0, bias=bdec[:, l, chn:chn + 1])
            act(b["gT"][:, :, 0:ncol], b["gT"][:, :, 0:ncol], AF.Ln, [T + "gT"], [T + "gT"], bias=1.0)
            ts(b["gT"][:, :, 0:ncol], b["gT"][:, :, 0:ncol], -1.0 / 16.0, None, ALU.mult, None, [T + "gT"], [T + "gT"])
            proj_tm(t, l, O_GV + gf * 512, 512, ev_v)
            m.barrier()
            for c in range(4):
                gens = []
                for hh in range(2):
                    h = gf * 2 + hh
                    gens.append(gla_chunk(b, hh, hh, c, Sg[l][:, h * 256:(h + 1) * 256], f"Sg{l}_{h}", 256,
                                          [ygT[:, h * 2 + k2, c * 128:(c + 1) * 128] for k2 in range(2)],
                                          [b["sgT"][:, hh * 2 + k2, c * 128:(c + 1) * 128] for k2 in range(2)],
                                          [gnorm[:, l, k2:k2 + 1] for k2 in range(2)], T))
                run_lockstep(gens)
            m.barrier()
            if samp:
                act(b["ea"], b["gT"][:, :, TT:TC], AF.Exp, [T + "gT"], [T + "ea"])
                for hh in range(2):
                    h = gf * 2 + hh
                    gla_decode(b, l, hh, h, d_sgla, o_glas, 256,
                               [ygT[:, h * 2 + k2, TT:TC] for k2 in range(2)],
                               [b["sgT"][:, hh * 2 + k2, TT:TC] for k2 in range(2)],
                               [gnorm[:, l, k2:k2 + 1] for k2 in range(2)], T)
            m.barrier()

    def merge(t, l):
        merged = carve(Y_OFF, [128, 16, TC], BF16)
        o = Y_OFF + 16 * TC * 2
        macc = [carve(o + i * TC * 4, [128, TC], F32) for i in range(2)]; o += 2 * TC * 4
        sgm = carve(o, [128, TC], F32); o += TC * 4
        mtmp = carve(o, [128, TC], F32); o += TC * 4
        assert o <= RBYTES
        ysrc = (ymT, yhT, ygT)
        cbs = colblocks(t)
        for fo2 in range(8):
            for bidx in range(3):
                wv, wk = load_slab(w_br[bidx][l][:, fo2 * 256:(fo2 + 1) * 256], 8, 256)
                gv, gk = load_slab(w_in[l][:, O_GATE + bidx * 2048 + fo2 * 256: O_GATE + bidx * 2048 + (fo2 + 1) * 256], 16, 256)
                for jj in range(2):
                    fo = fo2 * 2 + jj
                    for (c0, n) in cbs:
                        pP, pPk = next_ps(4)
                        pG, pGk = next_ps(4)
                        for k in range(8):
                            mm(pP[:, 0:n], wv[:, k, jj * 128:(jj + 1) * 128], ysrc[bidx][:, k, c0:c0 + n], k == 0, k == 7,
                               [wk, "ymT", "hy", "gy"], [pPk])
                        for k in range(16):
                            mm(pG[:, 0:n], gv[:, k, jj * 128:(jj + 1) * 128], hn[:, k, c0:c0 + n], k == 0, k == 15,
                               [gk, f"hn{k}"], [pGk])
                        act(sgm[:, 0:n], pG[:, 0:n], AF.Sigmoid, [pGk], ["sgm"])
                        mk = f"macc{jj}"
                        if bidx == 0:
                            tt(macc[jj][:, c0:c0 + n], sgm[:, 0:n], pP[:, 0:n], ALU.mult, ["sgm", pPk], [mk])
                        elif bidx == 1:
                            tt(mtmp[:, 0:n], sgm[:, 0:n], pP[:, 0:n], ALU.mult, ["sgm", pPk], ["mtmp"])
                            tt(macc[jj][:, c0:c0 + n], macc[jj][:, c0:c0 + n], mtmp[:, 0:n], ALU.add, [mk, "mtmp"], [mk])
                        else:
                            tt(mtmp[:, 0:n], sgm[:, 0:n], pP[:, 0:n], ALU.mult, ["sgm", pPk], ["mtmp"])
                            tt(merged[:, fo, c0:c0 + n], macc[jj][:, c0:c0 + n], mtmp[:, 0:n], ALU.add, [mk, "mtmp"], [f"mg{fo}"])
        for fo2 in range(8):
            wv, wk = load_slab(w_out[l][:, fo2 * 256:(fo2 + 1) * 256], 16, 256)
            for jj in range(2):
                fo = fo2 * 2 + jj
                for (c0, n) in cbs:
                    pp, pk = next_ps(4)
                    for k in range(16):
                        mm(pp[:, 0:n], wv[:, k, jj * 128:(jj + 1) * 128], merged[:, k, c0:c0 + n], k == 0, k == 15,
                           [wk, f"mg{k}"], [pk])
                    tt(xT[:, fo, c0:c0 + n], pp[:, 0:n], xT[:, fo, c0:c0 + n], ALU.add, [pk, f"xT{fo}"], [f"xT{fo}"])
        m.barrier()

    def mixer(t, l):
        rmsnorm_fm(t, 2 + l, hn, "hn")
        m.barrier()
        if "mamba" in mixparts:
            mamba(t, l)
        if "hgrn" in mixparts:
            hgrn(t, l)
        if "gla" in mixparts:
            gla(t, l)
        if "merge" in mixparts:
            merge(t, l)
        m.barrier()

    for (dst, src, key) in ((identf, d_identf, "identf"), (trif, d_trif, "trif"), (negm, d_negm, "negm"),
                            (nrm, d_nrm, "nrm")):
        dma("sp", dst, src, [], [key])
    for (dst, src, key) in ((convw, d_convw, "convw"), (convb, d_convb, "convb"), (hd, d_hd, "hd"),
                            (dskip, d_dskip, "dskip"), (mnormF, d_mnorm, "mnormF"), (lbl, d_lbl, "lbl"),
                            (hnorm, d_hnorm, "hnorm"), (gnorm, d_gnorm, "gnorm"), (wdec, d_wdec, "wdec"),
                            (bdec, d_bdec, "bdec"), (sel, d_sel, "sel"), (id16, d_id16, "id16")):
        dma("sp", dst, src, [], [key])
    act(aneg, hd[:, 2:4], AF.Exp, ["hd"], ["aneg"])
    ts(aneg, aneg, -1.0, None, ALU.mult, None, ["aneg"], ["aneg"])
    mset(lb, 0.0, ["lb"])
    tt(lb[:, 1, :], lbl[:, 1, :], lbl[:, 0, :], ALU.subtract, ["lbl", "lb"], ["lb"])
    act(lb[:, 1, :], lb[:, 1, :], AF.Sigmoid, ["lb"], ["lb"])
    ts(oml, lb, -1.0, 1.0, ALU.mult, ALU.add, ["lb"], ["oml"])
    ts(noml, lb, 1.0, -1.0, ALU.mult, ALU.add, ["lb"], ["noml"])
    ts(hnorm, hnorm, float(np.sqrt(128.0)), None, ALU.mult, None, ["hnorm"], ["hnorm"])
    ts(gnorm, gnorm, 16.0, None, ALU.mult, None, ["gnorm"], ["gnorm"])
    ts(bdec, bdec, -1.0, None, ALU.mult, None, ["bdec"], ["bdec"])
    mset(E01, 0.0, ["E01"])
    mset(E01[:, 0, 0:64], 1.0, ["E01"])
    mset(E01[:, 1, 64:128], 1.0, ["E01"])
    for l in range(2):
        mset(Sm[l], 0.0, [f"Sm{l}"])
        mset(Sh[l], 0.0, [f"Sh{l}_{h}" for h in range(8)])
        mset(Sg[l], 0.0, [f"Sg{l}_{h}" for h in range(4)])
        mset(cst[l], 0.0, [f"cst{l}"])
    mset(onesf, 1.0, ["onesf"])
    mset(onesb, 1.0, ["onesb"])
    cp("dve", identb, identf, ["identf"], ["identb"])

    for t in range(NT):
        dma("sp", xT[:, :, 0:TT], xTp[:, t * TT:(t + 1) * TT].rearrange("(c p) n -> p c n", p=128), [], XT_ALL)
        if t == 0:
            dma("sp", xT[:, :, TT:TC], xTs.rearrange("(c p) n -> p c n", p=128), [], XT_ALL)
        for l in range(nlayers):
            if "ffn1" in stages:
                ffn(t, l, 0)
            if "mix" in stages:
                mixer(t, l)
            if "ffn2" in stages:
                ffn(t, l, 1)
        m.barrier()
        rmsnorm_fm(t, 6, xT, "xT")
        dma("sp", yTp[:, t * TT:(t + 1) * TT].rearrange("(c p) n -> p c n", p=128), xT[:, :, 0:TT], XT_ALL, ["yTp"])
        if t == 0:
            dma("sp", yTs.rearrange("(c p) n -> p c n", p=128), xT[:, :, TT:TC], XT_ALL, ["yTs"])
        m.barrier()
    for l in range(2):
        dma("sp", o_convp[l], cst[l], [f"cst{l}"], [f"o_convp{l}"])
        dma("sp", o_ssmp[l], Sm[l], [f"Sm{l}"], [f"o_ssmp{l}"])
        dma("sp", o_hgrnp[l], Sh[l], [f"Sh{l}_{h}" for h in range(8)], [f"o_hgrnp{l}"])
        dma("sp", o_glap[l], Sg[l], [f"Sg{l}_{h}" for h in range(4)], [f"o_glap{l}"])
    m.wait_all("sp")
    m.emit()
    return nc


def _consts():
    i = np.arange(128)
    identf = np.eye(128, dtype=np.float32)
    trif = (i[:, None] <= i[None, :]).astype(np.float32)
    negm = np.where(i[:, None] <= i[None, :], 0.0, -30000.0).astype(np.float32)
    sel = np.zeros((16, 16, 128), np.float32)
    for h in range(16):
        sel[h, h, :] = 1.0
    id16 = np.eye(16, dtype=np.float32)
    return dict(identf=identf, trif=trif, negm=negm, sel=sel, id16=id16)


def _fm(v, nch):
    v = np.asarray(v, np.float32)
    lead = v.shape[:-1]
    v = v.reshape(lead + (nch, 128))
    v = np.moveaxis(v, -1, 0)
    return np.ascontiguousarray(v)


def _shared_inputs(inp):
    f = lambda k: np.ascontiguousarray(np.asarray(inp[k], np.float32))
    sh = dict(w_gu1=f("ffn1_w_gate_up"), w_d1=f("ffn1_w_down"), w_gu2=f("ffn2_w_gate_up"), w_d2=f("ffn2_w_down"),
              w_in=f("w_in"), w_bm=f("w_branch_mamba"), w_bh=f("w_branch_hgrn"), w_bg=f("w_branch_gla"),
              w_out=f("w_out"))
    nrm = np.stack([f("ffn1_norm")[0], f("ffn1_norm")[1], f("mix_norm")[0], f("mix_norm")[1],
                    f("ffn2_norm")[0], f("ffn2_norm")[1], f("final_norm")], 0)
    sh["nrm"] = _fm(nrm, 16)
    sh["convw"] = np.ascontiguousarray(np.transpose(_fm(f("conv_w"), 12), (0, 1, 3, 2)))
    sh["convb"] = _fm(f("conv_b"), 12)
    sh["hd"] = np.ascontiguousarray(np.concatenate([f("dt_bias").T, f("a_log").T], 1))
    sh["dskip_bc"] = np.ascontiguousarray(np.broadcast_to(f("d_skip")[None], (128, 2, 16)))
    sh["mnormF"] = _fm(f("mamba_norm"), 8)
    sh["lbl"] = _fm(f("hgrn_lb_logits"), 8)
    sh["hnorm"] = np.ascontiguousarray(f("hgrn_norm").T)
    sh["gnorm"] = _fm(f("gla_norm"), 2)
    sh["wdec"] = np.ascontiguousarray(np.transpose(f("gla_w_decay"), (1, 0, 2)))
    sh["bdec"] = _fm(f("gla_b_decay"), 4)
    sh.update(_consts())
    return sh


def _core_inputs(inp, sh, b, tok0, NT, s0):
    xp = np.asarray(inp["x_prompt"], np.float32)
    xs = np.asarray(inp["x_sample"], np.float32)
    d = dict(sh)
    d["xTp"] = np.ascontiguousarray(xp[b, tok0:tok0 + NT * TT, :].T)
    d["xTs"] = np.ascontiguousarray(xs[s0:s0 + NS, 0, :].T)
    sc = np.asarray(inp["state_conv"], np.float32)[:, s0:s0 + NS]
    sc = sc.reshape(2, NS, 3, 12, 128)
    d["sconv"] = np.ascontiguousarray(np.transpose(sc, (4, 0, 3, 1, 2)))
    d["s_ssm"] = np.ascontiguousarray(np.asarray(inp["state_ssm"], np.float32)[:, s0:s0 + NS].reshape(2, NS, 8, 128, 128))
    d["s_hgrn"] = np.ascontiguousarray(np.asarray(inp["state_hgrn"], np.float32)[:, s0:s0 + NS])
    d["s_gla"] = np.ascontiguousarray(np.asarray(inp["state_gla"], np.float32)[:, s0:s0 + NS])
    return d


def _unpack_core(r, NT):
    o = {}
    o["y_p"] = np.ascontiguousarray(r["yTp"].T)
    o["y_s"] = np.ascontiguousarray(r["yTs"].T)
    o["conv_p"] = np.transpose(r["conv_p"], (0, 3, 2, 1)).reshape(2, 3, 1536)
    o["ssm_p"] = np.transpose(r["ssm_p"].reshape(2, 128, 16, 64), (0, 2, 3, 1))
    o["hgrn_p"] = np.transpose(r["hgrn_p"].reshape(2, 128, 8, 128), (0, 2, 1, 3))
    o["gla_p"] = np.transpose(r["gla_p"].reshape(2, 128, 4, 256), (0, 2, 1, 3))
    o["conv_s"] = np.transpose(r["conv_s"], (0, 3, 4, 2, 1)).reshape(2, NS, 3, 1536)
    o["ssm_s"] = r["ssm_s"].reshape(2, NS, 16, 64, 128)
    o["hgrn_s"] = r["hgrn_s"]
    o["gla_s"] = r["gla_s"]
    return o


_NC_CACHE = {}


def kernel(**inputs):
    NT = 4
    key = ("full", NT)
    if key not in _NC_CACHE:
        _NC_CACHE[key] = build(NT)
    nc = _NC_CACHE[key]
    sh = _shared_inputs(inputs)
    in_maps = [_core_inputs(inputs, sh, c % 4, 0, NT, c * NS) for c in range(8)]
    res = run_bass_kernel_spmd(nc, in_maps, core_ids=list(range(8)))
    outs = [_unpack_core(r, NT) for r in res.results]
    y_prompt = np.stack([outs[b]["y_p"] for b in range(4)], 0)
    y_sample = np.concatenate([o["y_s"] for o in outs], 0)[:, None, :]
    st = lambda k: np.ascontiguousarray(np.stack([outs[b][k] for b in range(4)], 1))
    cat = lambda k: np.ascontiguousarray(np.concatenate([o[k] for o in outs], 1))
    return (y_prompt.astype(np.float32), y_sample.astype(np.float32),
            st("conv_p"), st("ssm_p"), st("hgrn_p"), st("gla_p"),
            cat("conv_s"), cat("ssm_s"), cat("hgrn_s"), cat("gla_s"))
```

```python
import os
import numpy as np
import ml_dtypes
import concourse.bass as bass
import concourse.mybir as mybir
from concourse.bass_utils import run_bass_kernel_spmd

F32 = mybir.dt.float32
BF16 = mybir.dt.bfloat16
AF = mybir.ActivationFunctionType
ALU = mybir.AluOpType

ENGS = ("pe", "act", "dve", "pool", "sp")

D = 2048
DFF = 5632
TT = 512
NS = 16
TC = TT + NS
EPS = 1e-6
O_Z, O_XBC, O_DT, O_HQ, O_HF, O_HI, O_HG, O_GQ, O_GK, O_GV, O_GG, O_GA, O_GATE = (
    0, 1024, 2560, 2576, 3600, 4624, 5648, 6672, 7184, 7696, 8720, 9744, 9760)


class MK:
    N_DMA_SEMS = 6

    def __init__(self, nc):
        self.nc = nc
        self.lists = {e: [] for e in ENGS}
        self.cnt = {e: 0 for e in ENGS}
        self.sem = {e: nc.alloc_semaphore(name=f"c_{e}") for e in ENGS}
        self.dsem = {e: [nc.alloc_semaphore(name=f"d_{e}{i}") for i in range(self.N_DMA_SEMS)]
                     for e in ("sp", "pool", "act")}
        self.dcnt = {e: 0 for e in self.dsem}
        self.dval = {e: [0] * self.N_DMA_SEMS for e in self.dsem}
        self.known = {e: {} for e in ENGS}
        self.last_w = {}
        self.readers = {}
        self.n_instr = 0

    def all_sems(self):
        return list(self.sem.values()) + [s for v in self.dsem.values() for s in v]

    def _need(self, eng, tok, waits):
        if tok is None:
            return
        sem, val, sid = tok
        if self.known[eng].get(sid, 0) >= val:
            return
        prev = waits.get(sid)
        if prev is None or prev[1] < val:
            waits[sid] = (sem, val)

    def _deps(self, eng, reads, writes, skip_self=False):
        waits = {}
        own = "c_" + eng
        for k in reads:
            self._need(eng, self.last_w.get(k), waits)
        for k in writes:
            self._need(eng, self.last_w.get(k), waits)
            for t in self.readers.get(k, ()):
                self._need(eng, t, waits)
        if skip_self:
            waits.pop(own, None)
        for sid, (sem, val) in waits.items():
            self.known[eng][sid] = val
        return list(waits.values())

    def _commit(self, tok, reads, writes):
        for k in writes:
            self.last_w[k] = tok
            self.readers[k] = []
        for k in reads:
            self.readers.setdefault(k, []).append(tok)

    def op(self, eng, fn, reads=(), writes=(), skip_self=False):
        waits = self._deps(eng, reads, writes, skip_self)
        self.cnt[eng] += 1
        tok = (self.sem[eng], self.cnt[eng], "c_" + eng)
        self.lists[eng].append((waits, fn, self.sem[eng], 1))
        self._commit(tok, reads, writes)
        self.n_instr += 1
        return tok

    def dma(self, q, fn, reads=(), writes=()):
        waits = self._deps(q, reads, writes)
        i = self.dcnt[q] % self.N_DMA_SEMS
        self.dcnt[q] += 1
        sem = self.dsem[q][i]
        sid = f"d_{q}{i}"
        prev = self.dval[q][i]
        if prev and self.known[q].get(sid, 0) < prev:
            waits.append((sem, prev))
            self.known[q][sid] = prev
        self.dval[q][i] = prev + 16
        tok = (sem, prev + 16, sid)
        self.lists[q].append((waits, fn, sem, 16))
        self._commit(tok, reads, writes)
        self.n_instr += 1
        return tok

    def _all_tokens(self, skip_pool):
        toks = []
        for e in ENGS:
            if self.cnt[e] and not (skip_pool and e == "pool"):
                toks.append((self.sem[e], self.cnt[e], "c_" + e))
        for q in self.dsem:
            if skip_pool and q == "pool":
                continue
            for i in range(self.N_DMA_SEMS):
                if self.dval[q][i]:
                    toks.append((self.dsem[q][i], self.dval[q][i], f"d_{q}{i}"))
        return toks

    def barrier(self, engines=("pe", "act", "dve", "sp")):
        toks = self._all_tokens(skip_pool=True)
        for eng in engines:
            waits = {}
            for t in toks:
                self._need(eng, t, waits)
            for sid, (sem, val) in waits.items():
                self.known[eng][sid] = val
            if waits:
                self.lists[eng].append((list(waits.values()), None, None, 0))

    def wait_all(self, eng):
        waits = {}
        for t in self._all_tokens(skip_pool=False):
            self._need(eng, t, waits)
        self.lists[eng].append((list(waits.values()), None, None, 0))

    def emit(self):
        nc = self.nc
        lists = self.lists
        sems = self.all_sems()
        with nc.Block() as b0:
            @b0.sync
            def _(e):
                for s in sems:
                    e.sem_clear(s)
        with nc.Block() as block:
            def run(e, items):
                for waits, fn, sem, inc in items:
                    for (s, v) in waits:
                        e.wait_ge(s, v)
                    if fn is not None:
                        fn(e).then_inc(sem, inc)

            @block.tensor
            def _(e):
                run(e, lists["pe"])

            @block.scalar
            def _(e):
                run(e, lists["act"])

            @block.vector
            def _(e):
                run(e, lists["dve"])

            @block.gpsimd
            def _(e):
                run(e, lists["pool"])

            @block.sync
            def _(e):
                run(e, lists["sp"])


def build(NT, stages=("ffn1", "mix", "ffn2"), nlayers=2, dbg=None, mixparts=("mamba", "hgrn", "gla", "merge")):
    nc = bass.Bass("TRN2", target_bir_lowering=False)
    m = MK(nc)
    NTOK = NT * TT

    def din(name, shape, dt=F32):
        return nc.dram_tensor(name, list(shape), dt, kind="ExternalInput").ap()

    def dout(name, shape, dt=F32):
        return nc.dram_tensor(name, list(shape), dt, kind="ExternalOutput").ap()

    def sb(name, shape, dt=F32):
        return nc.alloc_sbuf_tensor("s_" + name, list(shape), dt).ap()

    xTp = din("xTp", [D, NTOK]); xTs = din("xTs", [D, NS])
    w_gu = [din("w_gu1", [2, D, 2 * DFF]), din("w_gu2", [2, D, 2 * DFF])]
    w_dn = [din("w_d1", [2, DFF, D]), din("w_d2", [2, DFF, D])]
    w_in = din("w_in", [2, D, 15904])
    w_br = [din("w_bm", [2, 1024, D]), din("w_bh", [2, 1024, D]), din("w_bg", [2, 1024, D])]
    w_out = din("w_out", [2, D, D])
    d_nrm = din("nrm", [128, 7, 16])
    d_convw = din("convw", [128, 2, 12, 4]); d_convb = din("convb", [128, 2, 12])
    d_hd = din("hd", [16, 4])
    d_dskip = din("dskip_bc", [128, 2, 16]); d_mnorm = din("mnormF", [128, 2, 8])
    d_lbl = din("lbl", [128, 2, 8]); d_hnorm = din("hnorm", [128, 2]); d_gnorm = din("gnorm", [128, 2, 2])
    d_wdec = din("wdec", [16, 2, 512]); d_bdec = din("bdec", [128, 2, 4])
    d_sconv = din("sconv", [128, 2, 12, NS, 3])
    d_sssm = din("s_ssm", [2, NS, 8, 128, 128])
    d_shgrn = din("s_hgrn", [2, NS, 8, 128, 128]); d_sgla = din("s_gla", [2, NS, 4, 128, 256])
    d_identf = din("identf", [128, 128]); d_trif = din("trif", [128, 128]); d_negm = din("negm", [128, 128])
    d_sel = din("sel", [16, 16, 128])
    d_id16 = din("id16", [16, 16])

    yTp = dout("yTp", [D, NTOK]); yTs = dout("yTs", [D, NS])
    o_convp = dout("conv_p", [2, 128, 12, 3]); o_ssmp = dout("ssm_p", [2, 128, 1024])
    o_hgrnp = dout("hgrn_p", [2, 128, 1024]); o_glap = dout("gla_p", [2, 128, 1024])
    o_convs = dout("conv_s", [2, 128, 12, NS, 3]); o_ssms = dout("ssm_s", [2, NS, 8, 128, 128])
    o_hgrns = dout("hgrn_s", [2, NS, 8, 128, 128]); o_glas = dout("gla_s", [2, NS, 4, 128, 256])
    dbg_out = {}
    if dbg:
        for name, shape in dbg.items():
            dbg_out[name] = dout("dbg_" + name, shape)

    xT = sb("xT", [128, 16, TC]); hn = sb("hn", [128, 16, TC], BF16)
    NSLAB = 4
    slabs = [sb(f"slab{i}", [128, 4096], BF16) for i in range(NSLAB)]
    identf = sb("identf", [128, 128]); identb = sb("identb", [128, 128], BF16)
    trif = sb("trif", [128, 128]); negm = sb("negm", [128, 128])
    onesf = sb("onesf", [128, 128]); onesb = sb("onesb", [128, 128], BF16)
    nrm = sb("nrm", [128, 7, 16])
    RBYTES = 84480
    convw = sb("convw", [128, 2, 12, 4]); convb = sb("convb", [128, 2, 12])
    hd = sb("hd", [16, 4]); aneg = sb("aneg", [16, 2])
    dskip = sb("dskip", [128, 2, 16]); mnormF = sb("mnormF", [128, 2, 8])
    lbl = sb("lbl", [128, 2, 8]); lb = sb("lb", [128, 2, 8]); oml = sb("oml", [128, 2, 8]); noml = sb("noml", [128, 2, 8])
    hnorm = sb("hnorm", [128, 2]); gnorm = sb("gnorm", [128, 2, 2])
    wdec = sb("wdec", [16, 2, 512]); bdec = sb("bdec", [128, 2, 4])
    sel = sb("sel", [16, 16, 128]); id16 = sb("id16", [16, 16]); E01 = sb("E01", [16, 2, 128])
    Sm = [sb(f"Sm{l}", [128, 1024]) for l in range(2)]
    Sh = [sb(f"Sh{l}", [128, 1024]) for l in range(2)]
    Sg = [sb(f"Sg{l}", [128, 1024]) for l in range(2)]
    cst = [sb(f"cst{l}", [128, 12, 3]) for l in range(2)]
    Smb = sb("Smb", [128, 1024], BF16)
    R = sb("R", [128, RBYTES // 4])
    ps = [nc.alloc_psum_tensor(f"p_ps{i}", [128, 512], F32).ap() for i in range(8)]

    def carve(off, shape, dt):
        esz = 2 if dt == BF16 else 4
        n = int(np.prod(shape[1:]))
        assert off % 4 == 0 and off + n * esz <= RBYTES, (off, shape)
        v = R[:, off // 4: off // 4 + (n * esz + 3) // 4]
        if dt != F32:
            v = v.bitcast(dt)[:, 0:n]
        if len(shape) == 3:
            v = v.rearrange("p (a b) -> p a b", a=shape[1])
        elif len(shape) == 4:
            v = v.rearrange("p (a b c) -> p a b c", a=shape[1], b=shape[2])
        return v[0:shape[0]]

    sqs = [carve(RBYTES - 6144 + i * 2048, [128, 512], F32) for i in range(2)]
    rstd = carve(RBYTES - 2048, [128, 512], F32)

    def act(out, in_, func, r, w, **kw):
        return m.op("act", lambda e: e.activation(out=out, in_=in_, func=func, **kw), r, w)

    def ts(out, in0, s1, s2, op0, op1, r, w):
        if op1 is None:
            return m.op("dve", lambda e: e.tensor_scalar(out=out, in0=in0, scalar1=s1, scalar2=None, op0=op0), r, w)
        return m.op("dve", lambda e: e.tensor_scalar(out=out, in0=in0, scalar1=s1, scalar2=s2, op0=op0, op1=op1), r, w)

    def stt(out, in0, scalar, in1, op0, op1, r, w):
        return m.op("dve", lambda e: e.scalar_tensor_tensor(out=out, in0=in0, scalar=scalar, in1=in1,
                                                            op0=op0, op1=op1), r, w)

    def tt(out, in0, in1, op, r, w):
        return m.op("dve", lambda e: e.tensor_tensor(out=out, in0=in0, in1=in1, op=op), r, w)

    def cp(eng, out, in_, r, w):
        if eng == "act":
            return m.op("act", lambda e: e.copy(out=out, in_=in_), r, w)
        return m.op(eng, lambda e: e.tensor_copy(out=out, in_=in_), r, w)

    def ttr(out, in0, in1, accum, r, w):
        m.op("dve", lambda e: e.tensor_tensor(out=out, in0=in0, in1=in1, op=ALU.mult), r, [w[0]])
        return m.op("dve", lambda e: e.reduce_sum(out=accum, in_=out, axis=mybir.AxisListType.X), [w[0]], w[1:])

    def mset(out, val, w):
        return m.op("dve", lambda e: e.memset(out, val), [], w)

    def mm(out, lhsT, rhs, start, stop, r, w):
        return m.op("pe", lambda e: e.matmul(out, lhsT=lhsT, rhs=rhs, start=start, stop=stop), r, w, skip_self=True)

    def tr(out, in_, ident, r, w):
        return m.op("pe", lambda e: e.transpose(out, in_, ident), r, w, skip_self=True)

    def dma(q, out, in_, r, w):
        return m.dma(q, lambda e: e.dma_start(out=out, in_=in_), r, w)

    def scan(out, d0, d1, init, op0, op1, r, w):
        return m.op("dve", lambda e: e.tensor_tensor_scan(out=out, data0=d0, data1=d1, initial=init,
                                                          op0=op0, op1=op1), r, w)

    slab_ctr = [0]

    def load_slab(src, kc, ncols):
        i = slab_ctr[0] % NSLAB
        slab_ctr[0] += 1
        view = slabs[i][:, 0:kc * ncols].rearrange("p (c n) -> p c n", c=kc)
        dma("pool", view, src.rearrange("(c p) n -> p c n", p=128), [], [f"slab{i}"])
        return view, f"slab{i}"

    psrot = [0]

    def next_ps(nb=6):
        b = psrot[0] % nb
        psrot[0] += 1
        return ps[b], f"ps{b}"

    def colblocks(t):
        return [(0, TT)] + ([(TT, NS)] if t == 0 else [])

    XT_ALL = [f"xT{c}" for c in range(16)]

    def rmsnorm_fm(t, widx, dst, dkey):
        for (c0, n) in colblocks(t):
            pss, psk = ps[7], "ps7"
            for c in range(16):
                sq = sqs[c % 2]
                act(sq[:, 0:n], xT[:, c, c0:c0 + n], AF.Square, [f"xT{c}"], [f"sq{c % 2}"])
                mm(pss[:, 0:n], onesf, sq[:, 0:n], c == 0, c == 15, [f"sq{c % 2}", "onesf"], [psk])
            ts(rstd[:, 0:n], pss[:, 0:n], 1.0 / D, EPS, ALU.mult, ALU.add, [psk], ["rstd"])
            act(rstd[:, 0:n], rstd[:, 0:n], AF.Ln, ["rstd"], ["rstd"])
            act(rstd[:, 0:n], rstd[:, 0:n], AF.Exp, ["rstd"], ["rstd"], scale=-0.5)
            for c in range(16):
                stt(dst[:, c, c0:c0 + n], xT[:, c, c0:c0 + n], nrm[:, widx, c:c + 1], rstd[:, 0:n],
                    ALU.mult, ALU.mult, [f"xT{c}", "rstd", "nrm"], [f"{dkey}{c}"])

    def ffn(t, l, which):
        hT = carve(0, [128, 44, TC], BF16)
        sg = [carve(46464 + i * 2112, [128, TC], F32) for i in range(2)]
        rmsnorm_fm(t, (0 if which == 0 else 4) + l, hn, "hn")
        wgu = w_gu[which][l]
        wdn = w_dn[which][l]
        cbs = colblocks(t)
        sgi = 0
        for s in range(22):
            gv, gk = load_slab(wgu[:, s * 256:(s + 1) * 256], 16, 256)
            uv, uk = load_slab(wgu[:, DFF + s * 256: DFF + (s + 1) * 256], 16, 256)
            for jj in range(2):
                j = s * 2 + jj
                for (c0, n) in cbs:
                    pg, pgk = next_ps()
                    pu, puk = next_ps()
                    for k in range(16):
                        mm(pg[:, 0:n], gv[:, k, jj * 128:(jj + 1) * 128], hn[:, k, c0:c0 + n], k == 0, k == 15,
                           [gk, f"hn{k}"], [pgk])
                    for k in range(16):
                        mm(pu[:, 0:n], uv[:, k, jj * 128:(jj + 1) * 128], hn[:, k, c0:c0 + n], k == 0, k == 15,
                           [uk, f"hn{k}"], [puk])
                    sgt = sg[sgi % 2]
                    sgk = f"sg{sgi % 2}"
                    sgi += 1
                    act(sgt[:, 0:n], pg[:, 0:n], AF.Silu, [pgk], [sgk])
                    tt(hT[:, j, c0:c0 + n], sgt[:, 0:n], pu[:, 0:n], ALU.mult, [sgk, puk], [f"hT{j}"])
        for fo in range(16):
            dva, dka = load_slab(wdn[0:22 * 128, fo * 128:(fo + 1) * 128], 22, 128)
            dvb, dkb = load_slab(wdn[22 * 128:44 * 128, fo * 128:(fo + 1) * 128], 22, 128)
            for (c0, n) in cbs:
                pd, pdk = next_ps()
                for j in range(44):
                    dv, dk = (dva, dka) if j < 22 else (dvb, dkb)
                    mm(pd[:, 0:n], dv[:, j % 22, :], hT[:, j, c0:c0 + n], j == 0, j == 43, [dk, f"hT{j}"], [pdk])
                stt(xT[:, fo, c0:c0 + n], pd[:, 0:n], 0.5, xT[:, fo, c0:c0 + n], ALU.mult, ALU.add,
                    [pdk, f"xT{fo}"], [f"xT{fo}"])
        m.barrier()

    Y_OFF = 25344
    ymT = carve(0, [128, 8, TC], BF16)
    yhT = carve(8448, [128, 8, TC], BF16)
    ygT = carve(16896, [128, 8, TC], BF16)

    def blocks(t):
        b = [(i * 128, 128) for i in range(4)]
        if t == 0:
            b.append((TT, NS))
        return b

    def proj_fm(t, l, col0, ncols, evac):
        done = 0
        while done < ncols:
            w = min(256, ncols - done)
            sv, sk = load_slab(w_in[l][:, col0 + done: col0 + done + w], 16, w)
            for jj in range((w + 127) // 128):
                mcols = min(128, w - jj * 128)
                for (c0, n) in colblocks(t):
                    pp, pk = next_ps(4)
                    for k in range(16):
                        mm(pp[0:mcols, 0:n], sv[:, k, jj * 128: jj * 128 + mcols], hn[:, k, c0:c0 + n], k == 0, k == 15,
                           [sk, f"hn{k}"], [pk])
                    evac(done // 128 + jj, c0, n, pp[0:mcols, 0:n], pk)
            done += w

    def proj_tm(t, l, col0, ncols, evac):
        done = 0
        while done < ncols:
            w = min(256, ncols - done)
            sv, sk = load_slab(w_in[l][:, col0 + done: col0 + done + w], 16, w)
            for bi, (b0, rows) in enumerate(blocks(t)):
                pp, pk = next_ps(4)
                for k in range(16):
                    mm(pp[0:rows, 0:w], hn[:, k, b0:b0 + rows], sv[:, k, :], k == 0, k == 15, [sk, f"hn{k}"], [pk])
                evac(bi, rows, done, w, pp[0:rows, 0:w], pk)
            done += w

    def mamba(t, l):
        o = Y_OFF
        X_tm = carve(o, [128, 5, 1024], BF16); o += 10240
        B_tm = carve(o, [128, 5, 256], BF16); o += 2560
        C_tm = carve(o, [128, 256], BF16); o += 512
        BCT = carve(o, [128, 4, TC], BF16); o += 4224
        cols = carve(o, [128, 5, 48], F32); o += 960
        expc = carve(o, [128, 5, 16], F32); o += 320
        eclbc = carve(o, [128, 4, 16], F32); o += 256
        dtT = carve(o, [128, TC], F32); o += 2112
        cumT = carve(o, [128, TC], F32); o += 2112
        ytm = carve(o, [128, 1024], F32); o += 4096
        zs = carve(o, [128, 1024], BF16); o += 2048
        yn = carve(o, [128, 1024], BF16); o += 2048
        ss2 = carve(o, [128, 4], F32); o += 16
        tmpx = carve(o, [128, 512], F32); o += 2048
        junk = tmpx
        U0 = o
        cstage = [carve(o + i * 2128, [128, 532], F32) for i in range(2)]; o += 4256
        xcs = [carve(o + i * 1056, [128, TC], BF16) for i in range(2)]; o += 2112
        cacc = carve(o, [128, 512], F32); o += 2048
        wT = carve(o, [128, TC], F32); o += 2112
        gaT_ = carve(o, [128, TC], F32); o += 2112
        dg4 = carve(o, [128, 4, 16], F32); o += 256
        ecl4 = carve(o, [128, 4], F32); o += 16
        xs_s = carve(o, [128, 12, NS], F32); o += 768
        acc_s = carve(o, [128, 12, NS], F32); o += 768
        tmp_s = carve(o, [128, 12, NS], F32); o += 768
        xc_s = carve(o, [128, 12, NS], BF16); o += 384
        cso = carve(o, [128, 12, NS, 3], F32); o += 2304
        sconv = carve(o, [128, 12, NS, 3], F32); o += 2304
        assert o <= RBYTES, o
        o = U0
        Dm = [carve(o + i * 512, [128, 128], F32) for i in range(2)]; o += 1024
        Lm = [carve(o + i * 512, [128, 128], F32) for i in range(2)]; o += 1024
        Mt = [carve(o + i * 256, [128, 128], BF16) for i in range(2)]; o += 512
        ys = carve(o, [128, 512], F32); o += 2048
        Xw = carve(o, [128, 512], BF16); o += 1024
        o = U0
        Xd = carve(o, [128, 1024], F32); o += 4096
        Xdm = carve(o, [128, NS, 128], BF16); o += 4096
        Dme = carve(o, [128, NS, 8], F32); o += 512
        Dmo = carve(o, [128, NS, 8], F32); o += 512
        decbc = carve(o, [128, NS, 8], F32); o += 512
        Cm = carve(o, [128, NS, 128], BF16); o += 4096
        Sst = [carve(o + i * 512, [128, 128], F32) for i in range(3)]; o += 1536
        Snw = [carve(o + i * 512, [128, 128], F32) for i in range(3)]; o += 1536
        ysT = carve(o, [128, 8, NS], F32); o += 512
        assert o <= RBYTES, o
        samp = (t == 0)
        nblk = 5 if samp else 4

        if samp:
            dma("sp", sconv, d_sconv[:, l], [], ["sconv"])

        def evac_xbc(c, c0, n, pp, pk):
            st = cstage[c % 2]
            sk_ = f"cstage{c % 2}"
            if c0 == 0:
                cp("act", st[:, 3:3 + TT], pp, [pk], [sk_])
                cp("dve", st[:, 0:3], cst[l][:, c, :], [f"cst{l}"], [sk_])
                ts(cacc, st[:, 0:TT], convw[:, l, c, 0:1], None, ALU.mult, None, [sk_, "convw"], ["cacc"])
                for i in range(1, 4):
                    stt(cacc, st[:, i:i + TT], convw[:, l, c, i:i + 1], cacc, ALU.mult, ALU.add,
                        [sk_, "convw", "cacc"], ["cacc"])
                cp("dve", cst[l][:, c, :], st[:, TT:TT + 3], [sk_], [f"cst{l}"])
                xc = xcs[c % 2]
                xk = f"xcs{c % 2}"
                act(xc[:, 0:TT], cacc, AF.Silu, ["cacc", "convb"], [xk], bias=convb[:, l, c:c + 1])
                if c < 10:
                    pt, ptk = ps[4 + c % 2], f"ps{4 + c % 2}"
                    ptb = pt.bitcast(BF16)
                    for b in range(4):
                        tr(ptb[:, b * 128:(b + 1) * 128], xc[:, b * 128:(b + 1) * 128], identb, [xk, "identb"], [ptk])
                    src = ptb[:, 0:512].rearrange("p (b f) -> p b f", b=4)
                    if c < 8:
                        cp("act", X_tm[:, 0:4, c * 128:(c + 1) * 128], src, [ptk], ["X_tm"])
                    else:
                        cp("act", B_tm[:, 0:4, (c - 8) * 128:(c - 7) * 128], src, [ptk], ["B_tm"])
                if c >= 8:
                    cp("dve", BCT[:, c - 8, 0:TT], xc[:, 0:TT], [xk], ["BCT"])
            else:
                cp("act", xs_s[:, c, :], pp, [pk], ["xs_s"])

        proj_fm(t, l, O_XBC, 1536, evac_xbc)

        if samp:
            def wb(i):
                return convw[:, l, :, i:i + 1].broadcast_to([128, 12, NS])
            tt(acc_s, sconv[:, :, :, 0], wb(0), ALU.mult, ["sconv", "convw"], ["acc_s"])
            for i in (1, 2):
                tt(tmp_s, sconv[:, :, :, i], wb(i), ALU.mult, ["sconv", "convw"], ["tmp_s"])
                tt(acc_s, acc_s, tmp_s, ALU.add, ["acc_s", "tmp_s"], ["acc_s"])
            tt(tmp_s, xs_s, wb(3), ALU.mult, ["xs_s", "convw"], ["tmp_s"])
            tt(acc_s, acc_s, tmp_s, ALU.add, ["acc_s", "tmp_s"], ["acc_s"])
            tt(acc_s, acc_s, convb[:, l, :].unsqueeze(2).broadcast_to([128, 12, NS]), ALU.add, ["acc_s", "convb"], ["acc_s"])
            act(xc_s, acc_s, AF.Silu, ["acc_s"], ["xc_s"])
            cp("dve", cso[:, :, :, 0:2], sconv[:, :, :, 1:3], ["sconv"], ["cso"])
            cp("dve", cso[:, :, :, 2], xs_s, ["xs_s", "cso"], ["cso"])
            dma("sp", o_convs[l], cso, ["cso"], [f"o_convs{l}"])
            pt, ptk = ps[4], "ps4"
            ptb = pt.bitcast(BF16)
            for c in range(8):
                tr(ptb[0:NS, c * 128:(c + 1) * 128], xc_s[:, c, :], identb, ["xc_s", "identb"], [ptk])
            cp("act", X_tm[0:NS, 4, :], ptb[0:NS, 0:1024], [ptk], ["X_tm"])
            pt, ptk = ps[5], "ps5"
            ptb = pt.bitcast(BF16)
            for c in range(4):
                tr(ptb[0:NS, c * 128:(c + 1) * 128], xc_s[:, 8 + c, :], identb, ["xc_s", "identb"], [ptk])
            cp("act", B_tm[0:NS, 4, :], ptb[0:NS, 0:256], [ptk], ["B_tm"])
            cp("act", C_tm[0:NS, :], ptb[0:NS, 256:512], [ptk], ["C_tm"])

        ncol = TC if samp else TT

        def evac_dt(c, c0, n, pp, pk):
            act(dtT[0:16, c0:c0 + n], pp, AF.Exp, [pk, "hd"], ["dtT"], bias=hd[:, l:l + 1])
        proj_fm(t, l, O_DT, 16, evac_dt)
        act(dtT[0:16, 0:ncol], dtT[0:16, 0:ncol], AF.Ln, ["dtT"], ["dtT"], bias=1.0)
        ts(gaT_[0:16, 0:ncol], dtT[0:16, 0:ncol], aneg[:, l:l + 1], None, ALU.mult, None, ["dtT", "aneg"], ["gaT"])
        for c in range(4):
            scan(cumT[0:16, c * 128:(c + 1) * 128], onesf[0:16, :], gaT_[0:16, c * 128:(c + 1) * 128], 0.0,
                 ALU.mult, ALU.add, ["gaT", "onesf"], ["cumT"])
            act(wT[0:16, c * 128:(c + 1) * 128], cumT[0:16, c * 128:(c + 1) * 128], AF.Exp, ["cumT"], ["wT"],
                scale=-1.0, bias=cumT[0:16, c * 128 + 127:c * 128 + 128])
        tt(wT[0:16, 0:TT], wT[0:16, 0:TT], dtT[0:16, 0:TT], ALU.mult, ["wT", "dtT"], ["wT"])
        if samp:
            cp("dve", cumT[0:16, TT:TC], gaT_[0:16, TT:TC], ["gaT"], ["cumT"])
            cp("dve", wT[0:16, TT:TC], dtT[0:16, TT:TC], ["dtT"], ["wT"])
        for bi, (b0, rows) in enumerate(blocks(t)):
            pt, ptk = ps[6], "ps6"
            for j, srcT in enumerate((dtT, cumT, wT)):
                tr(pt[0:rows, j * 16:(j + 1) * 16], srcT[0:16, b0:b0 + rows], identf[0:16, 0:16],
                   ["dtT", "cumT", "wT", "identf"], [ptk])
            cp("dve", cols[0:rows, bi, :], pt[0:rows, 0:48], [ptk], ["cols"])
        act(expc[:, 0:4, :], cols[:, 0:4, 16:32], AF.Exp, ["cols"], ["expc"])
        if samp:
            act(expc[0:NS, 4, :], cols[0:NS, 4, 16:32], AF.Exp, ["cols"], ["expc"])
        act(ecl4[0:16, :], cumT[0:16, 127:TT:128], AF.Exp, ["cumT"], ["ecl4"])
        tt(dg4[0:16], id16.unsqueeze(1).broadcast_to([16, 4, 16]), ecl4[0:16, :].unsqueeze(2).broadcast_to([16, 4, 16]),
           ALU.mult, ["id16", "ecl4"], ["dg4"])
        pt, ptk = ps[6], "ps6"
        mm(pt[:, 0:64], onesf[0:16, :], dg4[0:16].rearrange("p a b -> p (a b)"), True, True, ["onesf", "dg4"], [ptk])
        cp("dve", eclbc.rearrange("p a b -> p (a b)"), pt[:, 0:64], [ptk], ["eclbc"])

        cp("act", Smb, Sm[l], [f"Sm{l}"], ["Smb"])

        zsl = [load_slab(w_in[l][:, O_Z + q * 256: O_Z + (q + 1) * 256], 16, 256) for q in range(4)]

        def post_block(bi, b0, rows):
            for q in range(4):
                pp, pk = next_ps(4)
                for k in range(16):
                    mm(pp[0:rows, 0:256], hn[:, k, b0:b0 + rows], zsl[q][0][:, k, :], k == 0, k == 15,
                       [zsl[q][1], f"hn{k}"], [pk])
                act(zs[0:rows, q * 256:(q + 1) * 256], pp[0:rows, 0:256], AF.Silu, [pk], ["zs"])
            tt(ytm[0:rows], ytm[0:rows], zs[0:rows], ALU.mult, ["ytm", "zs"], ["ytm"])
            for g in range(2):
                act(junk[0:rows], ytm[0:rows, g * 512:(g + 1) * 512], AF.Square, ["ytm"], ["tmpx", "ss2"],
                    accum_out=ss2[0:rows, g:g + 1])
            ts(ss2[0:rows, 0:2], ss2[0:rows, 0:2], 1.0 / 512, EPS, ALU.mult, ALU.add, ["ss2"], ["ss2"])
            act(ss2[0:rows, 0:2], ss2[0:rows, 0:2], AF.Ln, ["ss2"], ["ss2"])
            act(ss2[0:rows, 0:2], ss2[0:rows, 0:2], AF.Exp, ["ss2"], ["ss2"], scale=-0.5)
            for g in range(2):
                ts(yn[0:rows, g * 512:(g + 1) * 512], ytm[0:rows, g * 512:(g + 1) * 512], ss2[0:rows, g:g + 1], None,
                   ALU.mult, None, ["ytm", "ss2"], ["yn"])
            pt, ptk = ps[6], "ps6"
            ptb = pt.bitcast(BF16)
            for cc in range(8):
                tr(ptb[:, cc * 128: cc * 128 + rows], yn[0:rows, cc * 128:(cc + 1) * 128], identb[0:rows, 0:rows],
                   ["yn", "identb"], [ptk])
            tt(ymT[:, :, b0:b0 + rows], ptb.rearrange("p (c r) -> p c r", c=8)[:, :, 0:rows],
               mnormF[:, l, :].unsqueeze(2).broadcast_to([128, 8, rows]), ALU.mult, [ptk, "mnormF"], ["ymT"])

        m.barrier()
        it = 0
        for c in range(4):
            ch = slice(c * 128, (c + 1) * 128)
            for g in range(2):
                pcb, pcbk = ps[4], "ps4"
                mm(pcb[:, g * 128:(g + 1) * 128], BCT[:, g, ch], BCT[:, 2 + g, ch], True, True, ["BCT"], [pcbk + f"_{g}"])
                pin, pink = ps[5], "ps5"
                mm(pin, BCT[:, 2 + g, ch], Smb[:, g * 512:(g + 1) * 512], True, True, ["BCT", "Smb"], [pink])
                pia, piak = ps[6], "ps6"
                for hp in range(4):
                    st_ = []
                    for hh in (2 * hp, 2 * hp + 1):
                        h = g * 8 + hh
                        pbk = f"ps{it % 4}"
                        pbs = ps[it % 4][:, 0:128]
                        st_.append((hh, h, pbs, pbk, Dm[it % 2], f"Dm{it % 2}", Lm[it % 2], f"Lm{it % 2}", Mt[it % 2], f"Mt{it % 2}"))
                        it += 1
                    for (hh, h, pbs, pbk, d_, dk_, l_, lk_, m_, mk_) in st_:
                        mm(pbs, sel[:, h, :], cumT[0:16, ch], True, True, ["sel", "cumT"], [pbk])
                    for (hh, h, pbs, pbk, d_, dk_, l_, lk_, m_, mk_) in st_:
                        stt(d_, pbs, cols[:, c, 16 + h:17 + h], negm, ALU.subtract, ALU.add, [pbk, "cols", "negm"], [dk_])
                    for (hh, h, pbs, pbk, d_, dk_, l_, lk_, m_, mk_) in st_:
                        act(l_, d_, AF.Exp, [dk_], [lk_])
                    for (hh, h, pbs, pbk, d_, dk_, l_, lk_, m_, mk_) in st_:
                        stt(m_, l_, cols[:, c, h:h + 1], pcb[:, g * 128:(g + 1) * 128], ALU.mult, ALU.mult,
                            [lk_, "cols", pcbk + f"_{g}"], [mk_])
                    for (hh, h, pbs, pbk, d_, dk_, l_, lk_, m_, mk_) in st_:
                        mm(pia[:, hh * 64:(hh + 1) * 64], m_, X_tm[:, c, h * 64:(h + 1) * 64], True, True, [mk_, "X_tm"], [piak])
                gs = slice(g * 512, (g + 1) * 512)
                e8 = expc[:, c, g * 8:(g + 1) * 8].unsqueeze(2).broadcast_to([128, 8, 64])
                tt(ys.rearrange("p (h q) -> p h q", h=8), pin.rearrange("p (h q) -> p h q", h=8), e8, ALU.mult,
                   [pink, "expc"], ["ys"])
                tt(ytm[:, gs], ys, pia, ALU.add, ["ys", piak], ["ytm"])
                d8 = dskip[:, l, g * 8:(g + 1) * 8].unsqueeze(2).broadcast_to([128, 8, 64])
                tt(tmpx.rearrange("p (h q) -> p h q", h=8), X_tm[:, c, gs].rearrange("p (h q) -> p h q", h=8), d8,
                   ALU.mult, ["X_tm", "dskip"], ["tmpx"])
                tt(ytm[:, gs], ytm[:, gs], tmpx, ALU.add, ["ytm", "tmpx"], ["ytm"])
                w8 = cols[:, c, 32 + g * 8:32 + (g + 1) * 8].unsqueeze(2).broadcast_to([128, 8, 64])
                tt(Xw.rearrange("p (h q) -> p h q", h=8), X_tm[:, c, gs].rearrange("p (h q) -> p h q", h=8), w8, ALU.mult,
                   ["X_tm", "cols"], ["Xw"])
                mm(pin, B_tm[:, c, g * 128:(g + 1) * 128], Xw, True, True, ["B_tm", "Xw"], [pink])
                k8 = eclbc[:, c, g * 8:(g + 1) * 8].unsqueeze(2).broadcast_to([128, 8, 64])
                tt(Sm[l][:, gs].rearrange("p (h q) -> p h q", h=8), Sm[l][:, gs].rearrange("p (h q) -> p h q", h=8), k8,
                   ALU.mult, [f"Sm{l}", "eclbc"], [f"Sm{l}"])
                tt(Sm[l][:, gs], Sm[l][:, gs], pin, ALU.add, [f"Sm{l}", pink], [f"Sm{l}"])
                cp("act", Smb[:, gs], Sm[l][:, gs], [f"Sm{l}"], ["Smb"])
            post_block(c, c * 128, 128)

        if samp:
            m.barrier()
            x4 = X_tm[0:NS, 4, :].rearrange("p (h q) -> p h q", h=16)
            tt(Xd[0:NS].rearrange("p (h q) -> p h q", h=16), x4, cols[0:NS, 4, 0:16].unsqueeze(2).broadcast_to([NS, 16, 64]),
               ALU.mult, ["X_tm", "cols"], ["Xd"])
            idb8 = id16.unsqueeze(2).broadcast_to([NS, NS, 8])
            tt(Dme[0:NS], expc[0:NS, 4, 0:16:2].unsqueeze(1).broadcast_to([NS, NS, 8]), idb8, ALU.mult, ["expc", "id16"], ["Dme"])
            tt(Dmo[0:NS], expc[0:NS, 4, 1:16:2].unsqueeze(1).broadcast_to([NS, NS, 8]), idb8, ALU.mult, ["expc", "id16"], ["Dmo"])
            pt, ptk = ps[6], "ps6"
            mm(pt[:, 0:128], E01[:, 0, :], Dme[0:NS].rearrange("p a b -> p (a b)"), True, False, ["E01", "Dme"], [ptk])
            mm(pt[:, 0:128], E01[:, 1, :], Dmo[0:NS].rearrange("p a b -> p (a b)"), False, True, ["E01", "Dmo"], [ptk])
            cp("dve", decbc.rearrange("p a b -> p (a b)"), pt[:, 0:128], [ptk], ["decbc"])
            idb128 = id16.unsqueeze(2).broadcast_to([NS, NS, 128])
            order = [(j, i) for j in range(8) for i in range(NS)]

            def loadm(idx):
                j_, i_ = order[idx]
                dma("sp", Sst[idx % 3], d_sssm[l, i_, j_], [], [f"Sst{idx % 3}"])
            PF = int(os.environ.get('MK_PF', '2'))
            for idx in range(PF):
                loadm(idx)
            si = 0
            for g in range(2):
                tt(Cm[0:NS], C_tm[0:NS, g * 128:(g + 1) * 128].unsqueeze(1).broadcast_to([NS, NS, 128]), idb128, ALU.mult,
                   ["C_tm", "id16"], ["Cm"])
                for q in range(4):
                    mm(ps[q], onesb[0:NS, :], Cm[0:NS, q * 4:(q + 1) * 4, :].rearrange("p a b -> p (a b)"), True, True,
                       ["onesb", "Cm"], [f"ps{q}"])
                for jj in range(4):
                    j = g * 4 + jj
                    tt(Xdm[0:NS], Xd[0:NS, j * 128:(j + 1) * 128].unsqueeze(1).broadcast_to([NS, NS, 128]), idb128, ALU.mult,
                       ["Xd", "id16"], ["Xdm"])
                    for i in range(NS):
                        s_in, sk_in = Sst[si % 3], f"Sst{si % 3}"
                        s_nw, sk_nw = Snw[si % 3], f"Snw{si % 3}"
                        po, pok = ps[4 + si % 2], f"ps{4 + si % 2}"
                        if PF == 0:
                            loadm(si)
                        mm(po[:, 0:128], Xdm[0:NS, i, :], B_tm[0:NS, 4, g * 128:(g + 1) * 128], True, True,
                           ["Xdm", "B_tm"], [pok])
                        stt(s_nw, s_in, decbc[:, i, j:j + 1], po[:, 0:128], ALU.mult, ALU.add, [sk_in, "decbc", pok], [sk_nw])
                        if PF and si + PF < len(order):
                            loadm(si + PF)
                        dma("sp", o_ssms[l, i, j], s_nw, [sk_nw], [f"o_ssms{l}_{i}_{j}"])
                        ttr(junk[:, 0:128], s_nw, ps[i // 4][:, (i % 4) * 128:(i % 4 + 1) * 128], ysT[:, j, i:i + 1],
                            [sk_nw, f"ps{i // 4}"], ["tmpx", "ysT"])
                        si += 1
            for half in range(2):
                pt, ptk = ps[4 + half], f"ps{4 + half}"
                for cc in range(4):
                    tr(pt[0:NS, cc * 128:(cc + 1) * 128], ysT[:, half * 4 + cc, :], identf, ["ysT", "identf"], [ptk])
                hs = slice(half * 512, (half + 1) * 512)
                d8 = dskip[0:NS, l, half * 8:(half + 1) * 8].unsqueeze(2).broadcast_to([NS, 8, 64])
                tt(tmpx[0:NS].rearrange("p (h q) -> p h q", h=8), X_tm[0:NS, 4, hs].rearrange("p (h q) -> p h q", h=8), d8,
                   ALU.mult, ["X_tm", "dskip"], ["tmpx"])
                tt(ytm[0:NS, hs], tmpx[0:NS], pt[0:NS, 0:512], ALU.add, ["tmpx", ptk], ["ytm"])
            post_block(4, TT, NS)
        m.barrier()

    def gla_alloc(o0, nh, dv, tagk):
        o = o0
        b = {}
        nvb = nh * dv // 128
        b["qT"] = carve(o, [128, nh, TC], F32); o += nh * TC * 4
        b["kT"] = carve(o, [128, nh, TC], F32); o += nh * TC * 4
        b["gT"] = carve(o, [128, nh, TC], F32); o += nh * TC * 4
        b["sgT"] = carve(o, [128, nvb, TC], BF16); o += nvb * TC * 2
        b["V"] = carve(o, [128, 5, 512], BF16); o += 5120
        b["sig"] = carve(o, [128, TC], F32); o += TC * 4
        for s_ in range(2):
            b[f"bT{s_}"] = carve(o, [128, 128], F32); o += 512
            b[f"e1{s_}"] = carve(o, [128, 128], F32); o += 512
            b[f"e2{s_}"] = carve(o, [128, 128], F32); o += 512
            b[f"qt{s_}"] = carve(o, [128, 128], BF16); o += 256
            b[f"kt{s_}"] = carve(o, [128, 128], BF16); o += 256
            b[f"At{s_}"] = carve(o, [128, 128], BF16); o += 256
            b[f"ktm{s_}"] = carve(o, [128, 128], BF16); o += 256
            b[f"Sr{s_}"] = carve(o, [128, 256], BF16); o += 512
            b[f"tU{s_}"] = carve(o, [128, 256], F32); o += 1024
            b[f"sq{s_}"] = [carve(o + i * 512, [128, 128], F32) for i in range(2)]; o += 1024
            b[f"rs{s_}"] = carve(o, [128, 128], F32); o += 512
            b[f"on{s_}"] = carve(o, [128, 128], F32); o += 512
            b[f"cc{s_}"] = carve(o, [128, 8], F32); o += 32
        b["ea"] = carve(o, [128, nh, NS], F32); o += nh * NS * 4
        b["Ktm"] = carve(o, [128, 128], BF16); o += 256
        b["Km"] = carve(o, [128, NS, 128], BF16); o += 4096
        b["Sin"] = [carve(o + i * dv * 4, [128, dv], F32) for i in range(3)]; o += 3 * dv * 4
        b["Snw"] = [carve(o + i * dv * 4, [128, dv], F32) for i in range(3)]; o += 3 * dv * 4
        assert o <= RBYTES, o
        for s_ in range(2):
            mset(b[f"At{s_}"][64:128, 0:64], 0.0, [f"{tagk}{s_}At"])
        return b

    def run_lockstep(gens):
        gens = list(gens)
        if os.environ.get("MK_NOLOCK"):
            for g_ in gens:
                for _ in g_:
                    pass
            return
        while gens:
            for g_ in list(gens):
                try:
                    next(g_)
                except StopIteration:
                    gens.remove(g_)

    def gla_post(b, s, po, pok, pss, pssk, dv, n, outs, sgs, nws, tag):
        nb = dv // 128
        T = f"{tag}{s}"
        sq, rs, on = b[f"sq{s}"], b[f"rs{s}"], b[f"on{s}"]
        for blk in range(nb):
            act(sq[blk][:, 0:n], po[:, blk * n:(blk + 1) * n], AF.Square, [pok], [f"{T}sq{blk}"])
            mm(pss[:, 0:n], onesf, sq[blk][:, 0:n], blk == 0, blk == nb - 1, [f"{T}sq{blk}", "onesf"], [pssk])
            yield
        ts(rs[:, 0:n], pss[:, 0:n], float(dv * EPS), None, ALU.add, None, [pssk], [T + "rs"])
        yield
        act(rs[:, 0:n], rs[:, 0:n], AF.Ln, [T + "rs"], [T + "rs"])
        yield
        act(rs[:, 0:n], rs[:, 0:n], AF.Exp, [T + "rs"], [T + "rs"], scale=-0.5)
        yield
        for blk in range(nb):
            tt(on[:, 0:n], po[:, blk * n:(blk + 1) * n], rs[:, 0:n], ALU.mult, [pok, T + "rs"], [T + "on"])
            yield
            stt(outs[blk], on[:, 0:n], nws[blk], sgs[blk], ALU.mult, ALU.mult, [T + "on", tag + "sg", "hnorm", "gnorm"],
                [tag + "y"])
            yield

    def gla_chunk(b, s, hh, c, S, skey, dv, outs, sgs, nws, tag):
        if os.environ.get("MK_ONESET"):
            s = 0
        ch = slice(c * 128, (c + 1) * 128)
        qc, kc, gc = b["qT"][:, hh, ch], b["kT"][:, hh, ch], b["gT"][:, hh, ch]
        Vc = b["V"][:, c, hh * dv:(hh + 1) * dv]
        bT, e1, e2, cc = b[f"bT{s}"], b[f"e1{s}"], b[f"e2{s}"], b[f"cc{s}"]
        qt, kt, At, ktm, Sr, tU = b[f"qt{s}"], b[f"kt{s}"], b[f"At{s}"], b[f"ktm{s}"], b[f"Sr{s}"], b[f"tU{s}"]
        T = f"{tag}{s}"
        G = tag
        pa, pak = ps[4 * s][:, 0:128], f"ps{4 * s}"
        pkb, pkk = ps[4 * s + 1].bitcast(BF16)[:, 0:128], f"ps{4 * s + 1}"
        pss, pssk = ps[4 * s][:, 0:128], f"ps{4 * s}"
        po, pok = ps[4 * s + 2][:, 0:256], f"ps{4 * s + 2}"
        pu, puk = ps[4 * s + 3][:, 0:256], f"ps{4 * s + 3}"
        scan(bT, onesf, gc, 0.0, ALU.mult, ALU.add, [G + "gT", "onesf"], [T + "bT"])
        yield
        ts(cc[:, 0:1], bT[:, 63:64], -1.0, None, ALU.mult, None, [T + "bT"], [T + "cc"])
        yield
        act(e1, bT, AF.Exp, [T + "bT", T + "cc"], [T + "e1"], bias=cc[:, 0:1])
        act(e2, bT, AF.Exp, [T + "bT"], [T + "e2"], scale=-1.0, bias=bT[:, 63:64])
        yield
        act(cc[:, 1:2], bT[:, 63:64], AF.Exp, [T + "bT"], [T + "cc"])
        act(cc[:, 2:3], bT[:, 127:128], AF.Exp, [T + "bT"], [T + "cc"])
        act(cc[:, 3:4], bT[:, 127:128], AF.Exp, [T + "bT", T + "cc"], [T + "cc"], bias=cc[:, 0:1])
        tt(qt, qc, e1, ALU.mult, [G + "qT", T + "e1"], [T + "qt"])
        tt(kt, kc, e2, ALU.mult, [G + "kT", T + "e2"], [T + "kt"])
        yield
        mm(pa[:, 64:128], kt, qt[:, 64:128], True, True, [T + "kt", T + "qt"], [pak])
        mm(pa[0:64, 0:64], kt[:, 0:64], qt[:, 0:64], True, True, [T + "kt", T + "qt"], [pak])
        tr(pkb, kt, identb, [T + "kt", "identb"], [pkk])
        yield
        tt(At[:, 64:128], pa[:, 64:128], trif[:, 64:128], ALU.mult, [pak, "trif"], [T + "At"])
        tt(At[0:64, 0:64], pa[0:64, 0:64], trif[0:64, 0:64], ALU.mult, [pak, "trif"], [T + "At"])
        cp("act", ktm, pkb, [pkk], [T + "ktm"])
        ts(Sr[:, 0:dv], S, cc[:, 1:2], None, ALU.mult, None, [skey, T + "cc"], [T + "Sr"])
        yield
        for blk in range(dv // 128):
            mm(po[:, blk * 128:(blk + 1) * 128], Vc[:, blk * 128:(blk + 1) * 128], At, True, False, [G + "V", T + "At"], [pok])
            mm(po[:, blk * 128:(blk + 1) * 128], Sr[:, blk * 128:(blk + 1) * 128], qt, False, True,
               [T + "Sr", T + "qt"], [pok])
        mm(pu[:, 0:dv], ktm, Vc, True, True, [T + "ktm", G + "V"], [puk])
        yield
        ts(tU[:, 0:dv], pu[:, 0:dv], cc[:, 3:4], None, ALU.mult, None, [puk, T + "cc"], [T + "tU"])
        yield
        stt(S, S, cc[:, 2:3], tU[:, 0:dv], ALU.mult, ALU.add, [skey, T + "cc", T + "tU"], [skey])
        yield
        yield from gla_post(b, s, po, pok, pss, pssk, dv, 128, outs, sgs, nws, tag)

    def gla_decode(b, l, hh, h, S_dram_in, S_dram_out, dv, outs, sgs, nws, tag):
        T = tag
        idb128 = id16.unsqueeze(2).broadcast_to([NS, NS, 128])
        pk_, pkk = ps[1], "ps1"
        tr(pk_[0:NS, 0:128], b["kT"][:, hh, TT:TC], identf, [T + "kT", "identf"], [pkk])
        cp("act", b["Ktm"][0:NS], pk_[0:NS, 0:128], [pkk], [T + "Ktm"])
        tt(b["Km"][0:NS], b["Ktm"][0:NS].unsqueeze(1).broadcast_to([NS, NS, 128]), idb128, ALU.mult, [T + "Ktm", "id16"], [T + "Km"])
        po, pok = ps[2], "ps2"
        nb = dv // 128
        PF = int(os.environ.get('MK_PF', '2'))

        def load(i):
            dma("sp", b["Sin"][i % 3][:, 0:dv], S_dram_in[l, i, h], [], [f"{T}Sin{i % 3}"])
        for i in range(min(PF, NS)):
            load(i)
        for i in range(NS):
            s_in, sk_in = b["Sin"][i % 3], f"{T}Sin{i % 3}"
            s_nw, sk_nw = b["Snw"][i % 3], f"{T}Snw{i % 3}"
            pu, puk = ps[5 + i % 2], f"ps{5 + i % 2}"
            if PF == 0:
                load(i)
            mm(pu[:, 0:dv], b["Km"][0:NS, i, :], b["V"][0:NS, 4, hh * dv:(hh + 1) * dv], True, True, [T + "Km", T + "V"], [puk])
            stt(s_nw[:, 0:dv], s_in[:, 0:dv], b["ea"][:, hh, i:i + 1], pu[:, 0:dv], ALU.mult, ALU.add, [sk_in, T + "ea", puk], [sk_nw])
            if PF and i + PF < NS:
                load(i + PF)
            dma("sp", S_dram_out[l, i, h], s_nw[:, 0:dv], [sk_nw], [f"{T}o_{l}_{i}_{h}"])
            for blk in range(nb):
                mm(po[:, blk * NS + i: blk * NS + i + 1], s_nw[:, blk * 128:(blk + 1) * 128], b["qT"][:, hh, TT + i:TT + i + 1],
                   True, True, [sk_nw, T + "qT"], [pok])
        for _ in gla_post(b, 0, po, pok, ps[4], "ps4", dv, NS, outs, sgs, nws, T):
            pass

    def hgrn(t, l):
        samp = (t == 0)
        for hf in range(2):
            b = gla_alloc(Y_OFF, 4, 128, "h")
            T = "h"

            def ev_q(c, c0, n, pp, pk):
                act(b["qT"][:, c, c0:c0 + n], pp, AF.Silu, [pk], [T + "qT"])
                ts(b["qT"][:, c, c0:c0 + n], b["qT"][:, c, c0:c0 + n], float(128.0 ** -0.5), None, ALU.mult, None, [T + "qT"], [T + "qT"])

            def ev_f(c, c0, n, pp, pk):
                chn = hf * 4 + c
                act(b["sig"][:, 0:n], pp, AF.Sigmoid, [pk], [T + "sig"])
                ts(b["gT"][:, c, c0:c0 + n], b["sig"][:, 0:n], oml[:, l, chn:chn + 1], lb[:, l, chn:chn + 1], ALU.mult, ALU.add,
                   [T + "sig", "oml", "lb"], [T + "gT"])
                act(b["gT"][:, c, c0:c0 + n], b["gT"][:, c, c0:c0 + n], AF.Ln, [T + "gT"], [T + "gT"])
                ts(b["kT"][:, c, c0:c0 + n], b["sig"][:, 0:n], noml[:, l, chn:chn + 1], oml[:, l, chn:chn + 1], ALU.mult, ALU.add,
                   [T + "sig", "oml", "noml"], [T + "kT"])

            def ev_g(c, c0, n, pp, pk):
                act(b["sgT"][:, c, c0:c0 + n], pp, AF.Silu, [pk], [T + "sg"])

            def ev_v(bi, rows, co, w, pp, pk):
                cp("act", b["V"][0:rows, bi, co:co + w], pp, [pk], [T + "V"])

            proj_fm(t, l, O_HQ + hf * 512, 512, ev_q)
            proj_fm(t, l, O_HF + hf * 512, 512, ev_f)
            proj_fm(t, l, O_HG + hf * 512, 512, ev_g)
            proj_tm(t, l, O_HI + hf * 512, 512, ev_v)
            m.barrier()
            for c in range(4):
                for hp in range(2):
                    gens = []
                    for s_ in range(2):
                        hh = hp * 2 + s_
                        h = hf * 4 + hh
                        gens.append(gla_chunk(b, s_, hh, c, Sh[l][:, h * 128:(h + 1) * 128], f"Sh{l}_{h}", 128,
                                              [yhT[:, h, c * 128:(c + 1) * 128]], [b["sgT"][:, hh, c * 128:(c + 1) * 128]],
                                              [hnorm[:, l:l + 1]], T))
                    run_lockstep(gens)
            m.barrier()
            if samp:
                act(b["ea"], b["gT"][:, :, TT:TC], AF.Exp, [T + "gT"], [T + "ea"])
                for hh in range(4):
                    h = hf * 4 + hh
                    gla_decode(b, l, hh, h, d_shgrn, o_hgrns, 128, [yhT[:, h, TT:TC]], [b["sgT"][:, hh, TT:TC]],
                               [hnorm[:, l:l + 1]], T)
            m.barrier()

    def gla(t, l):
        samp = (t == 0)
        ncol = TC if samp else TT
        for gf in range(2):
            b = gla_alloc(Y_OFF, 2, 256, "g")
            gaT = b["sig"]
            T = "g"

            def ev_q(c, c0, n, pp, pk):
                act(b["qT"][:, c, c0:c0 + n], pp, AF.Copy, [pk], [T + "qT"], scale=float(128.0 ** -0.5))

            def ev_k(c, c0, n, pp, pk):
                cp("act", b["kT"][:, c, c0:c0 + n], pp, [pk], [T + "kT"])

            def ev_g(c, c0, n, pp, pk):
                act(b["sgT"][:, c, c0:c0 + n], pp, AF.Silu, [pk], [T + "sg"])

            def ev_a(c, c0, n, pp, pk):
                cp("act", gaT[0:16, c0:c0 + n], pp, [pk], [T + "ga"])

            def ev_v(bi, rows, co, w, pp, pk):
                cp("act", b["V"][0:rows, bi, co:co + w], pp, [pk], [T + "V"])

            proj_fm(t, l, O_GQ + gf * 256, 256, ev_q)
            proj_fm(t, l, O_GK + gf * 256, 256, ev_k)
            proj_fm(t, l, O_GG + gf * 512, 512, ev_g)
            proj_fm(t, l, O_GA, 16, ev_a)
            for hh in range(2):
                chn = gf * 2 + hh
                for (c0, n) in colblocks(t):
                    pp, pk = next_ps(4)
                    mm(pp[:, 0:n], wdec[:, l, chn * 128:(chn + 1) * 128], gaT[0:16, c0:c0 + n], True, True, ["wdec", T + "ga"], [pk])
                    act(b["gT"][:, hh, c0:c0 + n], pp[:, 0:n], AF.Exp, [pk, "bdec"], [T + "gT"], scale=-1.0, bias=bdec[:, l, chn:chn + 1])
            act(b["gT"][:, :, 0:ncol], b["gT"][:, :, 0:ncol], AF.Ln, [T + "gT"], [T + "gT"], bias=1.0)
            ts(b["gT"][:, :, 0:ncol], b["gT"][:, :, 0:ncol], -1.0 / 16.0, None, ALU.mult, None, [T + "gT"], [T + "gT"])
            proj_tm(t, l, O_GV + gf * 512, 512, ev_v)
            m.barrier()
            for c in range(4):
                gens = []
                for hh in range(2):
                    h = gf * 2 + hh
                    gens.append(gla_chunk(b, hh, hh, c, Sg[l][:, h * 256:(h + 1) * 256], f"Sg{l}_{h}", 256,
                                          [ygT[:, h * 2 + k2, c * 128:(c + 1) * 128] for k2 in range(2)],
                                          [b["sgT"][:, hh * 2 + k2, c * 128:(c + 1) * 128] for k2 in range(2)],
                                          [gnorm[:, l, k2:k2 + 1] for k2 in range(2)], T))
                run_lockstep(gens)
            m.barrier()
            if samp:
                act(b["ea"], b["gT"][:, :, TT:TC], AF.Exp, [T + "gT"], [T + "ea"])
                for hh in range(2):
                    h = gf * 2 + hh
                    gla_decode(b, l, hh, h, d_sgla, o_glas, 256,
                               [ygT[:, h * 2 + k2, TT:TC] for k2 in range(2)],
                               [b["sgT"][:, hh * 2 + k2, TT:TC] for k2 in range(2)],
                               [gnorm[:, l, k2:k2 + 1] for k2 in range(2)], T)
            m.barrier()

    def merge(t, l):
        merged = carve(Y_OFF, [128, 16, TC], BF16)
        o = Y_OFF + 16 * TC * 2
        macc = [carve(o + i * TC * 4, [128, TC], F32) for i in range(2)]; o += 2 * TC * 4
        sgm = carve(o, [128, TC], F32); o += TC * 4
        mtmp = carve(o, [128, TC], F32); o += TC * 4
        assert o <= RBYTES
        ysrc = (ymT, yhT, ygT)
        cbs = colblocks(t)
        for fo2 in range(8):
            for bidx in range(3):
                wv, wk = load_slab(w_br[bidx][l][:, fo2 * 256:(fo2 + 1) * 256], 8, 256)
                gv, gk = load_slab(w_in[l][:, O_GATE + bidx * 2048 + fo2 * 256: O_GATE + bidx * 2048 + (fo2 + 1) * 256], 16, 256)
                for jj in range(2):
                    fo = fo2 * 2 + jj
                    for (c0, n) in cbs:
                        pP, pPk = next_ps(4)
                        pG, pGk = next_ps(4)
                        for k in range(8):
                            mm(pP[:, 0:n], wv[:, k, jj * 128:(jj + 1) * 128], ysrc[bidx][:, k, c0:c0 + n], k == 0, k == 7,
                               [wk, "ymT", "hy", "gy"], [pPk])
                        for k in range(16):
                            mm(pG[:, 0:n], gv[:, k, jj * 128:(jj + 1) * 128], hn[:, k, c0:c0 + n], k == 0, k == 15,
                               [gk, f"hn{k}"], [pGk])
                        act(sgm[:, 0:n], pG[:, 0:n], AF.Sigmoid, [pGk], ["sgm"])
                        mk = f"macc{jj}"
                        if bidx == 0:
                            tt(macc[jj][:, c0:c0 + n], sgm[:, 0:n], pP[:, 0:n], ALU.mult, ["sgm", pPk], [mk])
                        elif bidx == 1:
                            tt(mtmp[:, 0:n], sgm[:, 0:n], pP[:, 0:n], ALU.mult, ["sgm", pPk], ["mtmp"])
                            tt(macc[jj][:, c0:c0 + n], macc[jj][:, c0:c0 + n], mtmp[:, 0:n], ALU.add, [mk, "mtmp"], [mk])
                        else:
                            tt(mtmp[:, 0:n], sgm[:, 0:n], pP[:, 0:n], ALU.mult, ["sgm", pPk], ["mtmp"])
                            tt(merged[:, fo, c0:c0 + n], macc[jj][:, c0:c0 + n], mtmp[:, 0:n], ALU.add, [mk, "mtmp"], [f"mg{fo}"])
        for fo2 in range(8):
            wv, wk = load_slab(w_out[l][:, fo2 * 256:(fo2 + 1) * 256], 16, 256)
            for jj in range(2):
                fo = fo2 * 2 + jj
                for (c0, n) in cbs:
                    pp, pk = next_ps(4)
                    for k in range(16):
                        mm(pp[:, 0:n], wv[:, k, jj * 128:(jj + 1) * 128], merged[:, k, c0:c0 + n], k == 0, k == 15,
                           [wk, f"mg{k}"], [pk])
                    tt(xT[:, fo, c0:c0 + n], pp[:, 0:n], xT[:, fo, c0:c0 + n], ALU.add, [pk, f"xT{fo}"], [f"xT{fo}"])
        m.barrier()

    def mixer(t, l):
        rmsnorm_fm(t, 2 + l, hn, "hn")
        m.barrier()
        if "mamba" in mixparts:
            mamba(t, l)
        if "hgrn" in mixparts:
            hgrn(t, l)
        if "gla" in mixparts:
            gla(t, l)
        if "merge" in mixparts:
            merge(t, l)
        m.barrier()

    for (dst, src, key) in ((identf, d_identf, "identf"), (trif, d_trif, "trif"), (negm, d_negm, "negm"),
                            (nrm, d_nrm, "nrm")):
        dma("sp", dst, src, [], [key])
    for (dst, src, key) in ((convw, d_convw, "convw"), (convb, d_convb, "convb"), (hd, d_hd, "hd"),
                            (dskip, d_dskip, "dskip"), (mnormF, d_mnorm, "mnormF"), (lbl, d_lbl, "lbl"),
                            (hnorm, d_hnorm, "hnorm"), (gnorm, d_gnorm, "gnorm"), (wdec, d_wdec, "wdec"),
                            (bdec, d_bdec, "bdec"), (sel, d_sel, "sel"), (id16, d_id16, "id16")):
        dma("sp", dst, src, [], [key])
    act(aneg, hd[:, 2:4], AF.Exp, ["hd"], ["aneg"])
    ts(aneg, aneg, -1.0, None, ALU.mult, None, ["aneg"], ["aneg"])
    mset(lb, 0.0, ["lb"])
    tt(lb[:, 1, :], lbl[:, 1, :], lbl[:, 0, :], ALU.subtract, ["lbl", "lb"], ["lb"])
    act(lb[:, 1, :], lb[:, 1, :], AF.Sigmoid, ["lb"], ["lb"])
    ts(oml, lb, -1.0, 1.0, ALU.mult, ALU.add, ["lb"], ["oml"])
    ts(noml, lb, 1.0, -1.0, ALU.mult, ALU.add, ["lb"], ["noml"])
    ts(hnorm, hnorm, float(np.sqrt(128.0)), None, ALU.mult, None, ["hnorm"], ["hnorm"])
    ts(gnorm, gnorm, 16.0, None, ALU.mult, None, ["gnorm"], ["gnorm"])
    ts(bdec, bdec, -1.0, None, ALU.mult, None, ["bdec"], ["bdec"])
    mset(E01, 0.0, ["E01"])
    mset(E01[:, 0, 0:64], 1.0, ["E01"])
    mset(E01[:, 1, 64:128], 1.0, ["E01"])
    for l in range(2):
        mset(Sm[l], 0.0, [f"Sm{l}"])
        mset(Sh[l], 0.0, [f"Sh{l}_{h}" for h in range(8)])
        mset(Sg[l], 0.0, [f"Sg{l}_{h}" for h in range(4)])
        mset(cst[l], 0.0, [f"cst{l}"])
    mset(onesf, 1.0, ["onesf"])
    mset(onesb, 1.0, ["onesb"])
    cp("dve", identb, identf, ["identf"], ["identb"])

    for t in range(NT):
        dma("sp", xT[:, :, 0:TT], xTp[:, t * TT:(t + 1) * TT].rearrange("(c p) n -> p c n", p=128), [], XT_ALL)
        if t == 0:
            dma("sp", xT[:, :, TT:TC], xTs.rearrange("(c p) n -> p c n", p=128), [], XT_ALL)
        for l in range(nlayers):
            if "ffn1" in stages:
                ffn(t, l, 0)
            if "mix" in stages:
                mixer(t, l)
            if "ffn2" in stages:
                ffn(t, l, 1)
        m.barrier()
        rmsnorm_fm(t, 6, xT, "xT")
        dma("sp", yTp[:, t * TT:(t + 1) * TT].rearrange("(c p) n -> p c n", p=128), xT[:, :, 0:TT], XT_ALL, ["yTp"])
        if t == 0:
            dma("sp", yTs.rearrange("(c p) n -> p c n", p=128), xT[:, :, TT:TC], XT_ALL, ["yTs"])
        m.barrier()
    for l in range(2):
        dma("sp", o_convp[l], cst[l], [f"cst{l}"], [f"o_convp{l}"])
        dma("sp", o_ssmp[l], Sm[l], [f"Sm{l}"], [f"o_ssmp{l}"])
        dma("sp", o_hgrnp[l], Sh[l], [f"Sh{l}_{h}" for h in range(8)], [f"o_hgrnp{l}"])
        dma("sp", o_glap[l], Sg[l], [f"Sg{l}_{h}" for h in range(4)], [f"o_glap{l}"])
    m.wait_all("sp")
    m.emit()
    return nc


def _consts():
    i = np.arange(128)
    identf = np.eye(128, dtype=np.float32)
    trif = (i[:, None] <= i[None, :]).astype(np.float32)
    negm = np.where(i[:, None] <= i[None, :], 0.0, -30000.0).astype(np.float32)
    sel = np.zeros((16, 16, 128), np.float32)
    for h in range(16):
        sel[h, h, :] = 1.0
    id16 = np.eye(16, dtype=np.float32)
    return dict(identf=identf, trif=trif, negm=negm, sel=sel, id16=id16)


def _fm(v, nch):
    v = np.asarray(v, np.float32)
    lead = v.shape[:-1]
    v = v.reshape(lead + (nch, 128))
    v = np.moveaxis(v, -1, 0)
    return np.ascontiguousarray(v)


def _shared_inputs(inp):
    f = lambda k: np.ascontiguousarray(np.asarray(inp[k], np.float32))
    sh = dict(w_gu1=f("ffn1_w_gate_up"), w_d1=f("ffn1_w_down"), w_gu2=f("ffn2_w_gate_up"), w_d2=f("ffn2_w_down"),
              w_in=f("w_in"), w_bm=f("w_branch_mamba"), w_bh=f("w_branch_hgrn"), w_bg=f("w_branch_gla"),
              w_out=f("w_out"))
    nrm = np.stack([f("ffn1_norm")[0], f("ffn1_norm")[1], f("mix_norm")[0], f("mix_norm")[1],
                    f("ffn2_norm")[0], f("ffn2_norm")[1], f("final_norm")], 0)
    sh["nrm"] = _fm(nrm, 16)
    sh["convw"] = np.ascontiguousarray(np.transpose(_fm(f("conv_w"), 12), (0, 1, 3, 2)))
    sh["convb"] = _fm(f("conv_b"), 12)
    sh["hd"] = np.ascontiguousarray(np.concatenate([f("dt_bias").T, f("a_log").T], 1))
    sh["dskip_bc"] = np.ascontiguousarray(np.broadcast_to(f("d_skip")[None], (128, 2, 16)))
    sh["mnormF"] = _fm(f("mamba_norm"), 8)
    sh["lbl"] = _fm(f("hgrn_lb_logits"), 8)
    sh["hnorm"] = np.ascontiguousarray(f("hgrn_norm").T)
    sh["gnorm"] = _fm(f("gla_norm"), 2)
    sh["wdec"] = np.ascontiguousarray(np.transpose(f("gla_w_decay"), (1, 0, 2)))
    sh["bdec"] = _fm(f("gla_b_decay"), 4)
    sh.update(_consts())
    return sh


def _core_inputs(inp, sh, b, tok0, NT, s0):
    xp = np.asarray(inp["x_prompt"], np.float32)
    xs = np.asarray(inp["x_sample"], np.float32)
    d = dict(sh)
    d["xTp"] = np.ascontiguousarray(xp[b, tok0:tok0 + NT * TT, :].T)
    d["xTs"] = np.ascontiguousarray(xs[s0:s0 + NS, 0, :].T)
    sc = np.asarray(inp["state_conv"], np.float32)[:, s0:s0 + NS]
    sc = sc.reshape(2, NS, 3, 12, 128)
    d["sconv"] = np.ascontiguousarray(np.transpose(sc, (4, 0, 3, 1, 2)))
    d["s_ssm"] = np.ascontiguousarray(np.asarray(inp["state_ssm"], np.float32)[:, s0:s0 + NS].reshape(2, NS, 8, 128, 128))
    d["s_hgrn"] = np.ascontiguousarray(np.asarray(inp["state_hgrn"], np.float32)[:, s0:s0 + NS])
    d["s_gla"] = np.ascontiguousarray(np.asarray(inp["state_gla"], np.float32)[:, s0:s0 + NS])
    return d


def _unpack_core(r, NT):
    o = {}
    o["y_p"] = np.ascontiguousarray(r["yTp"].T)
    o["y_s"] = np.ascontiguousarray(r["yTs"].T)
    o["conv_p"] = np.transpose(r["conv_p"], (0, 3, 2, 1)).reshape(2, 3, 1536)
    o["ssm_p"] = np.transpose(r["ssm_p"].reshape(2, 128, 16, 64), (0, 2, 3, 1))
    o["hgrn_p"] = np.transpose(r["hgrn_p"].reshape(2, 128, 8, 128), (0, 2, 1, 3))
    o["gla_p"] = np.transpose(r["gla_p"].reshape(2, 128, 4, 256), (0, 2, 1, 3))
    o["conv_s"] = np.transpose(r["conv_s"], (0, 3, 4, 2, 1)).reshape(2, NS, 3, 1536)
    o["ssm_s"] = r["ssm_s"].reshape(2, NS, 16, 64, 128)
    o["hgrn_s"] = r["hgrn_s"]
    o["gla_s"] = r["gla_s"]
    return o


_NC_CACHE = {}


def kernel(**inputs):
    NT = 4
    key = ("full", NT)
    if key not in _NC_CACHE:
        _NC_CACHE[key] = build(NT)
    nc = _NC_CACHE[key]
    sh = _shared_inputs(inputs)
    in_maps = [_core_inputs(inputs, sh, c % 4, 0, NT, c * NS) for c in range(8)]
    res = run_bass_kernel_spmd(nc, in_maps, core_ids=list(range(8)))
    outs = [_unpack_core(r, NT) for r in res.results]
    y_prompt = np.stack([outs[b]["y_p"] for b in range(4)], 0)
    y_sample = np.concatenate([o["y_s"] for o in outs], 0)[:, None, :]
    st = lambda k: np.ascontiguousarray(np.stack([outs[b][k] for b in range(4)], 1))
    cat = lambda k: np.ascontiguousarray(np.concatenate([o[k] for o in outs], 1))
    return (y_prompt.astype(np.float32), y_sample.astype(np.float32),
            st("conv_p"), st("ssm_p"), st("hgrn_p"), st("gla_p"),
            cat("conv_s"), cat("ssm_s"), cat("hgrn_s"), cat("gla_s"))
```
